# Optimizing a Trainium2 kernel written in Bass

```python
import math
import jax, jax.numpy as jnp
from jax import lax
import numpy as np

D_MODEL = 2048
BATCH = 4
SEQ = 8192
DEPTH = 1

MIX_WIDTH = D_MODEL
ATTN_WIDTH = D_MODEL // 2
HEAD_DIM = 128
N_HEADS = ATTN_WIDTH // HEAD_DIM
DILATION_PATTERNS = ((128, 1), (512, 4), (2048, 16))
SSM_WIDTH = MIX_WIDTH - ATTN_WIDTH
SSM_GROUP = 16
N_SSM_GROUPS = SSM_WIDTH // SSM_GROUP
STATE_DIM = 64
SSM_CHUNK = 128
IN_WIDTH = 3 * ATTN_WIDTH + SSM_WIDTH
D_FF = 4 * D_MODEL
N_MOD = 6
EPS = 1e-6
DT_MIN = 1e-3
DT_MAX = 1e-1

kernel_name = "hymba_dilated_attn_s5_sqrelu_adaln"


def rms_norm(x, g):
    xf = x.astype(jnp.float32)
    y = xf * lax.rsqrt(jnp.mean(xf * xf, axis=-1, keepdims=True) + EPS) * g.astype(jnp.float32)
    return y.astype(x.dtype)


def alibi_slopes(n_heads):
    return 2.0 ** (-8.0 * (jnp.arange(n_heads, dtype=jnp.float32) + 1.0) / n_heads)


def dilated_pattern(q, k, v, slopes, window, dilation):
    b, s, h, e = q.shape
    n = window // dilation
    L = s // dilation
    nb = -(-L // n)
    Lp = nb * n

    def to_blocks(t):
        t = t.reshape(b, L, dilation, h, e)
        t = jnp.pad(t, ((0, 0), (0, Lp - L), (0, 0), (0, 0), (0, 0)))
        return t.reshape(b, nb, n, dilation, h, e)

    def with_prev(t):
        prev = jnp.pad(t, ((0, 0), (1, 0), (0, 0), (0, 0), (0, 0), (0, 0)))[:, :-1]
        return jnp.concatenate([prev, t], axis=2)

    qb = to_blocks(q)
    kw = with_prev(to_blocks(k))
    vw = with_prev(to_blocks(v))
    scores = jnp.einsum('bnqrhe,bnkrhe->bnrhqk', qb, kw) * (HEAD_DIM ** -0.5)

    qi = jnp.arange(n)[:, None]
    ki = jnp.arange(2 * n)[None, :]
    steps = qi - ki + n
    key_idx = jnp.arange(nb)[:, None, None] * n - n + ki
    valid = (steps >= 0) & (steps <= n) & (key_idx >= 0)
    bias = -slopes[:, None, None] * (steps * dilation).astype(jnp.float32)
    scores = scores + bias[None, None, None]
    scores = jnp.where(valid[None, :, None, None], scores, -jnp.inf)

    m = jnp.max(scores, axis=-1, keepdims=True)
    p = jnp.exp(scores - m)
    denom = jnp.sum(p, axis=-1, keepdims=True)
    out = jnp.einsum('bnrhqk,bnkrhe->bnqrhe', p, vw)
    denom_q = jnp.moveaxis(denom[..., 0], -1, 2)
    lse_q = jnp.moveaxis(m[..., 0], -1, 2) + jnp.log(denom_q)
    out = out / denom_q[..., None]
    out = out.reshape(b, Lp, dilation, h, e)[:, :L].reshape(b, s, h, e)
    lse = lse_q.reshape(b, Lp, dilation, h)[:, :L].reshape(b, s, h)
    return out, lse


def dilated_attention(q, k, v):
    slopes = alibi_slopes(N_HEADS)
    outs, lses = [], []
    for window, dilation in DILATION_PATTERNS:
        o, l = dilated_pattern(q, k, v, slopes, window, dilation)
        outs.append(o)
        lses.append(l)
    w = jax.nn.softmax(jnp.stack(lses, axis=0), axis=0)
    return jnp.sum(w[..., None] * jnp.stack(outs, axis=0), axis=0)


def _ssm_combine(e_i, e_j):
    a_i, b_i = e_i
    a_j, b_j = e_j
    return a_j * a_i, a_j * b_i + b_j


def s5_mixer(u, lam_re, lam_im, log_step, b_re, b_im, c_re, c_im, d_skip):
    bsz, s, _ = u.shape
    uf = u.astype(jnp.float32).reshape(bsz, s, N_SSM_GROUPS, SSM_GROUP)
    lam = lax.complex(lam_re.astype(jnp.float32), lam_im.astype(jnp.float32))
    step = jnp.exp(log_step.astype(jnp.float32))[:, None]
    a_bar = jnp.exp(lam * step)
    b_mat = lax.complex(b_re.astype(jnp.float32), b_im.astype(jnp.float32))
    b_bar = ((a_bar - 1.0) / lam)[..., None] * b_mat
    c_mat = lax.complex(c_re.astype(jnp.float32), c_im.astype(jnp.float32))
    n_chunks = s // SSM_CHUNK
    u_chunks = uf.reshape(bsz, n_chunks, SSM_CHUNK, N_SSM_GROUPS, SSM_GROUP).transpose(1, 0, 2, 3, 4)
    a_full = jnp.broadcast_to(a_bar, (bsz, SSM_CHUNK, N_SSM_GROUPS, STATE_DIM))

    def segment(h, u_c):
        bu = jnp.einsum('blgi,gpi->blgp', u_c.astype(jnp.complex64), b_bar)
        bu = bu.at[:, 0].add(a_bar * h)
        _, hs = lax.associative_scan(_ssm_combine, (a_full, bu), axis=1)
        y = jnp.real(jnp.einsum('blgp,gip->blgi', hs, c_mat))
        return hs[:, -1], y

    h0 = jnp.zeros((bsz, N_SSM_GROUPS, STATE_DIM), jnp.complex64)
    _, ys = lax.scan(segment, h0, u_chunks)
    y = ys.transpose(1, 0, 2, 3, 4).reshape(bsz, s, N_SSM_GROUPS, SSM_GROUP)
    y = y + d_skip.astype(jnp.float32).reshape(N_SSM_GROUPS, SSM_GROUP) * uf
    return y.reshape(bsz, s, SSM_WIDTH)


def hybrid_layer(x, c, w_ada, b_ada, norm1_g, w_in, q_norm_g, k_norm_g, lam_re, lam_im,
                 log_step, b_re, b_im, c_re, c_im, d_skip, w_glu, b_glu, attn_out_g,
                 ssm_out_g, w_out, norm2_g, w_ff1, w_ff2):
    bsz, s, _ = x.shape
    mod = (jax.nn.silu(c) @ w_ada + b_ada)[:, None, :]
    sh1, sc1, g1, sh2, sc2, g2 = jnp.split(mod, N_MOD, axis=-1)

    h = rms_norm(x, norm1_g) * (1.0 + sc1) + sh1
    proj = h @ w_in
    q, k, v, u = jnp.split(proj, [ATTN_WIDTH, 2 * ATTN_WIDTH, 3 * ATTN_WIDTH], axis=-1)
    q = rms_norm(q.reshape(bsz, s, N_HEADS, HEAD_DIM), q_norm_g).astype(jnp.float32)
    k = rms_norm(k.reshape(bsz, s, N_HEADS, HEAD_DIM), k_norm_g).astype(jnp.float32)
    v = v.reshape(bsz, s, N_HEADS, HEAD_DIM).astype(jnp.float32)
    attn = dilated_attention(q, k, v).reshape(bsz, s, ATTN_WIDTH).astype(x.dtype)

    y = jax.nn.gelu(s5_mixer(u, lam_re, lam_im, log_step, b_re, b_im, c_re, c_im, d_skip)).astype(x.dtype)
    ssm = y * jax.nn.sigmoid(y @ w_glu + b_glu)

    mixed = jnp.concatenate([rms_norm(attn, attn_out_g), rms_norm(ssm, ssm_out_g)], axis=-1) @ w_out
    x = x + g1 * mixed

    h2 = rms_norm(x, norm2_g) * (1.0 + sc2) + sh2
    ff = jnp.square(jax.nn.relu(h2 @ w_ff1)) @ w_ff2
    return x + g2 * ff


def setup_inputs(seed: int = 0) -> dict:
    key = jax.random.key(seed)
    ks = jax.random.split(key, 26)
    f32 = jnp.float32
    nrm = lambda k, shape, scale: jax.random.normal(k, shape, f32) * scale
    G, P = N_SSM_GROUPS, STATE_DIM
    lam_im_base = jnp.pi * jnp.arange(P, dtype=f32)
    return {
        "x": nrm(ks[0], (BATCH, SEQ, D_MODEL), 1.0),
        "c": nrm(ks[1], (BATCH, D_MODEL), 1.0),
        "w_ada": nrm(ks[2], (DEPTH, D_MODEL, N_MOD * D_MODEL), 0.5 * D_MODEL ** -0.5),
        "b_ada": nrm(ks[3], (DEPTH, N_MOD * D_MODEL), 0.01),
        "norm1_g": 1.0 + nrm(ks[4], (DEPTH, D_MODEL), 0.01),
        "w_in": nrm(ks[5], (DEPTH, D_MODEL, IN_WIDTH), D_MODEL ** -0.5),
        "q_norm_g": 1.0 + nrm(ks[6], (DEPTH, HEAD_DIM), 0.01),
        "k_norm_g": 1.0 + nrm(ks[7], (DEPTH, HEAD_DIM), 0.01),
        "lam_re": -0.5 + nrm(ks[8], (DEPTH, G, P), 0.01),
        "lam_im": lam_im_base + nrm(ks[9], (DEPTH, G, P), 0.01),
        "log_step": jax.random.uniform(ks[10], (DEPTH, G), f32, math.log(DT_MIN), math.log(DT_MAX)),
        "b_re": nrm(ks[11], (DEPTH, G, P, SSM_GROUP), (2.0 * SSM_GROUP) ** -0.5),
        "b_im": nrm(ks[12], (DEPTH, G, P, SSM_GROUP), (2.0 * SSM_GROUP) ** -0.5),
        "c_re": nrm(ks[13], (DEPTH, G, SSM_GROUP, P), (2.0 * P) ** -0.5),
        "c_im": nrm(ks[14], (DEPTH, G, SSM_GROUP, P), (2.0 * P) ** -0.5),
        "d_skip": nrm(ks[15], (DEPTH, SSM_WIDTH), 1.0),
        "w_glu": nrm(ks[16], (DEPTH, SSM_WIDTH, SSM_WIDTH), SSM_WIDTH ** -0.5),
        "b_glu": nrm(ks[17], (DEPTH, SSM_WIDTH), 0.01),
        "attn_out_g": 1.0 + nrm(ks[18], (DEPTH, ATTN_WIDTH), 0.01),
        "ssm_out_g": 1.0 + nrm(ks[19], (DEPTH, SSM_WIDTH), 0.01),
        "w_out": nrm(ks[20], (DEPTH, MIX_WIDTH, D_MODEL), MIX_WIDTH ** -0.5),
        "norm2_g": 1.0 + nrm(ks[21], (DEPTH, D_MODEL), 0.01),
        "w_ff1": nrm(ks[22], (DEPTH, D_MODEL, D_FF), D_MODEL ** -0.5),
        "w_ff2": nrm(ks[23], (DEPTH, D_FF, D_MODEL), D_FF ** -0.5),
    }


def reference(x, c, w_ada, b_ada, norm1_g, w_in, q_norm_g, k_norm_g, lam_re, lam_im, log_step,
              b_re, b_im, c_re, c_im, d_skip, w_glu, b_glu, attn_out_g, ssm_out_g, w_out,
              norm2_g, w_ff1, w_ff2):
    for l in range(DEPTH):
        x = hybrid_layer(x, c, w_ada[l], b_ada[l], norm1_g[l], w_in[l], q_norm_g[l], k_norm_g[l],
                         lam_re[l], lam_im[l], log_step[l], b_re[l], b_im[l], c_re[l], c_im[l],
                         d_skip[l], w_glu[l], b_glu[l], attn_out_g[l], ssm_out_g[l], w_out[l],
                         norm2_g[l], w_ff1[l], w_ff2[l])
    return x
```

```python
import bisect
import contextlib
import math
import os
import numpy as np
import concourse.bass as bass
import concourse.mybir as mybir
from concourse.bass_utils import run_bass_kernel_spmd

F32 = mybir.dt.float32
BF16 = mybir.dt.bfloat16
AF = mybir.ActivationFunctionType
ALU = mybir.AluOpType
AX = mybir.AxisListType
EPOCH = 30000
N_DMA_SEMS = 48
ESZ = {F32: 4, BF16: 2}

D_MODEL = 2048
SEQ = 8192
NTOK = 4096
NPRE = 4096
KVPRE = 2048
TT = 512
EPS = 1e-6


class V:
    __slots__ = ("ap", "space", "lo", "hi")

    def __init__(self, ap, space, lo, hi):
        self.ap, self.space, self.lo, self.hi = ap, space, lo, hi

    def w(self, ap):
        return V(ap, self.space, self.lo, self.hi)


class Tile:
    def __init__(self, ap, space, lo, free_shape, esz):
        self.ap, self.space, self.lo, self.free_shape, self.esz = ap, space, lo, tuple(free_shape), esz
        n = 1
        for s in free_shape:
            n *= s
        self.hi = lo + n * esz

    def full(self):
        return V(self.ap, self.space, self.lo, self.hi)

    def __getitem__(self, idx):
        if not isinstance(idx, tuple):
            idx = (idx,)
        return self.v(idx)

    def v(self, idx, p=None):
        idx = tuple(idx) + (slice(None),) * (len(self.free_shape) - len(idx))
        strides = []
        st = 1
        for s in reversed(self.free_shape):
            strides.append(st)
            st *= s
        strides = strides[::-1]
        mn = mx = 0
        for i, s, stv in zip(idx, self.free_shape, strides):
            if isinstance(i, int):
                mn += i * stv
                mx += i * stv
            else:
                a, b, c = i.indices(s)
                assert b > a
                last = a + ((b - 1 - a) // c) * c
                mn += a * stv
                mx += last * stv
        ps = slice(None) if p is None else p
        if self.space == "ps":
            return V(self.ap[(ps,) + idx], self.space, self.lo, self.hi)
        return V(self.ap[(ps,) + idx], self.space, self.lo + mn * self.esz, self.lo + (mx + 1) * self.esz)


def dr(ap, name, lo=0, hi=1 << 40):
    return V(ap, ("dram", name), lo, hi)


class Arena:
    def __init__(self, nc, nbytes, name="arena"):
        self.nbytes = nbytes
        self.t = nc.alloc_sbuf_tensor(name, [128, nbytes // 4], F32)
        self.ap = self.t.ap()
        self.off = 0
        self.marks = []
        self.peak = 0

    def tile(self, free_shape, dtype):
        esz = ESZ[dtype]
        n = 1
        for s in free_shape:
            n *= s
        nb = (n * esz + 63) // 64 * 64
        lo = self.off
        assert lo + nb <= self.nbytes, f"SBUF arena overflow {lo}+{nb}>{self.nbytes}"
        self.off += nb
        self.peak = max(self.peak, self.off)
        ap = self.ap[:, lo // 4:(lo + nb) // 4]
        if dtype != F32:
            ap = ap.bitcast(dtype)
        ap = ap[:, 0:n]
        if len(free_shape) > 1:
            names = " ".join(f"a{i}" for i in range(len(free_shape)))
            kw = {f"a{i}": s for i, s in enumerate(free_shape)}
            ap = ap.rearrange(f"p ({names}) -> p {names}", **kw)
        return Tile(ap, "sb", lo, free_shape, esz)

    def mark(self):
        self.marks.append(self.off)

    def release(self):
        self.off = self.marks.pop()


class Sched:
    ENGS = ["pe", "act", "dve", "pool", "sp"]

    def __init__(self, nc):
        self.nc = nc
        self.ops = {e: [] for e in self.ENGS}
        self.count = {e: 0 for e in self.ENGS}
        self.waited = {e: {} for e in self.ENGS}
        self.iv = {}
        self.dma_uses = [0] * N_DMA_SEMS
        self.dma_pool = {"sp": list(range(0, 28)), "pool": list(range(28, 44)), "act": list(range(44, 48))}
        self.dma_rr = {q: 0 for q in self.dma_pool}

    def _split(self, space, pos):
        starts, recs = self.iv.setdefault(space, ([], []))
        i = bisect.bisect_right(starts, pos) - 1
        if i >= 0:
            r = recs[i]
            if r["lo"] < pos < r["hi"]:
                r2 = {"lo": pos, "hi": r["hi"], "w": r["w"], "r": list(r["r"])}
                r["hi"] = pos
                starts.insert(i + 1, pos)
                recs.insert(i + 1, r2)

    def _records(self, space, lo, hi):
        starts, recs = self.iv.setdefault(space, ([], []))
        self._split(space, lo)
        self._split(space, hi)
        i = bisect.bisect_left(starts, lo)
        out = []
        cur = lo
        while cur < hi:
            if i < len(starts) and starts[i] == cur:
                out.append(recs[i])
                cur = recs[i]["hi"]
                i += 1
            else:
                nxt = min(starts[i] if i < len(starts) else hi, hi)
                r = {"lo": cur, "hi": nxt, "w": None, "r": []}
                starts.insert(i, cur)
                recs.insert(i, r)
                out.append(r)
                i += 1
                cur = nxt
        return out

    def _add(self, eng, fn, reads, writes, is_dma):
        if is_dma:
            pool_ = self.dma_pool[eng]
            si = pool_[self.dma_rr[eng] % len(pool_)]
            self.dma_rr[eng] += 1
            prev = self.dma_uses[si]
            self.dma_uses[si] += 1
            me = ("dma", si, 16 * (prev + 1))
        else:
            self.count[eng] += 1
            me = ("eng", eng, self.count[eng])
        raw = set()
        deps = set()
        for v in reads:
            for r in self._records(v.space, v.lo, v.hi):
                if r["w"] is not None:
                    raw.add(r["w"])
                    deps.add(r["w"])
                if v.space == "ps":
                    deps.update(d for d in r["r"] if d[1] != eng)
                r["r"].append(me)
                if len(r["r"]) > 16:
                    latest = {}
                    for d in r["r"]:
                        k = (d[0], d[1])
                        if k not in latest or d[2] > latest[k][2]:
                            latest[k] = d
                    r["r"] = list(latest.values())
        for v in writes:
            for r in self._records(v.space, v.lo, v.hi):
                if r["w"] is not None:
                    deps.add(r["w"])
                deps.update(r["r"])
            starts, rl = self.iv[v.space]
            i0 = bisect.bisect_left(starts, v.lo)
            i1 = bisect.bisect_left(starts, v.hi)
            del starts[i0:i1]
            del rl[i0:i1]
            starts.insert(i0, v.lo)
            rl.insert(i0, {"lo": v.lo, "hi": v.hi, "w": me, "r": []})
        deps.discard(me)
        if is_dma and prev > 0:
            deps.add(("dma", si, 16 * prev))
        need = {}
        for d in deps:
            if d[0] == "eng":
                if d[1] == eng and eng == "pe":
                    continue
                key = ("eng", d[1])
            else:
                key = ("dma", d[1])
            val = d[2]
            if self.waited[eng].get(key, 0) >= val:
                continue
            if need.get(key, 0) < val:
                need[key] = val
        for k, v in need.items():
            self.waited[eng][k] = v
        self.ops[eng].append((fn, list(need.items()), me))
        return me

    def op(self, eng, fn, reads=(), writes=()):
        return self._add(eng, fn, list(reads), list(writes), False)

    def dma(self, queue, out, in_, **kw):
        fn = lambda e: e.dma_start(out=out.ap, in_=in_.ap, **kw)
        return self._add(queue, fn, [in_], [out], True)

    def barrier(self):
        for e in self.ENGS:
            need = {}
            for e2 in self.ENGS:
                if e2 != e and self.count[e2] > 0:
                    need[("eng", e2)] = self.count[e2]
            for si, u in enumerate(self.dma_uses):
                if u > 0:
                    need[("dma", si)] = 16 * u
            waits = []
            for k, v in need.items():
                if self.waited[e].get(k, 0) < v:
                    self.waited[e][k] = v
                    waits.append((k, v))
            if waits:
                self.ops[e].append((None, waits, None))

    def emit(self):
        nc = self.nc
        with contextlib.ExitStack() as st:
            esem = {}
            for e in self.ENGS:
                nep = max((self.count[e] + EPOCH - 1) // EPOCH, 1)
                esem[e] = [st.enter_context(nc.semaphore(f"s_{e}_{i}")) for i in range(nep)]
            dsem = [st.enter_context(nc.semaphore(f"s_dma_{i}")) for i in range(N_DMA_SEMS)]
            block = st.enter_context(nc.Block())

            def replay(ename, eobj):
                for fn, waits, me in self.ops[ename]:
                    for key, val in waits:
                        if key[0] == "eng":
                            eobj.wait_ge(esem[key[1]][(val - 1) // EPOCH], (val - 1) % EPOCH + 1)
                        else:
                            eobj.wait_ge(dsem[key[1]], val)
                    if fn is None:
                        continue
                    inst = fn(eobj)
                    if me[0] == "eng":
                        n = me[2]
                        inst.then_inc(esem[ename][(n - 1) // EPOCH], 1)
                    else:
                        inst.then_inc(dsem[me[1]], 16)

            @block.tensor
            def _(e):
                replay("pe", e)

            @block.scalar
            def _(e):
                replay("act", e)

            @block.vector
            def _(e):
                replay("dve", e)

            @block.gpsimd
            def _(e):
                replay("pool", e)

            @block.sync
            def _(e):
                replay("sp", e)


class K:
    def __init__(self, nc):
        self.nc = nc
        self.s = Sched(nc)
        self.rr = 0

    def _apv(self, x):
        return x.ap if isinstance(x, V) else x

    def act(self, out, in_, func, bias=None, scale=None, accum_out=None):
        kw = {}
        rd = [in_]
        wr = [out]
        if bias is not None:
            kw["bias"] = self._apv(bias)
            if isinstance(bias, V):
                rd.append(bias)
        if scale is not None:
            kw["scale"] = self._apv(scale)
            if isinstance(scale, V):
                rd.append(scale)
        if accum_out is not None:
            kw["accum_out"] = accum_out.ap
            wr.append(accum_out)
        self.s.op("act", lambda e: e.activation(out=out.ap, in_=in_.ap, func=func, **kw), rd, wr)

    def tt(self, eng, out, a, b, op):
        self.s.op(eng, lambda e: e.tensor_tensor(out=out.ap, in0=a.ap, in1=b.ap, op=op), [a, b], [out])

    def ts(self, eng, out, a, s1, s2=None, op0=ALU.mult, op1=None, accum_out=None):
        rd = [a] + [x for x in (s1, s2) if isinstance(x, V)]
        kw = {}
        if op1 is not None:
            kw["op1"] = op1
        wr = [out]
        if accum_out is not None:
            kw["accum_out"] = accum_out.ap
            wr.append(accum_out)
        self.s.op(eng, lambda e: e.tensor_scalar(out=out.ap, in0=a.ap, scalar1=self._apv(s1), scalar2=self._apv(s2),
                                                 op0=op0, **kw), rd, wr)

    def stt(self, eng, out, a, scalar, b, op0, op1):
        rd = [a, b] + ([scalar] if isinstance(scalar, V) else [])
        self.s.op(eng, lambda e: e.scalar_tensor_tensor(out=out.ap, in0=a.ap, scalar=self._apv(scalar), in1=b.ap,
                                                        op0=op0, op1=op1), rd, [out])

    def copy(self, eng, out, in_):
        if eng == "act":
            self.s.op("act", lambda e: e.copy(out=out.ap, in_=in_.ap), [in_], [out])
        else:
            self.s.op(eng, lambda e: e.tensor_copy(out=out.ap, in_=in_.ap), [in_], [out])

    def memset(self, eng, out, val):
        self.s.op(eng, lambda e: e.memset(out.ap, val), [], [out])

    def recip(self, out, in_):
        self.s.op("dve", lambda e: e.reciprocal(out=out.ap, in_=in_.ap), [in_], [out])

    def reduce_sum(self, out, in_):
        self.s.op("dve", lambda e: e.tensor_reduce(out=out.ap, in_=in_.ap, axis=AX.X, op=ALU.add), [in_], [out])

    def mm(self, out, lhsT, rhs, start, stop):
        self.s.op("pe", lambda e: e.matmul(out.ap, lhsT=lhsT.ap, rhs=rhs.ap, start=start, stop=stop),
                  [lhsT, rhs], [out])

    def tr(self, out, in_, ident):
        self.s.op("pe", lambda e: e.transpose(out=out.ap, in_=in_.ap, identity=ident.ap), [in_, ident], [out])

    def dma(self, q, out, in_):
        self.s.dma(q, out, in_)

    def cast_rr(self, out, in_):
        eng = ("act", "dve")[self.rr % 2]
        self.rr += 1
        self.copy(eng, out, in_)


def bc(v, shape, axis):
    return v.w(v.ap.unsqueeze(axis).broadcast_to(shape))


DEBUG = os.environ.get("MK_DEBUG", "")


def build(stop_after=99):
    nc = bass.Bass("TRN2", target_bir_lowering=False)
    k = K(nc)
    dbg = set(DEBUG.split(",")) if DEBUG else set()

    def din(name, shape, dt=F32):
        return nc.dram_tensor(name, list(shape), dt, kind="ExternalInput").ap()

    def dscr(name, shape, dt):
        kind = "ExternalOutput" if name in dbg else "Internal"
        return nc.dram_tensor(name, list(shape), dt, kind=kind).ap()

    xo_d = din("xo", [NTOK, D_MODEL])
    xp_d = din("xp", [NPRE, D_MODEL])
    pmask_d = din("pmask", [128, 1])
    ccol_d = din("c_col", [128, 16])
    wada_d = din("w_ada", [2048, 12288])
    bada_d = din("b_ada", [1, 12288])
    g1n_d = din("g1n", [128, 16])
    g2n_d = din("g2n", [128, 16])
    win_d = din("w_in", [2048, 4096])
    wglu_d = din("w_glu", [1024, 1024])
    wout_d = din("w_out", [2048, 2048])
    wff1_d = din("w_ff1", [2048, 8192])
    wff2_d = din("w_ff2", [8192, 2048])
    qg_d = din("qg", [128, 1])
    kg_d = din("kg", [128, 1])
    og_d = din("og", [128, 16])
    bglu_d = din("bglu", [128, 8])
    dsk_d = din("dsk", [128, 8])
    ident_d = din("ident", [128, 128])
    tri_d = din("tri", [128, 128])
    abias_d = din("abias", [128, 24, 256])
    sB_d = {n: din("sB_" + n, [128, 8, 64]) for n in ("lre", "lim", "lst", "bre", "bim")}
    sS_d = {n: din("sS_" + n, [128, 32]) for n in ("lre", "lim", "lst")}
    bsel_d = {n: din("bsel_" + n, [128, 32, 32]) for n in ("re", "im")}
    cblk_d = {n: din("cblk_" + n, [128, 32, 128]) for n in ("re", "im")}
    mask8_d = din("mask8", [128, 8])

    out_d = nc.dram_tensor("out", [NTOK, D_MODEL], F32, kind="ExternalOutput").ap()

    winb_d = dscr("winb", [2048, 4096], BF16)
    wglub_d = dscr("wglub", [1024, 1024], BF16)
    woutb_d = dscr("woutb", [2048, 2048], BF16)
    wff1b_d = dscr("wff1b", [2048, 8192], BF16)
    wff2b_d = dscr("wff2b", [8192, 2048], BF16)
    QT_d = dscr("QT", [8, 128, NTOK], BF16)
    KT_d = dscr("KT", [8, 128, NTOK + KVPRE], BF16)
    Vs_d = dscr("Vs", [NTOK + KVPRE, 1024], BF16)
    UT_d = dscr("UT", [8, 128, NTOK], F32)
    Upre_d = dscr("Upre", [NPRE, 1024], BF16)
    NUM_d = [dscr(f"NUM{p}", [NTOK, 8, 129], F32) for p in range(3)]
    YG_d = dscr("YG", [8, 128, NTOK], F32)
    modbc_dbg = dscr("modbc_dbg", [128, 12288], F32) if "modbc_dbg" in dbg else None

    A = Arena(nc, 190 * 1024)
    ps = [Tile(nc.alloc_psum_tensor(f"ps{i}", [128, 512], F32).ap(), "ps", i * 2048, [512], 4) for i in range(8)]

    def psv(bank, a, b):
        return ps[bank][a:b]

    ident = A.tile([128], F32)
    k.dma("sp", ident.full(), dr(ident_d, "ident"))
    identb = A.tile([128], BF16)
    k.copy("dve", identb.full(), ident.full())
    ones_f = A.tile([128], F32)
    k.memset("dve", ones_f.full(), 1.0)
    pm = A.tile([1], F32)
    k.dma("sp", pm.full(), dr(pmask_d, "pmask"))
    g1n = A.tile([16], F32)
    g2n = A.tile([16], F32)
    og = A.tile([16], F32)
    bglu = A.tile([8], F32)
    dsk = A.tile([8], F32)
    qgs = A.tile([1], F32)
    kgs = A.tile([1], F32)
    for t_, d_, n_ in ((g1n, g1n_d, "g1n"), (g2n, g2n_d, "g2n"), (og, og_d, "og"), (bglu, bglu_d, "bglu"),
                       (dsk, dsk_d, "dsk"), (qgs, qg_d, "qg"), (kgs, kg_d, "kg")):
        k.dma("sp", t_.full(), dr(d_, n_))
    k.ts("dve", qgs.full(), qgs.full(), 128.0 ** -0.5, op0=ALU.mult)
    m1 = A.tile([16], F32)
    sh1 = A.tile([16], F32)
    m2 = A.tile([16], F32)
    sh2 = A.tile([16], F32)

    def cast_dma0(src_d, dst_d, name, R, rows_per):
        for r0 in range(0, R, rows_per):
            k.dma("pool", dr(dst_d[r0:r0 + rows_per, :], name, r0, r0 + rows_per), dr(src_d[r0:r0 + rows_per, :], name + "_src"))

    cast_dma0(win_d, winb_d, "winb", 2048, 256)
    cast_dma0(wglu_d, wglub_d, "wglub", 1024, 256)
    cast_dma0(wout_d, woutb_d, "woutb", 2048, 256)
    cast_dma0(wff1_d, wff1b_d, "wff1b", 2048, 128)
    cast_dma0(wff2_d, wff2b_d, "wff2b", 8192, 512)
    A.mark()
    ccol = A.tile([16], F32)
    k.dma("sp", ccol.full(), dr(ccol_d, "ccol"))
    scv = A.tile([16], F32)
    k.act(scv.full(), ccol.full(), AF.Silu)
    rep = A.tile([16, 128], F32)
    k.copy("dve", rep.full(), bc(scv.full(), [128, 16, 128], 2))
    modbc = A.tile([12288], F32)
    k.dma("sp", modbc.full(), dr(bada_d.partition_broadcast(128), "bada"))
    A.mark()
    wa = [A.tile([16, 512], F32) for _ in range(2)]
    wada_v = wada_d.rearrange("(kt p) c -> p kt c", p=128)
    for ch in range(24):
        buf = wa[ch % 2]
        k.dma("sp", buf.full(), dr(wada_v[:, :, ch * 512:(ch + 1) * 512], "w_ada"))
        pb = psv(ch % 2, 0, 512)
        for kt in range(16):
            k.mm(pb, rep[kt], buf[kt], kt == 0, kt == 15)
        k.tt("dve", modbc[ch * 512:(ch + 1) * 512], pb, modbc[ch * 512:(ch + 1) * 512], ALU.add)
    A.release()
    if modbc_dbg is not None:
        k.dma("pool", dr(modbc_dbg, "modbc_dbg"), modbc.full())
    if stop_after <= -2:
        k.s.barrier()
        k.s.emit()
        return nc
    sc1 = A.tile([16], F32)
    sc2 = A.tile([16], F32)
    for vi, (off, dst) in enumerate(((0, sh1), (2048, sc1), (6144, sh2), (8192, sc2))):
        for kq in range(4):
            bank = 2 + (vi * 4 + kq) % 2
            for j in range(4):
                kt = kq * 4 + j
                k.tr(psv(bank, j * 128, (j + 1) * 128), modbc[off + kt * 128: off + (kt + 1) * 128], ident.full())
            src = ps[bank].full()
            src = src.w(src.ap.rearrange("p (a b) -> p a b", b=128)[:, :, 0])
            k.copy("dve", dst[kq * 4:(kq + 1) * 4], src)
    k.stt("dve", m1.full(), sc1.full(), 1.0, g1n.full(), ALU.add, ALU.mult)
    k.stt("dve", m2.full(), sc2.full(), 1.0, g2n.full(), ALU.add, ALU.mult)
    A.release()
    g1bc_t = A.tile([2048], F32)
    g2bc_t = A.tile([2048], F32)
    k.copy("dve", g1bc_t.full(), modbc[4096:6144])
    k.copy("act", g2bc_t.full(), modbc[10240:12288])
    if stop_after <= 0:
        k.s.barrier()
        k.s.emit()
        return nc

    A.mark()
    xin = [A.tile([2048], F32) for _ in range(2)]
    junk = A.tile([2048], BF16)
    hT = [A.tile([16, 512], BF16) for _ in range(2)]
    wblk = [A.tile([16, 512], BF16) for _ in range(3)]
    sqt = [A.tile([512], F32) for _ in range(2)]
    rst = [A.tile([512], F32) for _ in range(2)]
    obt = [A.tile([512], BF16) for _ in range(3)]
    oft = [A.tile([512], F32) for _ in range(2)]
    stat = [A.tile([8], F32) for _ in range(2)]
    winb_v = winb_d.rearrange("(kt p) c -> p kt c", p=128)
    ctr = {"w": 0, "p": 0, "s": 0, "o": 0, "f": 0}
    n_tiles1 = 16 if stop_after >= 2 else int(os.environ.get("MK_P1_TILES", "16"))
    tl1 = os.environ.get("MK_P1_LIST", "")
    tiles1 = [int(t_) for t_ in tl1.split(",")] if tl1 else list(range(n_tiles1))
    pending = []

    def flush_pending():
        while pending:
            pending.pop(0)()

    def emit_norm(ti, sub):
        is_pre = ti < 8
        tok0 = (ti % 8) * TT
        src = xp_d if is_pre else xo_d
        h = hT[ti % 2]
        stt_ = stat[ti % 2]
        if sub == 0:
            k.memset("dve", stt_.full(), 0.0)
        xt = xin[(ti * 4 + sub) % 2]
        k.dma("sp", xt.full(), dr(src[tok0 + sub * 128: tok0 + (sub + 1) * 128, :], "x"))
        ss = stt_[sub:sub + 1]
        rs = stt_[4 + sub:5 + sub]
        k.act(junk.full(), xt.full(), AF.Square, accum_out=ss)
        k.act(rs, ss, AF.Sqrt, scale=1.0 / D_MODEL, bias=EPS)
        k.recip(rs, rs)
        k.ts("dve", xt.full(), xt.full(), rs, op0=ALU.mult)
        for kq in range(4):
            bank = kq % 2
            for j in range(4):
                kt = kq * 4 + j
                k.tr(psv(bank, j * 128, (j + 1) * 128), xt[kt * 128:(kt + 1) * 128], ident.full())
            for j in range(4):
                kt = kq * 4 + j
                k.act(h[kt, sub * 128:(sub + 1) * 128], psv(bank, j * 128, (j + 1) * 128), AF.Identity,
                      scale=m1[kt:kt + 1], bias=sh1[kt:kt + 1])

    for sub in range(4):
        emit_norm(tiles1[0], sub)
    for tidx, ti in enumerate(tiles1):
        is_pre = ti < 8
        tok0 = (ti % 8) * TT
        h = hT[ti % 2]
        nxt = tiles1[tidx + 1] if tidx + 1 < len(tiles1) else None
        nsub = [0]
        blocks = []
        if not is_pre:
            blocks += [("q", 0), ("q", 1)]
        if ti >= 4:
            blocks += [("k", 2), ("k", 3), ("v", 4), ("v", 5)]
        blocks += [("u", 6), ("u", 7)]
        w0 = (ti - 4) * TT
        for kind, cb in blocks:
            wb = wblk[ctr["w"] % 3]
            ctr["w"] += 1
            k.dma("sp", wb.full(), dr(winb_v[:, :, cb * 512:(cb + 1) * 512], "winb"))
            feature_major = kind in ("q", "k") or (kind == "u" and not is_pre)
            if feature_major:
                for co in range(4):
                    pb = psv(2 + ctr["p"] % 3, 0, 512)
                    ctr["p"] += 1
                    for kt in range(16):
                        k.mm(pb, wb[kt, co * 128:(co + 1) * 128], h[kt], kt == 0, kt == 15)
                    if kind in ("q", "k"):
                        sq = sqt[ctr["s"] % 2].full()
                        rs = rst[ctr["s"] % 2].full()
                        pss = psv(5 + ctr["s"] % 2, 0, 512)
                        ctr["s"] += 1
                        k.act(sq, pb, AF.Square)
                        flush_pending()

                        def fin(sq=sq, rs=rs, pss=pss, pb=pb, kind=kind, cb=cb, co=co, tok0=tok0, w0=w0):
                            k.mm(pss, ones_f.full(), sq, True, True)
                            k.act(rs, pss, AF.Sqrt, scale=1.0 / 128, bias=EPS)
                            k.recip(rs, rs)
                            ob = obt[ctr["o"] % 3].full()
                            ctr["o"] += 1
                            gv = qgs if kind == "q" else kgs
                            k.stt("dve", ob, pb, gv.full(), rs, ALU.mult, ALU.mult)
                            head = (cb % 2) * 4 + co
                            if kind == "q":
                                k.dma("pool", dr(QT_d[head, :, tok0:tok0 + TT], "QT", tok0, tok0 + TT), ob)
                            else:
                                k.dma("pool", dr(KT_d[head, :, w0:w0 + TT], "KT", w0, w0 + TT), ob)
                        pending.append(fin)
                    else:
                        flush_pending()
                        of = oft[ctr["f"] % 2].full()
                        ctr["f"] += 1
                        k.copy("act", of, pb)
                        ut = (cb - 6) * 4 + co
                        k.dma("pool", dr(UT_d[ut, :, tok0:tok0 + TT], "UT", tok0, tok0 + TT), of)
            else:
                for sub in range(4):
                    pb = psv(2 + ctr["p"] % 3, 0, 512)
                    ctr["p"] += 1
                    for kt in range(16):
                        k.mm(pb, h[kt, sub * 128:(sub + 1) * 128], wb[kt], kt == 0, kt == 15)
                    flush_pending()
                    ob = obt[ctr["o"] % 3].full()
                    ctr["o"] += 1
                    if kind == "v":
                        if is_pre:
                            k.ts("dve", ob, pb, pm.full(), op0=ALU.mult)
                        else:
                            k.copy("act", ob, pb)
                        r0 = w0 + sub * 128
                        k.dma("pool", dr(Vs_d[r0:r0 + 128, (cb - 4) * 512:(cb - 3) * 512], "Vs", r0, r0 + 128), ob)
                    else:
                        k.copy("act", ob, pb)
                        r0 = tok0 + sub * 128
                        k.dma("pool", dr(Upre_d[r0:r0 + 128, (cb - 6) * 512:(cb - 5) * 512], "Upre", r0, r0 + 128), ob)
            if nxt is not None and nsub[0] < 4 and (len(blocks) - blocks.index((kind, cb)) <= 4 or len(blocks) <= 4):
                emit_norm(nxt, nsub[0])
                nsub[0] += 1
        while nxt is not None and nsub[0] < 4:
            emit_norm(nxt, nsub[0])
            nsub[0] += 1
    flush_pending()
    A.release()
    if stop_after <= 1:
        k.s.barrier()
        k.s.emit()
        return nc
    return build_rest(nc, k, A, ps, psv, locals())


def _consts():
    ident = np.eye(128, dtype=np.float32)
    tri = np.triu(np.ones((128, 128), np.float32))
    kk = np.arange(128)[:, None]
    qq = np.arange(128)[None, :]
    ab = np.full((128, 3, 8, 2, 128), -30000.0, np.float32)
    for pi, d in enumerate((1, 4, 16)):
        for h in range(8):
            slope = 2.0 ** (-8.0 * (h + 1.0) / 8.0)
            st0 = qq - kk + 128
            b0 = np.where(kk >= qq, -slope * st0 * d, -30000.0)
            st1 = qq - kk
            b1 = np.where(kk <= qq, -slope * st1 * d, -30000.0)
            ab[:, pi, h, 0, :] = b0
            ab[:, pi, h, 1, :] = b1
    mask8 = np.zeros((8, 16, 8), np.float32)
    for g in range(8):
        mask8[g, :, g] = 1.0
    return ident, tri, ab.reshape(128, 24, 256), mask8.reshape(128, 8)


def prep_inputs(inputs):
    L = 0
    f = lambda a: np.ascontiguousarray(np.asarray(a, dtype=np.float32))
    x = np.asarray(inputs["x"], dtype=np.float32)
    c = np.asarray(inputs["c"], dtype=np.float32)
    col16 = lambda v: f(np.asarray(v).reshape(16, 128).T)
    col8 = lambda v: f(np.asarray(v).reshape(8, 128).T)
    ident, tri, abias, mask8 = _consts()
    lam_re = np.asarray(inputs["lam_re"][L], np.float32)
    lam_im = np.asarray(inputs["lam_im"][L], np.float32)
    lst = np.asarray(inputs["log_step"][L], np.float32)
    b_re = np.asarray(inputs["b_re"][L], np.float32)
    b_im = np.asarray(inputs["b_im"][L], np.float32)
    c_re = np.asarray(inputs["c_re"][L], np.float32)
    c_im = np.asarray(inputs["c_im"][L], np.float32)

    def layB(a):
        t = a.reshape(8, 8, 64).transpose(1, 0, 2)
        return f(np.broadcast_to(t[:, None], (8, 16, 8, 64)).reshape(128, 8, 64))

    def layS(a):
        return f(a.reshape(32, 2, 64).transpose(1, 2, 0).reshape(128, 32))

    lstB = np.broadcast_to(lst[:, None], (64, 64))
    shared = {
        "w_ada": f(inputs["w_ada"][L]), "b_ada": f(np.asarray(inputs["b_ada"][L])[None, :]),
        "g1n": col16(inputs["norm1_g"][L]), "g2n": col16(inputs["norm2_g"][L]),
        "w_in": f(inputs["w_in"][L]), "w_glu": f(inputs["w_glu"][L]), "w_out": f(inputs["w_out"][L]),
        "w_ff1": f(inputs["w_ff1"][L]), "w_ff2": f(inputs["w_ff2"][L]),
        "qg": f(np.asarray(inputs["q_norm_g"][L]).reshape(128, 1)),
        "kg": f(np.asarray(inputs["k_norm_g"][L]).reshape(128, 1)),
        "og": col16(np.concatenate([np.asarray(inputs["attn_out_g"][L]), np.asarray(inputs["ssm_out_g"][L])])),
        "bglu": col8(inputs["b_glu"][L]), "dsk": col8(inputs["d_skip"][L]),
        "ident": ident, "tri": tri, "abias": abias, "mask8": mask8,
        "sB_lre": layB(lam_re), "sB_lim": layB(lam_im), "sB_lst": layB(lstB),
        "sB_bre": f(b_re.reshape(8, 8, 64, 16).transpose(1, 3, 0, 2).reshape(128, 8, 64)),
        "sB_bim": f(b_im.reshape(8, 8, 64, 16).transpose(1, 3, 0, 2).reshape(128, 8, 64)),
        "sS_lre": layS(lam_re), "sS_lim": layS(lam_im), "sS_lst": layS(lstB),
    }
    for nm, b in (("re", b_re), ("im", b_im)):
        arr = np.zeros((2, 64, 32, 2, 16), np.float32)
        br = b.reshape(32, 2, 64, 16)
        for gh in range(2):
            arr[gh, :, :, gh, :] = br[:, gh].transpose(1, 0, 2)
        shared["bsel_" + nm] = arr.reshape(128, 32, 32)
    for nm, cm in (("re", c_re), ("im", c_im)):
        arr = np.zeros((2, 64, 32, 8, 16), np.float32)
        for g in range(64):
            arr[g % 2, :, g // 2, g % 8, :] = cm[g].T
        shared["cblk_" + nm] = arr.reshape(128, 32, 128)
    in_maps = []
    zeros_pre = np.zeros((NPRE, D_MODEL), np.float32)
    for core in range(8):
        b, half = divmod(core, 2)
        m = dict(shared)
        m["xo"] = f(x[b, half * NTOK:(half + 1) * NTOK])
        m["xp"] = f(x[b, 0:NPRE]) if half == 1 else zeros_pre
        m["pmask"] = np.full((128, 1), float(half), np.float32)
        m["c_col"] = f(c[b].reshape(16, 128).T)
        in_maps.append(m)
    return in_maps


def kernel(**inputs):
    in_maps = prep_inputs(inputs)
    nc = build()
    res = run_bass_kernel_spmd(nc, in_maps, core_ids=list(range(8)))
    out = np.empty((4, SEQ, D_MODEL), np.float32)
    for core in range(8):
        b, half = divmod(core, 2)
        out[b, half * NTOK:(half + 1) * NTOK] = res.results[core]["out"]
    return out


def build_rest(nc, k, A, ps, psv, L):
    stop_after = L["stop_after"]
    ident, identb, ones_f, pm = L["ident"], L["identb"], L["ones_f"], L["pm"]
    m2, sh2, bglu, dsk = L["m2"], L["sh2"], L["bglu"], L["dsk"]
    QT_d, KT_d, Vs_d, UT_d, Upre_d, NUM_d, YG_d = (L[n] for n in ("QT_d", "KT_d", "Vs_d", "UT_d", "Upre_d", "NUM_d", "YG_d"))
    xo_d, out_d = L["xo_d"], L["out_d"]
    skip = set(os.environ.get("MK_SKIP", "").split(","))

    def finish():
        k.s.barrier()
        k.s.emit()
        return nc

    A.mark()
    if "attn" not in skip:
        abias = A.tile([24, 256], F32)
        k.dma("sp", abias.full(), dr(L["abias_d"], "abias"))
        KTw = [A.tile([4096], BF16) for _ in range(2)]
        QTs = [A.tile([2048], BF16) for _ in range(2)]
        V1 = [A.tile([17, 129], BF16) for _ in range(2)]
        V4 = [A.tile([5, 4, 129], BF16) for _ in range(2)]
        V16 = [A.tile([2, 16, 129], BF16) for _ in range(2)]
        ssb = [A.tile([256], F32) for _ in range(3)]
        pbf = [A.tile([256], BF16) for _ in range(3)]
        stg = [A.tile([16, 129], F32) for _ in range(2)]
        cu = {"u": 0, "g": 0}
        n_span = int(os.environ.get("MK_SPANS", "2"))
        n_head = int(os.environ.get("MK_HEADS", "8"))
        for sp in range(n_span):
            S0 = sp * 2048
            for h in range(n_head):
                bi = (sp * 8 + h) % 2
                kw_, qs_, v1, v4, v16 = KTw[bi], QTs[bi], V1[bi], V4[bi], V16[bi]
                hc = slice(h * 128, (h + 1) * 128)
                k.dma("sp", kw_.full(), dr(KT_d[h, :, S0:S0 + 4096], "KT", S0, S0 + 4096))
                k.dma("sp", qs_.full(), dr(QT_d[h, :, S0:S0 + 2048], "QT", S0, S0 + 2048))
                wb1 = S0 + 2048 - 128
                src1 = Vs_d[wb1:wb1 + 17 * 128, hc].rearrange("(b p) e -> p b e", p=128)
                k.dma("sp", v1[0:9, 0:128], dr(src1[:, 0:9, :], "Vs", wb1, wb1 + 9 * 128))
                k.dma("sp", v1[9:17, 0:128], dr(src1[:, 9:17, :], "Vs", wb1 + 9 * 128, wb1 + 17 * 128))
                wb4 = S0 + 2048 - 512
                src4 = Vs_d[wb4:wb4 + 2560, hc].rearrange("(j m r) e -> m j r e", j=5, m=128, r=4)
                for j4 in range(5):
                    k.dma("sp", v4[j4, :, 0:128], dr(src4[:, j4], "Vs", wb4 + 512 * j4, wb4 + 512 * (j4 + 1)))
                src16 = Vs_d[S0:S0 + 4096, hc].rearrange("(j m r) e -> m j r e", j=2, m=128, r=16)
                for jj in range(2):
                    k.dma("sp", v16[jj, :, 0:128], dr(src16[:, jj], "Vs", S0 + jj * 2048, S0 + (jj + 1) * 2048))
                k.memset("dve", v1[:, 128:129], 1.0)
                k.memset("dve", v4[:, :, 128:129], 1.0)
                k.memset("dve", v16[:, :, 128:129], 1.0)
                if sp == 0:
                    k.copy("dve", v1[0:1, 128:129], bc(pm.full(), [128, 1, 1], 1))
                    k.copy("dve", v4[0, :, 128:129], bc(pm.full(), [128, 4, 1], 1))
                    k.copy("dve", v16[0, :, 128:129], bc(pm.full(), [128, 16, 1], 1))
                for pi in range(3):
                    sg = stg[cu["g"] % 2]
                    cu["g"] += 1
                    for u in range(16):
                        if pi == 0:
                            q = qs_[128 * u:128 * (u + 1)]
                            kp = kw_[1920 + 128 * u:2048 + 128 * u]
                            kc = kw_[2048 + 128 * u:2176 + 128 * u]
                            vp, vc = v1[u], v1[u + 1]
                        elif pi == 1:
                            jq, r = divmod(u, 4)
                            q = qs_[slice(512 * jq + r, 512 * (jq + 1), 4)]
                            kp = kw_[slice(1536 + 512 * jq + r, 2048 + 512 * jq, 4)]
                            kc = kw_[slice(2048 + 512 * jq + r, 2560 + 512 * jq, 4)]
                            vp, vc = v4[jq, r], v4[jq + 1, r]
                        else:
                            r = u
                            q = qs_[slice(r, 2048, 16)]
                            kp = kw_[slice(r, 2048, 16)]
                            kc = kw_[slice(2048 + r, 4096, 16)]
                            vp, vc = v16[0, r], v16[1, r]
                        i = cu["u"]
                        cu["u"] += 1
                        pS = ps[i % 4]
                        pO = ps[4 + i % 4]
                        k.mm(pS[0:128], kp, q, True, True)
                        k.mm(pS[128:256], kc, q, True, True)
                        sb_ = ssb[i % 3].full()
                        pb_ = pbf[i % 3]
                        k.tt("dve", sb_, pS[0:256], abias[pi * 8 + h], ALU.add)
                        k.act(pb_.full(), sb_, AF.Exp)
                        k.mm(pO[0:129], pb_[0:128], vp, True, False)
                        k.mm(pO[0:129], pb_[128:256], vc, False, True)
                        k.copy("act", sg[u], pO[0:129])
                    nd = NUM_d[pi][S0:S0 + 2048, h, :]
                    if pi == 1:
                        dst = nd.rearrange("(j m r) e -> m j r e", j=4, m=128, r=4)
                        for j4 in range(4):
                            k.dma("pool", dr(dst[:, j4], f"NUM{pi}", S0 + 512 * j4, S0 + 512 * (j4 + 1)),
                                  sg[4 * j4:4 * (j4 + 1)])
                    else:
                        dst = nd.rearrange("(b m) e -> m b e", m=128) if pi == 0 else nd.rearrange("(m r) e -> m r e", r=16)
                        k.dma("pool", dr(dst, f"NUM{pi}", S0, S0 + 2048), sg.full())
    A.release()
    if stop_after <= 2:
        return finish()

    A.mark()
    if "ssm" not in skip:
        build_ssm(k, A, ps, L)
    A.release()
    if stop_after <= 3:
        return finish()
    build_phase3(k, A, ps, L)
    return finish()


def build_ssm(k, A, ps, L):
    pm, dsk = L["pm"], L["dsk"]
    UT_d, Upre_d, YG_d = L["UT_d"], L["Upre_d"], L["YG_d"]
    sB_d, sS_d, bsel_d, cblk_d = L["sB_d"], L["sS_d"], L["bsel_d"], L["cblk_d"]
    MUL, ADD, SUB = ALU.mult, ALU.add, ALU.subtract

    def load(shape, d_ap, name):
        t = A.tile(shape, F32)
        k.dma("sp", t.full(), dr(d_ap, name))
        return t

    def cpow(lre, lim, lst, fs, sign):
        T = lambda: A.tile(fs, F32)
        step, lr, li, m, sn, cs, zr, zi, t1, t2 = (T() for _ in range(10))
        k.act(step.full(), lst.full(), AF.Exp)
        k.tt("dve", lr.full(), lre.full(), step.full(), MUL)
        k.tt("dve", li.full(), lim.full(), step.full(), MUL)
        k.act(m.full(), lr.full(), AF.Exp, scale=sign / 16.0)
        k.act(sn.full(), li.full(), AF.Sin, scale=sign / 16.0)
        k.act(cs.full(), li.full(), AF.Sin, scale=-1.0 / 16.0, bias=math.pi / 2)
        k.tt("dve", zr.full(), m.full(), cs.full(), MUL)
        k.tt("dve", zi.full(), m.full(), sn.full(), MUL)
        for _ in range(4):
            k.tt("dve", t1.full(), zr.full(), zr.full(), MUL)
            k.tt("dve", t2.full(), zi.full(), zi.full(), MUL)
            k.stt("dve", zi.full(), zr.full(), 2.0, zi.full(), MUL, MUL)
            k.tt("dve", zr.full(), t1.full(), t2.full(), SUB)
        return zr, zi, lr, li

    def cmul(outr, outi, ar, ai, br, bi, t1, t2):
        k.tt("dve", t1, ar, br, MUL)
        k.tt("dve", t2, ai, bi, MUL)
        k.tt("dve", outr, t1, t2, SUB)
        k.tt("dve", t1, ar, bi, MUL)
        k.tt("dve", t2, ai, br, MUL)
        k.tt("dve", outi, t1, t2, ADD)

    def kappa(ar, ai, lre, lim, lst, fs):
        T = lambda: A.tile(fs, F32)
        am1, den, t1, t2, kr, ki = (T() for _ in range(6))
        k.ts("dve", am1.full(), ar.full(), -1.0, op0=ADD)
        k.tt("dve", t1.full(), lre.full(), lre.full(), MUL)
        k.tt("dve", t2.full(), lim.full(), lim.full(), MUL)
        k.tt("dve", den.full(), t1.full(), t2.full(), ADD)
        k.recip(den.full(), den.full())
        k.tt("dve", t1.full(), am1.full(), lre.full(), MUL)
        k.tt("dve", t2.full(), ai.full(), lim.full(), MUL)
        k.tt("dve", kr.full(), t1.full(), t2.full(), ADD)
        k.tt("dve", kr.full(), kr.full(), den.full(), MUL)
        k.tt("dve", t1.full(), ai.full(), lre.full(), MUL)
        k.tt("dve", t2.full(), am1.full(), lim.full(), MUL)
        k.tt("dve", ki.full(), t1.full(), t2.full(), SUB)
        k.tt("dve", ki.full(), ki.full(), den.full(), MUL)
        return kr, ki

    Bblk = {n: A.tile([8, 8, 64], BF16) for n in ("re", "im")}
    Cb = {n: A.tile([32, 128], BF16) for n in ("re", "imn")}
    Tpos = {n: A.tile([32, 128], F32) for n in ("re", "im")}
    TinvT = {n: A.tile([4096], F32) for n in ("re", "im")}
    a128 = {n: A.tile([32], F32) for n in ("re", "im")}
    trib = A.tile([128], BF16)
    cr = A.tile([32], F32)
    ci = A.tile([32], F32)
    Gend = {n: A.tile([32], F32) for n in ("re", "im")}
    sr_ = A.tile([32], F32)
    si_ = A.tile([32], F32)
    tA = A.tile([32], F32)
    tB = A.tile([32], F32)
    A.mark()
    TinvTb = {n: A.tile([4096], BF16) for n in ("re", "im")}
    Bsel = {n: A.tile([32, 32], F32) for n in ("re", "im")}

    A.mark()
    fsB = [8, 64]
    lre, lim, lst = (load(fsB, sB_d[n], "sB" + n) for n in ("lre", "lim", "lst"))
    bre, bim = load(fsB, sB_d["bre"], "sBbre"), load(fsB, sB_d["bim"], "sBbim")
    mask8 = load([8], L["mask8_d"], "mask8")
    ar, ai, _, _ = cpow(lre, lim, lst, fsB, 1.0)
    kr, ki = kappa(ar, ai, lre, lim, lst, fsB)
    BbR, BbI, t1, t2 = (A.tile(fsB, F32) for _ in range(4))
    cmul(BbR.full(), BbI.full(), kr.full(), ki.full(), bre.full(), bim.full(), t1.full(), t2.full())
    mb = mask8.full().w(mask8.ap.unsqueeze(1).unsqueeze(3).broadcast_to([128, 8, 8, 64]))
    for n, src in (("re", BbR), ("im", BbI)):
        sv = src.full().w(src.ap.unsqueeze(2).broadcast_to([128, 8, 8, 64]))
        k.tt("dve", Bblk[n].full(), sv, mb, MUL)
    A.release()

    ssm_stop = int(os.environ.get("MK_SSM_STOP", "9"))
    if ssm_stop <= 1:
        return
    A.mark()
    fsS = [32]
    lre, lim, lst = (load(fsS, sS_d[n], "sS" + n) for n in ("lre", "lim", "lst"))
    ar, ai, _, _ = cpow(lre, lim, lst, fsS, 1.0)
    ir, ii, _, _ = cpow(lre, lim, lst, fsS, -1.0)
    kr, ki = kappa(ar, ai, lre, lim, lst, fsS)
    sub_stop = int(os.environ.get("MK_SSM_SUB", "99"))
    if sub_stop <= 1:
        A.release()
        return
    A.mark()
    bsr = load([32, 32], bsel_d["re"], "bselre")
    bsi = load([32, 32], bsel_d["im"], "bselim")
    t1 = A.tile([32, 32], F32)
    t2 = A.tile([32, 32], F32)
    krb = kr.full().w(kr.ap.unsqueeze(2).broadcast_to([128, 32, 32]))
    kib = ki.full().w(ki.ap.unsqueeze(2).broadcast_to([128, 32, 32]))
    cmul(Bsel["re"].full(), Bsel["im"].full(), krb, kib, bsr.full(), bsi.full(), t1.full(), t2.full())
    A.release()
    if sub_stop <= 2:
        A.release()
        return
    A.mark()
    ctmp = A.tile([32, 128], F32)
    k.dma("sp", ctmp.full(), dr(cblk_d["re"], "cblkre"))
    k.copy("act", Cb["re"].full(), ctmp.full())
    ctmp2 = A.tile([32, 128], F32)
    k.dma("sp", ctmp2.full(), dr(cblk_d["im"], "cblkim"))
    k.act(Cb["imn"].full(), ctmp2.full(), AF.Identity, scale=-1.0)
    A.release()
    trf = A.tile([128], F32)
    k.dma("sp", trf.full(), dr(L["tri_d"], "tri"))
    k.copy("dve", trib.full(), trf.full())
    if sub_stop <= 3:
        A.release()
        return

    def table(dst_r, dst_i, br_, bi_, want_p128=None):
        A.mark()
        pr = A.tile([32], F32)
        pi_ = A.tile([32], F32)
        q1 = A.tile([32], F32)
        q2 = A.tile([32], F32)
        x1 = Tile(A.ap[:, TinvT["re"].lo // 4:(TinvT["re"].lo + 8192) // 4].rearrange("p (a b) -> p a b", b=64),
                  "sb", TinvT["re"].lo, [32, 64], 4)
        x2 = Tile(A.ap[:, (TinvT["re"].lo + 8192) // 4:(TinvT["re"].lo + 16384) // 4].rearrange("p (a b) -> p a b", b=64),
                  "sb", TinvT["re"].lo + 8192, [32, 64], 4)
        k.memset("dve", dst_r[:, 0:1], 1.0)
        k.memset("dve", dst_i[:, 0:1], 0.0)
        k.copy("dve", dst_r[:, 1:2], br_.full().w(br_.ap.unsqueeze(2)))
        k.copy("dve", dst_i[:, 1:2], bi_.full().w(bi_.ap.unsqueeze(2)))
        k.copy("dve", pr.full(), br_.full())
        k.copy("dve", pi_.full(), bi_.full())
        for j in range(1, int(os.environ.get("MK_TAB_J", "8"))):
            k.tt("dve", q1.full(), pr.full(), pr.full(), MUL)
            k.tt("dve", q2.full(), pi_.full(), pi_.full(), MUL)
            k.stt("dve", pi_.full(), pr.full(), 2.0, pi_.full(), MUL, MUL)
            k.tt("dve", pr.full(), q1.full(), q2.full(), SUB)
            if j == 7:
                break
            n = 1 << j
            prb = pr.full().w(pr.ap.unsqueeze(2).broadcast_to([128, 32, n]))
            pib = pi_.full().w(pi_.ap.unsqueeze(2).broadcast_to([128, 32, n]))
            cmul(dst_r[:, n:2 * n], dst_i[:, n:2 * n], dst_r[:, 0:n], dst_i[:, 0:n], prb, pib,
                 x1[:, 0:n], x2[:, 0:n])
        if want_p128 is not None:
            k.copy("dve", want_p128[0].full(), pr.full())
            k.copy("dve", want_p128[1].full(), pi_.full())
        A.release()

    table(Tpos["re"], Tpos["im"], ar, ai, want_p128=(a128["re"], a128["im"]))
    if sub_stop <= 4:
        A.release()
        return
    TiS = {n: A.tile([32, 128], F32) for n in ("re", "im")}
    table(TiS["re"], TiS["im"], ir, ii)
    if sub_stop <= 5:
        A.release()
        return
    cnt = 0
    for n in ("re", "im"):
        for g4 in range(8):
            bank = cnt % 2
            cnt += 1
            for j in range(4):
                gp = g4 * 4 + j
                k.tr(ps[bank][j * 128:(j + 1) * 128], TiS[n][gp], L["ident"].full())
            k.copy("dve", TinvT[n][g4 * 512:(g4 + 1) * 512], ps[bank][0:512])
            k.copy("act", TinvTb[n][g4 * 512:(g4 + 1) * 512], TinvT[n][g4 * 512:(g4 + 1) * 512])
    A.release()

    def carry_update():
        k.tt("dve", sr_.full(), Gend["re"].full(), cr.full(), ADD)
        k.tt("dve", si_.full(), Gend["im"].full(), ci.full(), ADD)
        cmul(cr.full(), ci.full(), a128["re"].full(), a128["im"].full(), sr_.full(), si_.full(), tA.full(), tB.full())

    k.memset("dve", cr.full(), 0.0)
    k.memset("dve", ci.full(), 0.0)
    if ssm_stop <= 2:
        A.release()
        return

    A.mark()
    upb = [A.tile([1024], BF16) for _ in range(2)]
    w1 = [A.tile([512], F32) for _ in range(2)]
    w2 = [A.tile([512], F32) for _ in range(2)]
    n_pre = int(os.environ.get("MK_NPRE", "32"))
    for n in range(32 - n_pre, 32):
        up = upb[n % 2]
        k.dma("sp", up.full(), dr(Upre_d[n * 128:(n + 1) * 128, :], "Upre", n * 128, (n + 1) * 128))
        pso = (n % 2) * 4
        for gp in range(32):
            hf, c0 = divmod(gp, 16)
            k.mm(ps[pso + hf][c0 * 32:(c0 + 1) * 32], TinvTb["re"][gp * 128:(gp + 1) * 128], up[gp * 32:(gp + 1) * 32], True, True)
            k.mm(ps[pso + 2 + hf][c0 * 32:(c0 + 1) * 32], TinvTb["im"][gp * 128:(gp + 1) * 128], up[gp * 32:(gp + 1) * 32], True, True)
        for hf in range(2):
            Mre, Mim = ps[pso + hf][0:512], ps[pso + 2 + hf][0:512]
            bsr_ = Bsel["re"][hf * 16:(hf + 1) * 16]
            bsr_ = bsr_.w(bsr_.ap.rearrange("p a b -> p (a b)"))
            bsi_ = Bsel["im"][hf * 16:(hf + 1) * 16]
            bsi_ = bsi_.w(bsi_.ap.rearrange("p a b -> p (a b)"))
            x1, x2 = w1[hf].full(), w2[hf].full()
            k.tt("dve", x1, Mre, bsr_, MUL)
            k.tt("dve", x2, Mim, bsi_, MUL)
            k.tt("pool", x1, x1, x2, SUB)
            k.reduce_sum(Gend["re"][hf * 16:(hf + 1) * 16], x1.w(x1.ap.rearrange("p (a b) -> p a b", b=32)))
            k.tt("dve", x1, Mim, bsr_, MUL)
            k.tt("dve", x2, Mre, bsi_, MUL)
            k.tt("pool", x1, x1, x2, ADD)
            k.reduce_sum(Gend["im"][hf * 16:(hf + 1) * 16], x1.w(x1.ap.rearrange("p (a b) -> p a b", b=32)))
        carry_update()
    A.release()
    A.release()
    k.ts("dve", cr.full(), cr.full(), pm.full(), op0=MUL)
    k.ts("dve", ci.full(), ci.full(), pm.full(), op0=MUL)

    if ssm_stop <= 3:
        return
    A.mark()
    uTf = [A.tile([8, 128], F32) for _ in range(2)]
    ub = [A.tile([8, 128], BF16) for _ in range(2)]
    bpp = {n: [A.tile([512], BF16) for _ in range(2)] for n in ("re", "im")}
    f1 = [A.tile([512], F32) for _ in range(2)]
    f2 = [A.tile([512], F32) for _ in range(2)]
    hT = {n: [A.tile([4, 128], BF16) for _ in range(2)] for n in ("re", "im")}
    ytmp = [A.tile([128], F32) for _ in range(2)]
    ygt = [A.tile([8, 128], F32) for _ in range(2)]
    UT_v = UT_d.rearrange("k p t -> p k t")
    YG_v = YG_d.rearrange("k p t -> p k t")
    n_own = int(os.environ.get("MK_NOWN", "32"))
    cr2 = [cr, A.tile([32], F32)]
    ci2 = [ci, A.tile([32], F32)]

    def carry_update2(n):
        k.tt("dve", sr_.full(), Gend["re"].full(), cr2[n % 2].full(), ADD)
        k.tt("dve", si_.full(), Gend["im"].full(), ci2[n % 2].full(), ADD)
        cmul(cr2[(n + 1) % 2].full(), ci2[(n + 1) % 2].full(), a128["re"].full(), a128["im"].full(),
             sr_.full(), si_.full(), tA.full(), tB.full())

    def S1(it):
        n, kt = divmod(it, 8)
        if kt == 0:
            k.dma("sp", uTf[n % 2].full(), dr(UT_v[:, :, n * 128:(n + 1) * 128], "UT", n * 128, (n + 1) * 128))
            k.copy("act", ub[n % 2].full(), uTf[n % 2].full())
        u_b = ub[n % 2]
        k.mm(ps[0][0:512], u_b[kt], Bblk["re"][kt].w(Bblk["re"].ap[:, kt].rearrange("p a b -> p (a b)")), True, True)
        k.mm(ps[1][0:512], u_b[kt], Bblk["im"][kt].w(Bblk["im"].ap[:, kt].rearrange("p a b -> p (a b)")), True, True)

    def S2(it):
        n, kt = divmod(it, 8)
        i2 = it % 2
        fsl = slice(kt * 512, (kt + 1) * 512)
        x1, x2 = f1[0].full(), f2[0].full()
        x3, x4 = f1[1].full(), f2[1].full()
        k.tt("dve", x1, ps[0][0:512], TinvT["re"][fsl], MUL)
        k.tt("dve", x2, ps[1][0:512], TinvT["im"][fsl], MUL)
        k.tt("dve", x3, ps[0][0:512], TinvT["im"][fsl], MUL)
        k.tt("dve", x4, ps[1][0:512], TinvT["re"][fsl], MUL)
        k.tt("pool", bpp["re"][i2].full(), x1, x2, SUB)
        k.tt("pool", bpp["im"][i2].full(), x3, x4, ADD)

    def S3(it):
        n, kt = divmod(it, 8)
        i2 = it % 2
        pgr, pgi = ps[2 + 2 * i2], ps[3 + 2 * i2]
        br_, bi_ = bpp["re"][i2], bpp["im"][i2]
        for gq in range(4):
            k.mm(pgr[gq * 128:(gq + 1) * 128], br_[gq * 128:(gq + 1) * 128], trib.full(), True, True)
        for gq in range(4):
            k.mm(pgi[gq * 128:(gq + 1) * 128], bi_[gq * 128:(gq + 1) * 128], trib.full(), True, True)
        lastr = pgr.full().w(pgr.ap.rearrange("p (a b) -> p a b", b=128)[:, :, 127])
        lasti = pgi.full().w(pgi.ap.rearrange("p (a b) -> p a b", b=128)[:, :, 127])
        k.copy("act", Gend["re"][kt * 4:(kt + 1) * 4], lastr)
        k.copy("act", Gend["im"][kt * 4:(kt + 1) * 4], lasti)
        if kt == 7:
            carry_update2(n)

    def S4(it):
        n, kt = divmod(it, 8)
        i2 = it % 2
        pgr, pgi = ps[2 + 2 * i2], ps[3 + 2 * i2]
        hr, hi_ = hT["re"][i2], hT["im"][i2]
        crn, cin_ = cr2[n % 2], ci2[n % 2]
        gcr, gci = Gc["re"][i2], Gc["im"][i2]
        for gq in range(4):
            gp = kt * 4 + gq
            k.act(gcr[gq], pgr[gq * 128:(gq + 1) * 128], AF.Identity, bias=crn[gp:gp + 1])
            k.act(gci[gq], pgi[gq * 128:(gq + 1) * 128], AF.Identity, bias=cin_[gp:gp + 1])
        tr_ = Tpos["re"][kt * 4:(kt + 1) * 4]
        ti_ = Tpos["im"][kt * 4:(kt + 1) * 4]
        q1, q2, q3, q4 = (t_.full() for t_ in mtmp[i2])
        k.tt("dve", q1, gcr.full(), tr_, MUL)
        k.tt("dve", q2, gci.full(), ti_, MUL)
        k.tt("dve", q3, gcr.full(), ti_, MUL)
        k.tt("dve", q4, gci.full(), tr_, MUL)
        k.tt("pool", hr.full(), q1, q2, SUB)
        k.tt("pool", hi_.full(), q3, q4, ADD)

    def S5(it):
        n, kt = divmod(it, 8)
        i2 = it % 2
        hr, hi_ = hT["re"][i2], hT["im"][i2]
        py = ps[6 + i2]
        for gq in range(4):
            gp = kt * 4 + gq
            k.mm(py[0:128], Cb["re"][gp], hr[gq], gq == 0, False)
            k.mm(py[0:128], Cb["imn"][gp], hi_[gq], False, gq == 3)

    def S6(it):
        n, kt = divmod(it, 8)
        i2 = it % 2
        py = ps[6 + i2]
        yt = ytmp[i2].full()
        yg = ygt[n % 2]
        k.stt("dve", yt, uTf[n % 2][kt], dsk[kt:kt + 1], py[0:128], MUL, ADD)
        k.act(yg[kt], yt, AF.Gelu_apprx_tanh)
        if kt == 7:
            k.dma("pool", dr(YG_v[:, :, n * 128:(n + 1) * 128], "YG", n * 128, (n + 1) * 128), yg.full())

    Gc = {n_: [A.tile([4, 128], F32) for _ in range(2)] for n_ in ("re", "im")}
    mtmp = [[A.tile([4, 128], F32) for _ in range(4)] for _ in range(2)]
    stages = [S1, S2, S3, S4, S5, S6]
    n_it = n_own * 8
    for slot in range(n_it + len(stages) - 1):
        for si in range(len(stages) - 1, -1, -1):
            it_ = slot - si
            if 0 <= it_ < n_it:
                stages[si](it_)
    A.release()


def build_phase3(k, A, ps, L):
    ident, identb, ones_f = L["ident"], L["identb"], L["ones_f"]
    m2, sh2, bglu = L["m2"], L["sh2"], L["bglu"]
    og, g1bc_t, g2bc_t = L["og"], L["g1bc_t"], L["g2bc_t"]
    NUM_d, YG_d, xo_d, out_d = L["NUM_d"], L["YG_d"], L["xo_d"], L["out_d"]
    MUL, ADD = ALU.mult, ALU.add
    wglu_v = L["wglub_d"].rearrange("(kt p) c -> p kt c", p=128)
    wout_v = L["woutb_d"].rearrange("(kt p) c -> p kt c", p=128)
    wff1_v = L["wff1b_d"].rearrange("(kt p) c -> p kt c", p=128)
    wff2_v = L["wff2b_d"].rearrange("(kt p) c -> p kt c", p=128)
    YG_v = YG_d.rearrange("k p t -> p k t")
    A.mark()
    wblk = [A.tile([16, 512], BF16) for _ in range(3)]
    xmid = A.tile([4, 2048], F32)
    actT = A.tile([16, 512], BF16)
    stat = A.tile([16], F32)
    gtmp = [A.tile([512], F32) for _ in range(2)]
    cw = {"w": 0, "p": 0}

    def next_w():
        w = wblk[cw["w"] % 3]
        cw["w"] += 1
        return w

    n_t3 = int(os.environ.get("MK_P3_TILES", "8"))
    for ti in range(n_t3):
        tok0 = ti * TT
        for sub in range(4):
            k.dma("sp", xmid[sub], dr(xo_d[tok0 + sub * 128: tok0 + (sub + 1) * 128, :], "x"))
        k.memset("dve", stat.full(), 0.0)
        A.mark()
        nt_ = [A.tile([8, 129], F32) for _ in range(3)]
        nt = [nt_, nt_]
        attn = A.tile([8, 128], F32)
        attb = A.tile([1024], BF16)
        junk = A.tile([1024], BF16)
        rec = A.tile([8], F32)
        for sub in range(4):
            t0 = tok0 + sub * 128
            n0, n1, n2 = nt[sub % 2]
            for pi, t_ in enumerate((n0, n1, n2)):
                k.dma("sp", t_.full(), dr(NUM_d[pi][t0:t0 + 128], f"NUM{pi}", t0, t0 + 128))
            k.tt("dve", n0.full(), n0.full(), n1.full(), ADD)
            k.tt("dve", n0.full(), n0.full(), n2.full(), ADD)
            k.recip(rec.full(), n0[:, 128:129].w(n0.ap[:, :, 128]))
            k.tt("dve", attn.full(), n0[:, 0:128], rec.full().w(rec.ap.unsqueeze(2).broadcast_to([128, 8, 128])), MUL)
            ss = stat[sub:sub + 1]
            rs = stat[4 + sub:5 + sub]
            af = attn.full().w(attn.ap.rearrange("p a b -> p (a b)"))
            k.act(junk.full(), af, AF.Square, accum_out=ss)
            k.act(rs, ss, AF.Sqrt, scale=1.0 / 1024, bias=EPS)
            k.recip(rs, rs)
            k.ts("dve", attb.full(), af, rs, op0=MUL)
            pT = ps[sub % 2]
            pTb = pT.full().w(pT.ap.bitcast(BF16))
            for h in range(8):
                k.tr(pTb.w(pTb.ap[:, h * 128:(h + 1) * 128]), attb[h * 128:(h + 1) * 128], identb.full())
            for h in range(8):
                k.act(actT[h, sub * 128:(sub + 1) * 128], pTb.w(pTb.ap[:, h * 128:(h + 1) * 128]), AF.Identity,
                      scale=og[h:h + 1])
        yg = A.tile([8, 512], F32)
        ygb = A.tile([8, 512], BF16)
        ssm = yg
        gate = [A.tile([512], F32) for _ in range(2)]
        sq = [A.tile([512], F32) for _ in range(2)]
        rbc = A.tile([512], F32)
        k.dma("sp", yg.full(), dr(YG_v[:, :, tok0:tok0 + TT], "YG", tok0, tok0 + TT))
        k.copy("act", ygb.full(), yg.full())
        wg = next_w()
        wgv = wg.full().w(wg.ap.rearrange("p a b -> p (a b)").rearrange("p (a b) -> p a b", b=1024))
        k.dma("sp", wgv, dr(wglu_v, "wglub"))
        for co in range(8):
            pb = ps[2 + co % 2]
            for kt in range(8):
                k.mm(pb[0:512], wgv.w(wgv.ap[:, kt, co * 128:(co + 1) * 128]), ygb[kt], kt == 0, kt == 7)
            g_ = gate[co % 2].full()
            s_ = sq[co % 2].full()
            k.act(g_, pb[0:512], AF.Sigmoid, bias=bglu[co:co + 1])
            k.tt("dve", ssm[co], yg[co], g_, MUL)
            k.act(s_, ssm[co], AF.Square)
            k.mm(ps[4][0:512], ones_f.full(), s_, co == 0, co == 7)
        k.act(rbc.full(), ps[4][0:512], AF.Sqrt, scale=1.0 / 1024, bias=EPS)
        k.recip(rbc.full(), rbc.full())
        for co in range(8):
            k.stt("dve", actT[8 + co], ssm[co], og[8 + co:9 + co], rbc.full(), MUL, MUL)
        A.release()
        for cc in range(4):
            wb = next_w()
            k.dma("sp", wb.full(), dr(wout_v[:, :, cc * 512:(cc + 1) * 512], "woutb"))
            for sub in range(4):
                pb = ps[5 + cw["p"] % 2]
                cw["p"] += 1
                for kt in range(16):
                    k.mm(pb[0:512], actT[kt, sub * 128:(sub + 1) * 128], wb[kt], kt == 0, kt == 15)
                xs = xmid[sub, cc * 512:(cc + 1) * 512]
                gt_ = gtmp[cw["p"] % 2].full()
                k.tt("dve", gt_, pb[0:512], g1bc_t[cc * 512:(cc + 1) * 512], MUL)
                k.tt("pool", xs, xs, gt_, ADD)
        A.mark()
        hid = A.tile([64, 512], BF16)
        xn_t = Tile(A.ap[:, hid.lo // 4:(hid.lo + 8192) // 4], "sb", hid.lo, [2048], 4)
        xn = [xn_t, xn_t]
        junk2 = Tile(A.ap[:, (hid.lo + 8192) // 4:(hid.lo + 12288) // 4].bitcast(BF16), "sb", hid.lo + 8192, [2048], 2)
        rl = [A.tile([512], F32) for _ in range(2)]
        ot = gtmp
        for sub in range(4):
            ss = stat[8 + sub:9 + sub]
            rs = stat[12 + sub:13 + sub]
            k.act(junk2.full(), xmid[sub], AF.Square, accum_out=ss)
            k.act(rs, ss, AF.Sqrt, scale=1.0 / D_MODEL, bias=EPS)
            k.recip(rs, rs)
            x_ = xn[sub % 2]
            k.ts("dve", x_.full(), xmid[sub], rs, op0=MUL)
            for kq in range(4):
                bank = kq % 2
                for j in range(4):
                    kt = kq * 4 + j
                    k.tr(ps[bank][j * 128:(j + 1) * 128], x_[kt * 128:(kt + 1) * 128], ident.full())
                for j in range(4):
                    kt = kq * 4 + j
                    k.act(actT[kt, sub * 128:(sub + 1) * 128], ps[bank][j * 128:(j + 1) * 128], AF.Identity,
                          scale=m2[kt:kt + 1], bias=sh2[kt:kt + 1])
        for blk in range(16):
            wb = next_w()
            k.dma("sp", wb.full(), dr(wff1_v[:, :, blk * 512:(blk + 1) * 512], "wff1b"))
            for co in range(4):
                pb = ps[2 + cw["p"] % 3]
                cw["p"] += 1
                for kt in range(16):
                    k.mm(pb[0:512], wb[kt, co * 128:(co + 1) * 128], actT[kt], kt == 0, kt == 15)
                r_ = rl[(blk * 4 + co) % 2].full()
                k.act(r_, pb[0:512], AF.Relu)
                k.tt("pool", hid[blk * 4 + co], r_, r_, MUL)
        for cc in range(4):
            bset = (cc % 2) * 4
            for q in range(4):
                wb = next_w()
                k.dma("sp", wb.full(), dr(wff2_v[:, q * 16:(q + 1) * 16, cc * 512:(cc + 1) * 512], "wff2b"))
                for sub in range(4):
                    for kt in range(16):
                        k.mm(ps[bset + sub][0:512], hid[q * 16 + kt, sub * 128:(sub + 1) * 128], wb[kt],
                             q == 0 and kt == 0, q == 3 and kt == 15)
            for sub in range(4):
                o_ = ot[sub % 2].full()
                k.tt("dve", o_, ps[bset + sub][0:512], g2bc_t[cc * 512:(cc + 1) * 512], MUL)
                k.tt("pool", o_, o_, xmid[sub, cc * 512:(cc + 1) * 512], ADD)
                r0 = tok0 + sub * 128
                k.dma("pool", dr(out_d[r0:r0 + 128, cc * 512:(cc + 1) * 512], "out", r0 * 4 + cc, r0 * 4 + cc + 1), o_)
        A.release()
    A.release()
```

```python
import bisect
import contextlib
import math
import os
import numpy as np
import concourse.bass as bass
import concourse.mybir as mybir
from concourse.bass_utils import run_bass_kernel_spmd

F32 = mybir.dt.float32
BF16 = mybir.dt.bfloat16
AF = mybir.ActivationFunctionType
ALU = mybir.AluOpType
AX = mybir.AxisListType
EPOCH = 30000
N_DMA_SEMS = 48
ESZ = {F32: 4, BF16: 2}

D_MODEL = 2048
SEQ = 8192
NTOK = 4096
NPRE = 4096
KVPRE = 2048
TT = 512
EPS = 1e-6


class V:
    __slots__ = ("ap", "space", "lo", "hi")

    def __init__(self, ap, space, lo, hi):
        self.ap, self.space, self.lo, self.hi = ap, space, lo, hi

    def w(self, ap):
        return V(ap, self.space, self.lo, self.hi)


class Tile:
    def __init__(self, ap, space, lo, free_shape, esz):
        self.ap, self.space, self.lo, self.free_shape, self.esz = ap, space, lo, tuple(free_shape), esz
        n = 1
        for s in free_shape:
            n *= s
        self.hi = lo + n * esz

    def full(self):
        return V(self.ap, self.space, self.lo, self.hi)

    def __getitem__(self, idx):
        if not isinstance(idx, tuple):
            idx = (idx,)
        return self.v(idx)

    def v(self, idx, p=None):
        idx = tuple(idx) + (slice(None),) * (len(self.free_shape) - len(idx))
        strides = []
        st = 1
        for s in reversed(self.free_shape):
            strides.append(st)
            st *= s
        strides = strides[::-1]
        mn = mx = 0
        for i, s, stv in zip(idx, self.free_shape, strides):
            if isinstance(i, int):
                mn += i * stv
                mx += i * stv
            else:
                a, b, c = i.indices(s)
                assert b > a
                last = a + ((b - 1 - a) // c) * c
                mn += a * stv
                mx += last * stv
        ps = slice(None) if p is None else p
        if self.space == "ps":
            return V(self.ap[(ps,) + idx], self.space, self.lo, self.hi)
        return V(self.ap[(ps,) + idx], self.space, self.lo + mn * self.esz, self.lo + (mx + 1) * self.esz)


def dr(ap, name, lo=0, hi=1 << 40):
    return V(ap, ("dram", name), lo, hi)


class Arena:
    def __init__(self, nc, nbytes, name="arena"):
        self.nbytes = nbytes
        self.t = nc.alloc_sbuf_tensor(name, [128, nbytes // 4], F32)
        self.ap = self.t.ap()
        self.off = 0
        self.marks = []
        self.peak = 0

    def tile(self, free_shape, dtype):
        esz = ESZ[dtype]
        n = 1
        for s in free_shape:
            n *= s
        nb = (n * esz + 63) // 64 * 64
        lo = self.off
        assert lo + nb <= self.nbytes, f"SBUF arena overflow {lo}+{nb}>{self.nbytes}"
        self.off += nb
        self.peak = max(self.peak, self.off)
        ap = self.ap[:, lo // 4:(lo + nb) // 4]
        if dtype != F32:
            ap = ap.bitcast(dtype)
        ap = ap[:, 0:n]
        if len(free_shape) > 1:
            names = " ".join(f"a{i}" for i in range(len(free_shape)))
            kw = {f"a{i}": s for i, s in enumerate(free_shape)}
            ap = ap.rearrange(f"p ({names}) -> p {names}", **kw)
        return Tile(ap, "sb", lo, free_shape, esz)

    def mark(self):
        self.marks.append(self.off)

    def release(self):
        self.off = self.marks.pop()


class Sched:
    ENGS = ["pe", "act", "dve", "pool", "sp"]

    def __init__(self, nc):
        self.nc = nc
        self.ops = {e: [] for e in self.ENGS}
        self.count = {e: 0 for e in self.ENGS}
        self.waited = {e: {} for e in self.ENGS}
        self.iv = {}
        self.dma_uses = [0] * N_DMA_SEMS
        self.dma_pool = {"sp": list(range(0, 28)), "pool": list(range(28, 44)), "act": list(range(44, 48))}
        self.dma_rr = {q: 0 for q in self.dma_pool}

    def _split(self, space, pos):
        starts, recs = self.iv.setdefault(space, ([], []))
        i = bisect.bisect_right(starts, pos) - 1
        if i >= 0:
            r = recs[i]
            if r["lo"] < pos < r["hi"]:
                r2 = {"lo": pos, "hi": r["hi"], "w": r["w"], "r": list(r["r"])}
                r["hi"] = pos
                starts.insert(i + 1, pos)
                recs.insert(i + 1, r2)

    def _records(self, space, lo, hi):
        starts, recs = self.iv.setdefault(space, ([], []))
        self._split(space, lo)
        self._split(space, hi)
        i = bisect.bisect_left(starts, lo)
        out = []
        cur = lo
        while cur < hi:
            if i < len(starts) and starts[i] == cur:
                out.append(recs[i])
                cur = recs[i]["hi"]
                i += 1
            else:
                nxt = min(starts[i] if i < len(starts) else hi, hi)
                r = {"lo": cur, "hi": nxt, "w": None, "r": []}
                starts.insert(i, cur)
                recs.insert(i, r)
                out.append(r)
                i += 1
                cur = nxt
        return out

    def _add(self, eng, fn, reads, writes, is_dma):
        if is_dma:
            pool_ = self.dma_pool[eng]
            si = pool_[self.dma_rr[eng] % len(pool_)]
            self.dma_rr[eng] += 1
            prev = self.dma_uses[si]
            self.dma_uses[si] += 1
            me = ("dma", si, 16 * (prev + 1))
        else:
            self.count[eng] += 1
            me = ("eng", eng, self.count[eng])
        raw = set()
        deps = set()
        for v in reads:
            for r in self._records(v.space, v.lo, v.hi):
                if r["w"] is not None:
                    raw.add(r["w"])
                    deps.add(r["w"])
                if v.space == "ps":
                    deps.update(d for d in r["r"] if d[1] != eng)
                r["r"].append(me)
                if len(r["r"]) > 16:
                    latest = {}
                    for d in r["r"]:
                        k = (d[0], d[1])
                        if k not in latest or d[2] > latest[k][2]:
                            latest[k] = d
                    r["r"] = list(latest.values())
        for v in writes:
            for r in self._records(v.space, v.lo, v.hi):
                if r["w"] is not None:
                    deps.add(r["w"])
                deps.update(r["r"])
            starts, rl = self.iv[v.space]
            i0 = bisect.bisect_left(starts, v.lo)
            i1 = bisect.bisect_left(starts, v.hi)
            del starts[i0:i1]
            del rl[i0:i1]
            starts.insert(i0, v.lo)
            rl.insert(i0, {"lo": v.lo, "hi": v.hi, "w": me, "r": []})
        deps.discard(me)
        if is_dma and prev > 0:
            deps.add(("dma", si, 16 * prev))
        need = {}
        for d in deps:
            if d[0] == "eng":
                if d[1] == eng and eng == "pe":
                    continue
                key = ("eng", d[1])
            else:
                key = ("dma", d[1])
            val = d[2]
            if self.waited[eng].get(key, 0) >= val:
                continue
            if need.get(key, 0) < val:
                need[key] = val
        for k, v in need.items():
            self.waited[eng][k] = v
        self.ops[eng].append((fn, list(need.items()), me))
        return me

    def op(self, eng, fn, reads=(), writes=()):
        return self._add(eng, fn, list(reads), list(writes), False)

    def dma(self, queue, out, in_, **kw):
        fn = lambda e: e.dma_start(out=out.ap, in_=in_.ap, **kw)
        return self._add(queue, fn, [in_], [out], True)

    def barrier(self):
        for e in self.ENGS:
            need = {}
            for e2 in self.ENGS:
                if e2 != e and self.count[e2] > 0:
                    need[("eng", e2)] = self.count[e2]
            for si, u in enumerate(self.dma_uses):
                if u > 0:
                    need[("dma", si)] = 16 * u
            waits = []
            for k, v in need.items():
                if self.waited[e].get(k, 0) < v:
                    self.waited[e][k] = v
                    waits.append((k, v))
            if waits:
                self.ops[e].append((None, waits, None))

    def emit(self):
        nc = self.nc
        with contextlib.ExitStack() as st:
            esem = {}
            for e in self.ENGS:
                nep = max((self.count[e] + EPOCH - 1) // EPOCH, 1)
                esem[e] = [st.enter_context(nc.semaphore(f"s_{e}_{i}")) for i in range(nep)]
            dsem = [st.enter_context(nc.semaphore(f"s_dma_{i}")) for i in range(N_DMA_SEMS)]
            block = st.enter_context(nc.Block())

            def replay(ename, eobj):
                for fn, waits, me in self.ops[ename]:
                    for key, val in waits:
                        if key[0] == "eng":
                            eobj.wait_ge(esem[key[1]][(val - 1) // EPOCH], (val - 1) % EPOCH + 1)
                        else:
                            eobj.wait_ge(dsem[key[1]], val)
                    if fn is None:
                        continue
                    inst = fn(eobj)
                    if me[0] == "eng":
                        n = me[2]
                        inst.then_inc(esem[ename][(n - 1) // EPOCH], 1)
                    else:
                        inst.then_inc(dsem[me[1]], 16)

            @block.tensor
            def _(e):
                replay("pe", e)

            @block.scalar
            def _(e):
                replay("act", e)

            @block.vector
            def _(e):
                replay("dve", e)

            @block.gpsimd
            def _(e):
                replay("pool", e)

            @block.sync
            def _(e):
                replay("sp", e)


class K:
    def __init__(self, nc):
        self.nc = nc
        self.s = Sched(nc)
        self.rr = 0

    def _apv(self, x):
        return x.ap if isinstance(x, V) else x

    def act(self, out, in_, func, bias=None, scale=None, accum_out=None):
        kw = {}
        rd = [in_]
        wr = [out]
        if bias is not None:
            kw["bias"] = self._apv(bias)
            if isinstance(bias, V):
                rd.append(bias)
        if scale is not None:
            kw["scale"] = self._apv(scale)
            if isinstance(scale, V):
                rd.append(scale)
        if accum_out is not None:
            kw["accum_out"] = accum_out.ap
            wr.append(accum_out)
        self.s.op("act", lambda e: e.activation(out=out.ap, in_=in_.ap, func=func, **kw), rd, wr)

    def tt(self, eng, out, a, b, op):
        self.s.op(eng, lambda e: e.tensor_tensor(out=out.ap, in0=a.ap, in1=b.ap, op=op), [a, b], [out])

    def ts(self, eng, out, a, s1, s2=None, op0=ALU.mult, op1=None, accum_out=None):
        rd = [a] + [x for x in (s1, s2) if isinstance(x, V)]
        kw = {}
        if op1 is not None:
            kw["op1"] = op1
        wr = [out]
        if accum_out is not None:
            kw["accum_out"] = accum_out.ap
            wr.append(accum_out)
        self.s.op(eng, lambda e: e.tensor_scalar(out=out.ap, in0=a.ap, scalar1=self._apv(s1), scalar2=self._apv(s2),
                                                 op0=op0, **kw), rd, wr)

    def stt(self, eng, out, a, scalar, b, op0, op1):
        rd = [a, b] + ([scalar] if isinstance(scalar, V) else [])
        self.s.op(eng, lambda e: e.scalar_tensor_tensor(out=out.ap, in0=a.ap, scalar=self._apv(scalar), in1=b.ap,
                                                        op0=op0, op1=op1), rd, [out])

    def copy(self, eng, out, in_):
        if eng == "act":
            self.s.op("act", lambda e: e.copy(out=out.ap, in_=in_.ap), [in_], [out])
        else:
            self.s.op(eng, lambda e: e.tensor_copy(out=out.ap, in_=in_.ap), [in_], [out])

    def memset(self, eng, out, val):
        self.s.op(eng, lambda e: e.memset(out.ap, val), [], [out])

    def recip(self, out, in_):
        self.s.op("dve", lambda e: e.reciprocal(out=out.ap, in_=in_.ap), [in_], [out])

    def reduce_sum(self, out, in_):
        self.s.op("dve", lambda e: e.tensor_reduce(out=out.ap, in_=in_.ap, axis=AX.X, op=ALU.add), [in_], [out])

    def mm(self, out, lhsT, rhs, start, stop):
        self.s.op("pe", lambda e: e.matmul(out.ap, lhsT=lhsT.ap, rhs=rhs.ap, start=start, stop=stop),
                  [lhsT, rhs], [out])

    def tr(self, out, in_, ident):
        self.s.op("pe", lambda e: e.transpose(out=out.ap, in_=in_.ap, identity=ident.ap), [in_, ident], [out])

    def dma(self, q, out, in_):
        self.s.dma(q, out, in_)

    def cast_rr(self, out, in_):
        eng = ("act", "dve")[self.rr % 2]
        self.rr += 1
        self.copy(eng, out, in_)


def bc(v, shape, axis):
    return v.w(v.ap.unsqueeze(axis).broadcast_to(shape))


DEBUG = os.environ.get("MK_DEBUG", "")


def build(stop_after=99):
    nc = bass.Bass("TRN2", target_bir_lowering=False)
    k = K(nc)
    dbg = set(DEBUG.split(",")) if DEBUG else set()

    def din(name, shape, dt=F32):
        return nc.dram_tensor(name, list(shape), dt, kind="ExternalInput").ap()

    def dscr(name, shape, dt):
        kind = "ExternalOutput" if name in dbg else "Internal"
        return nc.dram_tensor(name, list(shape), dt, kind=kind).ap()

    xo_d = din("xo", [NTOK, D_MODEL])
    xp_d = din("xp", [NPRE, D_MODEL])
    pmask_d = din("pmask", [128, 1])
    ccol_d = din("c_col", [128, 16])
    wada_d = din("w_ada", [2048, 12288])
    bada_d = din("b_ada", [1, 12288])
    g1n_d = din("g1n", [128, 16])
    g2n_d = din("g2n", [128, 16])
    win_d = din("w_in", [2048, 4096])
    wglu_d = din("w_glu", [1024, 1024])
    wout_d = din("w_out", [2048, 2048])
    wff1_d = din("w_ff1", [2048, 8192])
    wff2_d = din("w_ff2", [8192, 2048])
    qg_d = din("qg", [128, 1])
    kg_d = din("kg", [128, 1])
    og_d = din("og", [128, 16])
    bglu_d = din("bglu", [128, 8])
    dsk_d = din("dsk", [128, 8])
    ident_d = din("ident", [128, 128])
    tri_d = din("tri", [128, 128])
    abias_d = din("abias", [128, 24, 256])
    sB_d = {n: din("sB_" + n, [128, 8, 64]) for n in ("lre", "lim", "lst", "bre", "bim")}
    sS_d = {n: din("sS_" + n, [128, 32]) for n in ("lre", "lim", "lst")}
    bsel_d = {n: din("bsel_" + n, [128, 32, 32]) for n in ("re", "im")}
    cblk_d = {n: din("cblk_" + n, [128, 32, 128]) for n in ("re", "im")}
    mask8_d = din("mask8", [128, 8])

    out_d = nc.dram_tensor("out", [NTOK, D_MODEL], F32, kind="ExternalOutput").ap()

    winb_d = dscr("winb", [2048, 4096], BF16)
    wglub_d = dscr("wglub", [1024, 1024], BF16)
    woutb_d = dscr("woutb", [2048, 2048], BF16)
    wff1b_d = dscr("wff1b", [2048, 8192], BF16)
    wff2b_d = dscr("wff2b", [8192, 2048], BF16)
    QT_d = dscr("QT", [8, 128, NTOK], BF16)
    KT_d = dscr("KT", [8, 128, NTOK + KVPRE], BF16)
    Vs_d = dscr("Vs", [NTOK + KVPRE, 1024], BF16)
    UT_d = dscr("UT", [8, 128, NTOK], F32)
    Upre_d = dscr("Upre", [NPRE, 1024], BF16)
    NUM_d = [dscr(f"NUM{p}", [NTOK, 8, 129], F32) for p in range(3)]
    YG_d = dscr("YG", [8, 128, NTOK], F32)
    modbc_dbg = dscr("modbc_dbg", [128, 12288], F32) if "modbc_dbg" in dbg else None

    A = Arena(nc, 190 * 1024)
    ps = [Tile(nc.alloc_psum_tensor(f"ps{i}", [128, 512], F32).ap(), "ps", i * 2048, [512], 4) for i in range(8)]

    def psv(bank, a, b):
        return ps[bank][a:b]

    ident = A.tile([128], F32)
    k.dma("sp", ident.full(), dr(ident_d, "ident"))
    identb = A.tile([128], BF16)
    k.copy("dve", identb.full(), ident.full())
    ones_f = A.tile([128], F32)
    k.memset("dve", ones_f.full(), 1.0)
    pm = A.tile([1], F32)
    k.dma("sp", pm.full(), dr(pmask_d, "pmask"))
    g1n = A.tile([16], F32)
    g2n = A.tile([16], F32)
    og = A.tile([16], F32)
    bglu = A.tile([8], F32)
    dsk = A.tile([8], F32)
    qgs = A.tile([1], F32)
    kgs = A.tile([1], F32)
    for t_, d_, n_ in ((g1n, g1n_d, "g1n"), (g2n, g2n_d, "g2n"), (og, og_d, "og"), (bglu, bglu_d, "bglu"),
                       (dsk, dsk_d, "dsk"), (qgs, qg_d, "qg"), (kgs, kg_d, "kg")):
        k.dma("sp", t_.full(), dr(d_, n_))
    k.ts("dve", qgs.full(), qgs.full(), 128.0 ** -0.5, op0=ALU.mult)
    m1 = A.tile([16], F32)
    sh1 = A.tile([16], F32)
    m2 = A.tile([16], F32)
    sh2 = A.tile([16], F32)

    def cast_dma0(src_d, dst_d, name, R, rows_per):
        for r0 in range(0, R, rows_per):
            k.dma("pool", dr(dst_d[r0:r0 + rows_per, :], name, r0, r0 + rows_per), dr(src_d[r0:r0 + rows_per, :], name + "_src"))

    cast_dma0(win_d, winb_d, "winb", 2048, 256)

    if True:
        cast_dma0(wglu_d, wglub_d, "wglub", 1024, 256)
        cast_dma0(wout_d, woutb_d, "woutb", 2048, 256)
        cast_dma0(wff1_d, wff1b_d, "wff1b", 2048, 128)
        cast_dma0(wff2_d, wff2b_d, "wff2b", 8192, 512)
    g1bc_t = A.tile([2048], F32)
    g2bc_t = A.tile([2048], F32)
    wada_v = wada_d.rearrange("(kt p) c -> p kt c", p=128)
    A.mark()
    ccol = A.tile([16], F32)
    k.dma("sp", ccol.full(), dr(ccol_d, "ccol"))
    scv = A.tile([16], F32)
    k.act(scv.full(), ccol.full(), AF.Silu)
    rep = A.tile([16, 128], F32)
    k.copy("dve", rep.full(), bc(scv.full(), [128, 16, 128], 2))
    shsc2 = A.tile([4096], F32)
    wa1 = A.tile([16, 512], F32)
    A.mark()
    mod1 = A.tile([4096], F32)
    wa0 = A.tile([16, 512], F32)
    k.dma("sp", mod1.full(), dr(bada_d[:, 0:4096].partition_broadcast(128), "bada"))
    was = [wa0, wa1]
    for ch in range(8):
        buf = was[ch % 2]
        k.dma("sp", buf.full(), dr(wada_v[:, :, ch * 512:(ch + 1) * 512], "w_ada"))
        pb = psv(ch % 2, 0, 512)
        for kt in range(16):
            k.mm(pb, rep[kt], buf[kt], kt == 0, kt == 15)
        k.tt("dve", mod1[ch * 512:(ch + 1) * 512], pb, mod1[ch * 512:(ch + 1) * 512], ALU.add)

    def to_cols(srcrow, off, dst, bank0):
        for kq in range(4):
            bank = bank0 + kq % 2
            for j in range(4):
                kt = kq * 4 + j
                k.tr(psv(bank, j * 128, (j + 1) * 128), srcrow[off + kt * 128: off + (kt + 1) * 128], ident.full())
            src = ps[bank].full()
            src = src.w(src.ap.rearrange("p (a b) -> p a b", b=128)[:, :, 0])
            k.copy("dve", dst[kq * 4:(kq + 1) * 4], src)

    sc1 = A.tile([16], F32)
    to_cols(mod1, 0, sh1, 2)
    to_cols(mod1, 2048, sc1, 2)
    k.stt("dve", m1.full(), sc1.full(), 1.0, g1n.full(), ALU.add, ALU.mult)
    A.release()
    k.dma("sp", g1bc_t.full(), dr(bada_d[:, 4096:6144].partition_broadcast(128), "bada"))
    k.dma("sp", shsc2.full(), dr(bada_d[:, 6144:10240].partition_broadcast(128), "bada"))
    k.dma("sp", g2bc_t.full(), dr(bada_d[:, 10240:12288].partition_broadcast(128), "bada"))
    mod_todo = list(range(8, 24))

    def emit_mod_chunk():
        if not mod_todo:
            return
        ch = mod_todo.pop(0)
        k.dma("sp", wa1.full(), dr(wada_v[:, :, ch * 512:(ch + 1) * 512], "w_ada"))
        pb = psv(7, 0, 512)
        for kt in range(16):
            k.mm(pb, rep[kt], wa1[kt], kt == 0, kt == 15)
        if ch < 12:
            dst = g1bc_t[(ch - 8) * 512:(ch - 7) * 512]
        elif ch < 20:
            dst = shsc2[(ch - 12) * 512:(ch - 11) * 512]
        else:
            dst = g2bc_t[(ch - 20) * 512:(ch - 19) * 512]
        k.tt("dve", dst, pb, dst, ALU.add)

    A.mark()
    xin = [A.tile([2048], F32) for _ in range(2)]
    junk = A.tile([2048], BF16)
    hT = [A.tile([16, 512], BF16) for _ in range(2)]
    wblk = [A.tile([16, 512], BF16) for _ in range(3)]
    sqt = [A.tile([512], F32) for _ in range(2)]
    rst = [A.tile([512], F32) for _ in range(2)]
    obt = [A.tile([512], BF16) for _ in range(3)]
    oft = [A.tile([512], F32) for _ in range(2)]
    stat = [A.tile([8], F32) for _ in range(2)]
    winb_v = winb_d.rearrange("(kt p) c -> p kt c", p=128)
    ctr = {"w": 0, "p": 0, "s": 0, "o": 0, "f": 0}
    n_tiles1 = 16 if stop_after >= 2 else int(os.environ.get("MK_P1_TILES", "16"))
    tl1 = os.environ.get("MK_P1_LIST", "")
    tiles1 = [int(t_) for t_ in tl1.split(",")] if tl1 else list(range(n_tiles1))
    pending = []

    def flush_pending():
        while pending:
            pending.pop(0)()

    def emit_norm(ti, sub):
        is_pre = ti < 8
        tok0 = (ti % 8) * TT
        src = xp_d if is_pre else xo_d
        h = hT[ti % 2]
        stt_ = stat[ti % 2]
        if sub == 0:
            k.memset("dve", stt_.full(), 0.0)
        xt = xin[(ti * 4 + sub) % 2]
        k.dma("sp", xt.full(), dr(src[tok0 + sub * 128: tok0 + (sub + 1) * 128, :], "x"))
        ss = stt_[sub:sub + 1]
        rs = stt_[4 + sub:5 + sub]
        k.act(junk.full(), xt.full(), AF.Square, accum_out=ss)
        k.act(rs, ss, AF.Sqrt, scale=1.0 / D_MODEL, bias=EPS)
        k.recip(rs, rs)
        k.ts("dve", xt.full(), xt.full(), rs, op0=ALU.mult)
        for kq in range(4):
            bank = kq % 2
            for j in range(4):
                kt = kq * 4 + j
                k.tr(psv(bank, j * 128, (j + 1) * 128), xt[kt * 128:(kt + 1) * 128], ident.full())
            for j in range(4):
                kt = kq * 4 + j
                k.act(h[kt, sub * 128:(sub + 1) * 128], psv(bank, j * 128, (j + 1) * 128), AF.Identity,
                      scale=m1[kt:kt + 1], bias=sh1[kt:kt + 1])

    for sub in range(4):
        emit_norm(tiles1[0], sub)
    for tidx, ti in enumerate(tiles1):
        emit_mod_chunk()
        is_pre = ti < 8
        tok0 = (ti % 8) * TT
        h = hT[ti % 2]
        nxt = tiles1[tidx + 1] if tidx + 1 < len(tiles1) else None
        nsub = [0]
        blocks = []
        if not is_pre:
            blocks += [("q", 0), ("q", 1)]
        if ti >= 4:
            blocks += [("k", 2), ("k", 3), ("v", 4), ("v", 5)]
        blocks += [("u", 6), ("u", 7)]
        w0 = (ti - 4) * TT
        for kind, cb in blocks:
            wb = wblk[ctr["w"] % 3]
            ctr["w"] += 1
            k.dma("sp", wb.full(), dr(winb_v[:, :, cb * 512:(cb + 1) * 512], "winb"))
            feature_major = kind in ("q", "k") or (kind == "u" and not is_pre)
            if feature_major:
                for co in range(4):
                    pb = psv(2 + ctr["p"] % 3, 0, 512)
                    ctr["p"] += 1
                    for kt in range(16):
                        k.mm(pb, wb[kt, co * 128:(co + 1) * 128], h[kt], kt == 0, kt == 15)
                    if kind in ("q", "k"):
                        sq = sqt[ctr["s"] % 2].full()
                        rs = rst[ctr["s"] % 2].full()
                        pss = psv(5 + ctr["s"] % 2, 0, 512)
                        ctr["s"] += 1
                        k.act(sq, pb, AF.Square)
                        flush_pending()

                        def fin(sq=sq, rs=rs, pss=pss, pb=pb, kind=kind, cb=cb, co=co, tok0=tok0, w0=w0):
                            k.mm(pss, ones_f.full(), sq, True, True)
                            k.act(rs, pss, AF.Sqrt, scale=1.0 / 128, bias=EPS)
                            k.recip(rs, rs)
                            ob = obt[ctr["o"] % 3].full()
                            ctr["o"] += 1
                            gv = qgs if kind == "q" else kgs
                            k.stt("dve", ob, pb, gv.full(), rs, ALU.mult, ALU.mult)
                            head = (cb % 2) * 4 + co
                            if kind == "q":
                                k.dma("pool", dr(QT_d[head, :, tok0:tok0 + TT], "QT", tok0, tok0 + TT), ob)
                            else:
                                k.dma("pool", dr(KT_d[head, :, w0:w0 + TT], "KT", w0, w0 + TT), ob)
                        pending.append(fin)
                    else:
                        flush_pending()
                        of = oft[ctr["f"] % 2].full()
                        ctr["f"] += 1
                        k.copy("act", of, pb)
                        ut = (cb - 6) * 4 + co
                        k.dma("pool", dr(UT_d[ut, :, tok0:tok0 + TT], "UT", tok0, tok0 + TT), of)
            else:
                for sub in range(4):
                    pb = psv(2 + ctr["p"] % 3, 0, 512)
                    ctr["p"] += 1
                    for kt in range(16):
                        k.mm(pb, h[kt, sub * 128:(sub + 1) * 128], wb[kt], kt == 0, kt == 15)
                    flush_pending()
                    ob = obt[ctr["o"] % 3].full()
                    ctr["o"] += 1
                    if kind == "v":
                        if is_pre:
                            k.ts("dve", ob, pb, pm.full(), op0=ALU.mult)
                        else:
                            k.copy("act", ob, pb)
                        r0 = w0 + sub * 128
                        k.dma("pool", dr(Vs_d[r0:r0 + 128, (cb - 4) * 512:(cb - 3) * 512], "Vs", r0, r0 + 128), ob)
                    else:
                        k.copy("act", ob, pb)
                        r0 = tok0 + sub * 128
                        k.dma("pool", dr(Upre_d[r0:r0 + 128, (cb - 6) * 512:(cb - 5) * 512], "Upre", r0, r0 + 128), ob)
            if nxt is not None and nsub[0] < 4 and (len(blocks) - blocks.index((kind, cb)) <= 4 or len(blocks) <= 4):
                emit_norm(nxt, nsub[0])
                nsub[0] += 1
        while nxt is not None and nsub[0] < 4:
            emit_norm(nxt, nsub[0])
            nsub[0] += 1
    flush_pending()
    A.release()
    while mod_todo:
        emit_mod_chunk()
    sc2 = A.tile([16], F32)
    to_cols(shsc2, 0, sh2, 2)
    to_cols(shsc2, 2048, sc2, 2)
    k.stt("dve", m2.full(), sc2.full(), 1.0, g2n.full(), ALU.add, ALU.mult)
    A.release()
    if stop_after <= 1:
        k.s.barrier()
        k.s.emit()
        return nc
    return build_rest(nc, k, A, ps, psv, locals())


def _consts():
    ident = np.eye(128, dtype=np.float32)
    tri = np.triu(np.ones((128, 128), np.float32))
    kk = np.arange(128)[:, None]
    qq = np.arange(128)[None, :]
    ab = np.full((128, 3, 8, 2, 128), -30000.0, np.float32)
    for pi, d in enumerate((1, 4, 16)):
        for h in range(8):
            slope = 2.0 ** (-8.0 * (h + 1.0) / 8.0)
            st0 = qq - kk + 128
            b0 = np.where(kk >= qq, -slope * st0 * d, -30000.0)
            st1 = qq - kk
            b1 = np.where(kk <= qq, -slope * st1 * d, -30000.0)
            ab[:, pi, h, 0, :] = b0
            ab[:, pi, h, 1, :] = b1
    mask8 = np.zeros((8, 16, 8), np.float32)
    for g in range(8):
        mask8[g, :, g] = 1.0
    return ident, tri, ab.reshape(128, 24, 256), mask8.reshape(128, 8)


def prep_inputs(inputs):
    L = 0
    f = lambda a: np.ascontiguousarray(np.asarray(a, dtype=np.float32))
    x = np.asarray(inputs["x"], dtype=np.float32)
    c = np.asarray(inputs["c"], dtype=np.float32)
    col16 = lambda v: f(np.asarray(v).reshape(16, 128).T)
    col8 = lambda v: f(np.asarray(v).reshape(8, 128).T)
    ident, tri, abias, mask8 = _consts()
    lam_re = np.asarray(inputs["lam_re"][L], np.float32)
    lam_im = np.asarray(inputs["lam_im"][L], np.float32)
    lst = np.asarray(inputs["log_step"][L], np.float32)
    b_re = np.asarray(inputs["b_re"][L], np.float32)
    b_im = np.asarray(inputs["b_im"][L], np.float32)
    c_re = np.asarray(inputs["c_re"][L], np.float32)
    c_im = np.asarray(inputs["c_im"][L], np.float32)

    def layB(a):
        t = a.reshape(8, 8, 64).transpose(1, 0, 2)
        return f(np.broadcast_to(t[:, None], (8, 16, 8, 64)).reshape(128, 8, 64))

    def layS(a):
        return f(a.reshape(32, 2, 64).transpose(1, 2, 0).reshape(128, 32))

    lstB = np.broadcast_to(lst[:, None], (64, 64))
    shared = {
        "w_ada": f(inputs["w_ada"][L]), "b_ada": f(np.asarray(inputs["b_ada"][L])[None, :]),
        "g1n": col16(inputs["norm1_g"][L]), "g2n": col16(inputs["norm2_g"][L]),
        "w_in": f(inputs["w_in"][L]), "w_glu": f(inputs["w_glu"][L]), "w_out": f(inputs["w_out"][L]),
        "w_ff1": f(inputs["w_ff1"][L]), "w_ff2": f(inputs["w_ff2"][L]),
        "qg": f(np.asarray(inputs["q_norm_g"][L]).reshape(128, 1)),
        "kg": f(np.asarray(inputs["k_norm_g"][L]).reshape(128, 1)),
        "og": col16(np.concatenate([np.asarray(inputs["attn_out_g"][L]), np.asarray(inputs["ssm_out_g"][L])])),
        "bglu": col8(inputs["b_glu"][L]), "dsk": col8(inputs["d_skip"][L]),
        "ident": ident, "tri": tri, "abias": abias, "mask8": mask8,
        "sB_lre": layB(lam_re), "sB_lim": layB(lam_im), "sB_lst": layB(lstB),
        "sB_bre": f(b_re.reshape(8, 8, 64, 16).transpose(1, 3, 0, 2).reshape(128, 8, 64)),
        "sB_bim": f(b_im.reshape(8, 8, 64, 16).transpose(1, 3, 0, 2).reshape(128, 8, 64)),
        "sS_lre": layS(lam_re), "sS_lim": layS(lam_im), "sS_lst": layS(lstB),
    }
    for nm, b in (("re", b_re), ("im", b_im)):
        arr = np.zeros((2, 64, 32, 2, 16), np.float32)
        br = b.reshape(32, 2, 64, 16)
        for gh in range(2):
            arr[gh, :, :, gh, :] = br[:, gh].transpose(1, 0, 2)
        shared["bsel_" + nm] = arr.reshape(128, 32, 32)
    for nm, cm in (("re", c_re), ("im", c_im)):
        arr = np.zeros((2, 64, 32, 8, 16), np.float32)
        for g in range(64):
            arr[g % 2, :, g // 2, g % 8, :] = cm[g].T
        shared["cblk_" + nm] = arr.reshape(128, 32, 128)
    in_maps = []
    zeros_pre = np.zeros((NPRE, D_MODEL), np.float32)
    for core in range(8):
        b, half = divmod(core, 2)
        m = dict(shared)
        m["xo"] = f(x[b, half * NTOK:(half + 1) * NTOK])
        m["xp"] = f(x[b, 0:NPRE]) if half == 1 else zeros_pre
        m["pmask"] = np.full((128, 1), float(half), np.float32)
        m["c_col"] = f(c[b].reshape(16, 128).T)
        in_maps.append(m)
    return in_maps


def kernel(**inputs):
    in_maps = prep_inputs(inputs)
    nc = build()
    res = run_bass_kernel_spmd(nc, in_maps, core_ids=list(range(8)))
    out = np.empty((4, SEQ, D_MODEL), np.float32)
    for core in range(8):
        b, half = divmod(core, 2)
        out[b, half * NTOK:(half + 1) * NTOK] = res.results[core]["out"]
    return out


def build_rest(nc, k, A, ps, psv, L):
    stop_after = L["stop_after"]
    ident, identb, ones_f, pm = L["ident"], L["identb"], L["ones_f"], L["pm"]
    m2, sh2, bglu, dsk = L["m2"], L["sh2"], L["bglu"], L["dsk"]
    QT_d, KT_d, Vs_d, UT_d, Upre_d, NUM_d, YG_d = (L[n] for n in ("QT_d", "KT_d", "Vs_d", "UT_d", "Upre_d", "NUM_d", "YG_d"))
    xo_d, out_d = L["xo_d"], L["out_d"]
    skip = set(os.environ.get("MK_SKIP", "").split(","))

    def finish():
        k.s.barrier()
        k.s.emit()
        return nc

    A.mark()
    if "attn" not in skip:
        abias = A.tile([24, 256], F32)
        k.dma("sp", abias.full(), dr(L["abias_d"], "abias"))
        KTw = [A.tile([4096], BF16) for _ in range(2)]
        QTs = [A.tile([2048], BF16) for _ in range(2)]
        V1 = [A.tile([17, 129], BF16) for _ in range(2)]
        V4 = [A.tile([5, 4, 129], BF16) for _ in range(2)]
        V16 = [A.tile([2, 16, 129], BF16) for _ in range(2)]
        ssb = [A.tile([256], F32) for _ in range(4)]
        pbf = [A.tile([256], BF16) for _ in range(4)]
        stg = [A.tile([16, 129], F32) for _ in range(2)]
        cu = {"u": 0, "g": 0}
        n_span = int(os.environ.get("MK_SPANS", "2"))
        n_head = int(os.environ.get("MK_HEADS", "8"))
        for sp in range(n_span):
            S0 = sp * 2048
            for h in range(n_head):
                bi = (sp * 8 + h) % 2
                kw_, qs_, v1, v4, v16 = KTw[bi], QTs[bi], V1[bi], V4[bi], V16[bi]
                hc = slice(h * 128, (h + 1) * 128)
                k.dma("sp", kw_.full(), dr(KT_d[h, :, S0:S0 + 4096], "KT", S0, S0 + 4096))
                k.dma("sp", qs_.full(), dr(QT_d[h, :, S0:S0 + 2048], "QT", S0, S0 + 2048))
                wb1 = S0 + 2048 - 128
                src1 = Vs_d[wb1:wb1 + 17 * 128, hc].rearrange("(b p) e -> p b e", p=128)
                k.dma("sp", v1[0:9, 0:128], dr(src1[:, 0:9, :], "Vs", wb1, wb1 + 9 * 128))
                k.dma("sp", v1[9:17, 0:128], dr(src1[:, 9:17, :], "Vs", wb1 + 9 * 128, wb1 + 17 * 128))
                wb4 = S0 + 2048 - 512
                src4 = Vs_d[wb4:wb4 + 2560, hc].rearrange("(j m r) e -> m j r e", j=5, m=128, r=4)
                for j4 in range(5):
                    k.dma("sp", v4[j4, :, 0:128], dr(src4[:, j4], "Vs", wb4 + 512 * j4, wb4 + 512 * (j4 + 1)))
                src16 = Vs_d[S0:S0 + 4096, hc].rearrange("(j m r) e -> m j r e", j=2, m=128, r=16)
                for jj in range(2):
                    k.dma("sp", v16[jj, :, 0:128], dr(src16[:, jj], "Vs", S0 + jj * 2048, S0 + (jj + 1) * 2048))
                k.memset("dve", v1[:, 128:129], 1.0)
                k.memset("dve", v4[:, :, 128:129], 1.0)
                k.memset("dve", v16[:, :, 128:129], 1.0)
                if sp == 0:
                    k.copy("dve", v1[0:1, 128:129], bc(pm.full(), [128, 1, 1], 1))
                    k.copy("dve", v4[0, :, 128:129], bc(pm.full(), [128, 4, 1], 1))
                    k.copy("dve", v16[0, :, 128:129], bc(pm.full(), [128, 16, 1], 1))
                for pi in range(3):
                    sg = stg[cu["g"] % 2]
                    cu["g"] += 1

                    def operands(u, pi=pi):
                        if pi == 0:
                            return (qs_[128 * u:128 * (u + 1)], kw_[1920 + 128 * u:2048 + 128 * u],
                                    kw_[2048 + 128 * u:2176 + 128 * u], v1[u], v1[u + 1])
                        if pi == 1:
                            jq, r = divmod(u, 4)
                            return (qs_[slice(512 * jq + r, 512 * (jq + 1), 4)],
                                    kw_[slice(1536 + 512 * jq + r, 2048 + 512 * jq, 4)],
                                    kw_[slice(2048 + 512 * jq + r, 2560 + 512 * jq, 4)], v4[jq, r], v4[jq + 1, r])
                        r = u
                        return (qs_[slice(r, 2048, 16)], kw_[slice(r, 2048, 16)], kw_[slice(2048 + r, 4096, 16)],
                                v16[0, r], v16[1, r])

                    base = cu["u"]
                    cu["u"] += 16

                    def stageA(u, pi=pi, base=base):
                        q, kp, kc, vp, vc = operands(u)
                        i = base + u
                        pS = ps[i % 4]
                        k.mm(pS[0:128], kp, q, True, True)
                        k.mm(pS[128:256], kc, q, True, True)
                        sb_ = ssb[i % 4].full()
                        k.tt("dve", sb_, pS[0:256], abias[pi * 8 + h], ALU.add)
                        k.act(pbf[i % 4].full(), sb_, AF.Exp)

                    def stageB(u, sg=sg, base=base):
                        q, kp, kc, vp, vc = operands(u)
                        i = base + u
                        pO = ps[4 + i % 4]
                        pb_ = pbf[i % 4]
                        k.mm(pO[0:129], pb_[0:128], vp, True, False)
                        k.mm(pO[0:129], pb_[128:256], vc, False, True)
                        k.copy("act", sg[u], pO[0:129])

                    SK = 2
                    for step in range(16 + SK):
                        if step < 16:
                            stageA(step)
                        if step - SK >= 0:
                            stageB(step - SK)
                    nd = NUM_d[pi][S0:S0 + 2048, h, :]
                    if pi == 1:
                        dst = nd.rearrange("(j m r) e -> m j r e", j=4, m=128, r=4)
                        for j4 in range(4):
                            k.dma("pool", dr(dst[:, j4], f"NUM{pi}", S0 + 512 * j4, S0 + 512 * (j4 + 1)),
                                  sg[4 * j4:4 * (j4 + 1)])
                    else:
                        dst = nd.rearrange("(b m) e -> m b e", m=128) if pi == 0 else nd.rearrange("(m r) e -> m r e", r=16)
                        k.dma("pool", dr(dst, f"NUM{pi}", S0, S0 + 2048), sg.full())
    A.release()
    if stop_after <= 2:
        return finish()

    A.mark()
    if "ssm" not in skip:
        build_ssm(k, A, ps, L)
    A.release()
    if stop_after <= 3:
        return finish()
    build_phase3(k, A, ps, L)
    return finish()


def build_ssm(k, A, ps, L):
    pm, dsk = L["pm"], L["dsk"]
    UT_d, Upre_d, YG_d = L["UT_d"], L["Upre_d"], L["YG_d"]
    sB_d, sS_d, bsel_d, cblk_d = L["sB_d"], L["sS_d"], L["bsel_d"], L["cblk_d"]
    MUL, ADD, SUB = ALU.mult, ALU.add, ALU.subtract

    def load(shape, d_ap, name):
        t = A.tile(shape, F32)
        k.dma("sp", t.full(), dr(d_ap, name))
        return t

    def cpow(lre, lim, lst, fs, sign):
        T = lambda: A.tile(fs, F32)
        step, lr, li, m, sn, cs, zr, zi, t1, t2 = (T() for _ in range(10))
        k.act(step.full(), lst.full(), AF.Exp)
        k.tt("dve", lr.full(), lre.full(), step.full(), MUL)
        k.tt("dve", li.full(), lim.full(), step.full(), MUL)
        k.act(m.full(), lr.full(), AF.Exp, scale=sign / 16.0)
        k.act(sn.full(), li.full(), AF.Sin, scale=sign / 16.0)
        k.act(cs.full(), li.full(), AF.Sin, scale=-1.0 / 16.0, bias=math.pi / 2)
        k.tt("dve", zr.full(), m.full(), cs.full(), MUL)
        k.tt("dve", zi.full(), m.full(), sn.full(), MUL)
        for _ in range(4):
            k.tt("dve", t1.full(), zr.full(), zr.full(), MUL)
            k.tt("dve", t2.full(), zi.full(), zi.full(), MUL)
            k.stt("dve", zi.full(), zr.full(), 2.0, zi.full(), MUL, MUL)
            k.tt("dve", zr.full(), t1.full(), t2.full(), SUB)
        return zr, zi, lr, li

    def cmul(outr, outi, ar, ai, br, bi, t1, t2):
        k.tt("dve", t1, ar, br, MUL)
        k.tt("dve", t2, ai, bi, MUL)
        k.tt("dve", outr, t1, t2, SUB)
        k.tt("dve", t1, ar, bi, MUL)
        k.tt("dve", t2, ai, br, MUL)
        k.tt("dve", outi, t1, t2, ADD)

    def kappa(ar, ai, lre, lim, lst, fs):
        T = lambda: A.tile(fs, F32)
        am1, den, t1, t2, kr, ki = (T() for _ in range(6))
        k.ts("dve", am1.full(), ar.full(), -1.0, op0=ADD)
        k.tt("dve", t1.full(), lre.full(), lre.full(), MUL)
        k.tt("dve", t2.full(), lim.full(), lim.full(), MUL)
        k.tt("dve", den.full(), t1.full(), t2.full(), ADD)
        k.recip(den.full(), den.full())
        k.tt("dve", t1.full(), am1.full(), lre.full(), MUL)
        k.tt("dve", t2.full(), ai.full(), lim.full(), MUL)
        k.tt("dve", kr.full(), t1.full(), t2.full(), ADD)
        k.tt("dve", kr.full(), kr.full(), den.full(), MUL)
        k.tt("dve", t1.full(), ai.full(), lre.full(), MUL)
        k.tt("dve", t2.full(), am1.full(), lim.full(), MUL)
        k.tt("dve", ki.full(), t1.full(), t2.full(), SUB)
        k.tt("dve", ki.full(), ki.full(), den.full(), MUL)
        return kr, ki

    Bblk = {n: A.tile([8, 8, 64], BF16) for n in ("re", "im")}
    Cb = {n: A.tile([32, 128], BF16) for n in ("re", "imn")}
    Tpos = {n: A.tile([32, 128], F32) for n in ("re", "im")}
    TinvT = {n: A.tile([4096], F32) for n in ("re", "im")}
    a128 = {n: A.tile([32], F32) for n in ("re", "im")}
    trib = A.tile([128], BF16)
    cr = A.tile([32], F32)
    ci = A.tile([32], F32)
    Gend = {n: A.tile([32], F32) for n in ("re", "im")}
    sr_ = A.tile([32], F32)
    si_ = A.tile([32], F32)
    tA = A.tile([32], F32)
    tB = A.tile([32], F32)
    A.mark()
    TinvTb = {n: A.tile([4096], BF16) for n in ("re", "im")}
    Bsel = {n: A.tile([32, 32], F32) for n in ("re", "im")}

    A.mark()
    fsB = [8, 64]
    lre, lim, lst = (load(fsB, sB_d[n], "sB" + n) for n in ("lre", "lim", "lst"))
    bre, bim = load(fsB, sB_d["bre"], "sBbre"), load(fsB, sB_d["bim"], "sBbim")
    mask8 = load([8], L["mask8_d"], "mask8")
    ar, ai, _, _ = cpow(lre, lim, lst, fsB, 1.0)
    kr, ki = kappa(ar, ai, lre, lim, lst, fsB)
    BbR, BbI, t1, t2 = (A.tile(fsB, F32) for _ in range(4))
    cmul(BbR.full(), BbI.full(), kr.full(), ki.full(), bre.full(), bim.full(), t1.full(), t2.full())
    mb = mask8.full().w(mask8.ap.unsqueeze(1).unsqueeze(3).broadcast_to([128, 8, 8, 64]))
    for n, src in (("re", BbR), ("im", BbI)):
        sv = src.full().w(src.ap.unsqueeze(2).broadcast_to([128, 8, 8, 64]))
        k.tt("dve", Bblk[n].full(), sv, mb, MUL)
    A.release()

    ssm_stop = int(os.environ.get("MK_SSM_STOP", "9"))
    if ssm_stop <= 1:
        return
    A.mark()
    fsS = [32]
    lre, lim, lst = (load(fsS, sS_d[n], "sS" + n) for n in ("lre", "lim", "lst"))
    ar, ai, _, _ = cpow(lre, lim, lst, fsS, 1.0)
    ir, ii, _, _ = cpow(lre, lim, lst, fsS, -1.0)
    kr, ki = kappa(ar, ai, lre, lim, lst, fsS)
    sub_stop = int(os.environ.get("MK_SSM_SUB", "99"))
    if sub_stop <= 1:
        A.release()
        return
    A.mark()
    bsr = load([32, 32], bsel_d["re"], "bselre")
    bsi = load([32, 32], bsel_d["im"], "bselim")
    t1 = A.tile([32, 32], F32)
    t2 = A.tile([32, 32], F32)
    krb = kr.full().w(kr.ap.unsqueeze(2).broadcast_to([128, 32, 32]))
    kib = ki.full().w(ki.ap.unsqueeze(2).broadcast_to([128, 32, 32]))
    cmul(Bsel["re"].full(), Bsel["im"].full(), krb, kib, bsr.full(), bsi.full(), t1.full(), t2.full())
    A.release()
    if sub_stop <= 2:
        A.release()
        return
    A.mark()
    ctmp = A.tile([32, 128], F32)
    k.dma("sp", ctmp.full(), dr(cblk_d["re"], "cblkre"))
    k.copy("act", Cb["re"].full(), ctmp.full())
    ctmp2 = A.tile([32, 128], F32)
    k.dma("sp", ctmp2.full(), dr(cblk_d["im"], "cblkim"))
    k.act(Cb["imn"].full(), ctmp2.full(), AF.Identity, scale=-1.0)
    A.release()
    trf = A.tile([128], F32)
    k.dma("sp", trf.full(), dr(L["tri_d"], "tri"))
    k.copy("dve", trib.full(), trf.full())
    if sub_stop <= 3:
        A.release()
        return

    def table(dst_r, dst_i, br_, bi_, want_p128=None):
        A.mark()
        pr = A.tile([32], F32)
        pi_ = A.tile([32], F32)
        q1 = A.tile([32], F32)
        q2 = A.tile([32], F32)
        x1 = Tile(A.ap[:, TinvT["re"].lo // 4:(TinvT["re"].lo + 8192) // 4].rearrange("p (a b) -> p a b", b=64),
                  "sb", TinvT["re"].lo, [32, 64], 4)
        x2 = Tile(A.ap[:, (TinvT["re"].lo + 8192) // 4:(TinvT["re"].lo + 16384) // 4].rearrange("p (a b) -> p a b", b=64),
                  "sb", TinvT["re"].lo + 8192, [32, 64], 4)
        k.memset("dve", dst_r[:, 0:1], 1.0)
        k.memset("dve", dst_i[:, 0:1], 0.0)
        k.copy("dve", dst_r[:, 1:2], br_.full().w(br_.ap.unsqueeze(2)))
        k.copy("dve", dst_i[:, 1:2], bi_.full().w(bi_.ap.unsqueeze(2)))
        k.copy("dve", pr.full(), br_.full())
        k.copy("dve", pi_.full(), bi_.full())
        for j in range(1, int(os.environ.get("MK_TAB_J", "8"))):
            k.tt("dve", q1.full(), pr.full(), pr.full(), MUL)
            k.tt("dve", q2.full(), pi_.full(), pi_.full(), MUL)
            k.stt("dve", pi_.full(), pr.full(), 2.0, pi_.full(), MUL, MUL)
            k.tt("dve", pr.full(), q1.full(), q2.full(), SUB)
            if j == 7:
                break
            n = 1 << j
            prb = pr.full().w(pr.ap.unsqueeze(2).broadcast_to([128, 32, n]))
            pib = pi_.full().w(pi_.ap.unsqueeze(2).broadcast_to([128, 32, n]))
            cmul(dst_r[:, n:2 * n], dst_i[:, n:2 * n], dst_r[:, 0:n], dst_i[:, 0:n], prb, pib,
                 x1[:, 0:n], x2[:, 0:n])
        if want_p128 is not None:
            k.copy("dve", want_p128[0].full(), pr.full())
            k.copy("dve", want_p128[1].full(), pi_.full())
        A.release()

    table(Tpos["re"], Tpos["im"], ar, ai, want_p128=(a128["re"], a128["im"]))
    if sub_stop <= 4:
        A.release()
        return
    TiS = {n: A.tile([32, 128], F32) for n in ("re", "im")}
    table(TiS["re"], TiS["im"], ir, ii)
    if sub_stop <= 5:
        A.release()
        return
    cnt = 0
    for n in ("re", "im"):
        for g4 in range(8):
            bank = cnt % 2
            cnt += 1
            for j in range(4):
                gp = g4 * 4 + j
                k.tr(ps[bank][j * 128:(j + 1) * 128], TiS[n][gp], L["ident"].full())
            k.copy("dve", TinvT[n][g4 * 512:(g4 + 1) * 512], ps[bank][0:512])
            k.copy("act", TinvTb[n][g4 * 512:(g4 + 1) * 512], TinvT[n][g4 * 512:(g4 + 1) * 512])
    A.release()

    def carry_update():
        k.tt("dve", sr_.full(), Gend["re"].full(), cr.full(), ADD)
        k.tt("dve", si_.full(), Gend["im"].full(), ci.full(), ADD)
        cmul(cr.full(), ci.full(), a128["re"].full(), a128["im"].full(), sr_.full(), si_.full(), tA.full(), tB.full())

    k.memset("dve", cr.full(), 0.0)
    k.memset("dve", ci.full(), 0.0)
    if ssm_stop <= 2:
        A.release()
        return

    A.mark()
    upb = [A.tile([1024], BF16) for _ in range(2)]
    w1 = [A.tile([512], F32) for _ in range(2)]
    w2 = [A.tile([512], F32) for _ in range(2)]
    n_pre = int(os.environ.get("MK_NPRE", "32"))
    for n in range(32 - n_pre, 32):
        up = upb[n % 2]
        k.dma("sp", up.full(), dr(Upre_d[n * 128:(n + 1) * 128, :], "Upre", n * 128, (n + 1) * 128))
        pso = (n % 2) * 4
        for gp in range(32):
            hf, c0 = divmod(gp, 16)
            k.mm(ps[pso + hf][c0 * 32:(c0 + 1) * 32], TinvTb["re"][gp * 128:(gp + 1) * 128], up[gp * 32:(gp + 1) * 32], True, True)
            k.mm(ps[pso + 2 + hf][c0 * 32:(c0 + 1) * 32], TinvTb["im"][gp * 128:(gp + 1) * 128], up[gp * 32:(gp + 1) * 32], True, True)
        for hf in range(2):
            Mre, Mim = ps[pso + hf][0:512], ps[pso + 2 + hf][0:512]
            bsr_ = Bsel["re"][hf * 16:(hf + 1) * 16]
            bsr_ = bsr_.w(bsr_.ap.rearrange("p a b -> p (a b)"))
            bsi_ = Bsel["im"][hf * 16:(hf + 1) * 16]
            bsi_ = bsi_.w(bsi_.ap.rearrange("p a b -> p (a b)"))
            x1, x2 = w1[hf].full(), w2[hf].full()
            k.tt("dve", x1, Mre, bsr_, MUL)
            k.tt("dve", x2, Mim, bsi_, MUL)
            k.tt("pool", x1, x1, x2, SUB)
            k.reduce_sum(Gend["re"][hf * 16:(hf + 1) * 16], x1.w(x1.ap.rearrange("p (a b) -> p a b", b=32)))
            k.tt("dve", x1, Mim, bsr_, MUL)
            k.tt("dve", x2, Mre, bsi_, MUL)
            k.tt("pool", x1, x1, x2, ADD)
            k.reduce_sum(Gend["im"][hf * 16:(hf + 1) * 16], x1.w(x1.ap.rearrange("p (a b) -> p a b", b=32)))
        carry_update()
    A.release()
    A.release()
    k.ts("dve", cr.full(), cr.full(), pm.full(), op0=MUL)
    k.ts("dve", ci.full(), ci.full(), pm.full(), op0=MUL)

    if ssm_stop <= 3:
        return
    A.mark()
    uTf = [A.tile([8, 128], F32) for _ in range(2)]
    ub = [A.tile([8, 128], BF16) for _ in range(2)]
    bpp = {n: [A.tile([512], BF16) for _ in range(2)] for n in ("re", "im")}
    f1 = [A.tile([512], F32) for _ in range(2)]
    f2 = [A.tile([512], F32) for _ in range(2)]
    hT = {n: [A.tile([4, 128], BF16) for _ in range(2)] for n in ("re", "im")}
    ytmp = [A.tile([128], F32) for _ in range(2)]
    ygt = [A.tile([8, 128], F32) for _ in range(2)]
    UT_v = UT_d.rearrange("k p t -> p k t")
    YG_v = YG_d.rearrange("k p t -> p k t")
    n_own = int(os.environ.get("MK_NOWN", "32"))
    cr2 = [cr, A.tile([32], F32)]
    ci2 = [ci, A.tile([32], F32)]

    def carry_update2(n):
        k.tt("dve", sr_.full(), Gend["re"].full(), cr2[n % 2].full(), ADD)
        k.tt("dve", si_.full(), Gend["im"].full(), ci2[n % 2].full(), ADD)
        cmul(cr2[(n + 1) % 2].full(), ci2[(n + 1) % 2].full(), a128["re"].full(), a128["im"].full(),
             sr_.full(), si_.full(), tA.full(), tB.full())

    def S1(it):
        n, kt = divmod(it, 8)
        if kt == 0:
            k.dma("sp", uTf[n % 2].full(), dr(UT_v[:, :, n * 128:(n + 1) * 128], "UT", n * 128, (n + 1) * 128))
            k.copy("act", ub[n % 2].full(), uTf[n % 2].full())
        u_b = ub[n % 2]
        k.mm(ps[0][0:512], u_b[kt], Bblk["re"][kt].w(Bblk["re"].ap[:, kt].rearrange("p a b -> p (a b)")), True, True)
        k.mm(ps[1][0:512], u_b[kt], Bblk["im"][kt].w(Bblk["im"].ap[:, kt].rearrange("p a b -> p (a b)")), True, True)

    def S2(it):
        n, kt = divmod(it, 8)
        i2 = it % 2
        fsl = slice(kt * 512, (kt + 1) * 512)
        x1, x2 = f1[0].full(), f2[0].full()
        x3, x4 = f1[1].full(), f2[1].full()
        k.tt("dve", x1, ps[0][0:512], TinvT["re"][fsl], MUL)
        k.tt("dve", x2, ps[1][0:512], TinvT["im"][fsl], MUL)
        k.tt("dve", x3, ps[0][0:512], TinvT["im"][fsl], MUL)
        k.tt("dve", x4, ps[1][0:512], TinvT["re"][fsl], MUL)
        k.tt("pool", bpp["re"][i2].full(), x1, x2, SUB)
        k.tt("pool", bpp["im"][i2].full(), x3, x4, ADD)

    def S3(it):
        n, kt = divmod(it, 8)
        i2 = it % 2
        pgr, pgi = ps[2 + 2 * i2], ps[3 + 2 * i2]
        br_, bi_ = bpp["re"][i2], bpp["im"][i2]
        for gq in range(4):
            k.mm(pgr[gq * 128:(gq + 1) * 128], br_[gq * 128:(gq + 1) * 128], trib.full(), True, True)
        for gq in range(4):
            k.mm(pgi[gq * 128:(gq + 1) * 128], bi_[gq * 128:(gq + 1) * 128], trib.full(), True, True)
        lastr = pgr.full().w(pgr.ap.rearrange("p (a b) -> p a b", b=128)[:, :, 127])
        lasti = pgi.full().w(pgi.ap.rearrange("p (a b) -> p a b", b=128)[:, :, 127])
        k.copy("act", Gend["re"][kt * 4:(kt + 1) * 4], lastr)
        k.copy("act", Gend["im"][kt * 4:(kt + 1) * 4], lasti)
        if kt == 7:
            carry_update2(n)

    def S4(it):
        n, kt = divmod(it, 8)
        i2 = it % 2
        pgr, pgi = ps[2 + 2 * i2], ps[3 + 2 * i2]
        hr, hi_ = hT["re"][i2], hT["im"][i2]
        crn, cin_ = cr2[n % 2], ci2[n % 2]
        gcr, gci = Gc["re"][i2], Gc["im"][i2]
        for gq in range(4):
            gp = kt * 4 + gq
            k.act(gcr[gq], pgr[gq * 128:(gq + 1) * 128], AF.Identity, bias=crn[gp:gp + 1])
            k.act(gci[gq], pgi[gq * 128:(gq + 1) * 128], AF.Identity, bias=cin_[gp:gp + 1])
        tr_ = Tpos["re"][kt * 4:(kt + 1) * 4]
        ti_ = Tpos["im"][kt * 4:(kt + 1) * 4]
        q1, q2, q3, q4 = (t_.full() for t_ in mtmp[i2])
        k.tt("dve", q1, gcr.full(), tr_, MUL)
        k.tt("dve", q2, gci.full(), ti_, MUL)
        k.tt("dve", q3, gcr.full(), ti_, MUL)
        k.tt("dve", q4, gci.full(), tr_, MUL)
        k.tt("pool", hr.full(), q1, q2, SUB)
        k.tt("pool", hi_.full(), q3, q4, ADD)

    def S5(it):
        n, kt = divmod(it, 8)
        i2 = it % 2
        hr, hi_ = hT["re"][i2], hT["im"][i2]
        py = ps[6 + i2]
        for gq in range(4):
            gp = kt * 4 + gq
            k.mm(py[0:128], Cb["re"][gp], hr[gq], gq == 0, False)
            k.mm(py[0:128], Cb["imn"][gp], hi_[gq], False, gq == 3)

    def S6(it):
        n, kt = divmod(it, 8)
        i2 = it % 2
        py = ps[6 + i2]
        yt = ytmp[i2].full()
        yg = ygt[n % 2]
        k.stt("dve", yt, uTf[n % 2][kt], dsk[kt:kt + 1], py[0:128], MUL, ADD)
        k.act(yg[kt], yt, AF.Gelu_apprx_tanh)
        if kt == 7:
            k.dma("pool", dr(YG_v[:, :, n * 128:(n + 1) * 128], "YG", n * 128, (n + 1) * 128), yg.full())

    Gc = {n_: [A.tile([4, 128], F32) for _ in range(2)] for n_ in ("re", "im")}
    mtmp = [[A.tile([4, 128], F32) for _ in range(4)] for _ in range(2)]
    stages = [S1, S2, S3, S4, S5, S6]
    n_it = n_own * 8
    for slot in range(n_it + len(stages) - 1):
        for si in range(len(stages) - 1, -1, -1):
            it_ = slot - si
            if 0 <= it_ < n_it:
                stages[si](it_)
    A.release()


def build_phase3(k, A, ps, L):
    ident, identb, ones_f = L["ident"], L["identb"], L["ones_f"]
    m2, sh2, bglu = L["m2"], L["sh2"], L["bglu"]
    og, g1bc_t, g2bc_t = L["og"], L["g1bc_t"], L["g2bc_t"]
    NUM_d, YG_d, xo_d, out_d = L["NUM_d"], L["YG_d"], L["xo_d"], L["out_d"]
    MUL, ADD = ALU.mult, ALU.add
    wglu_v = L["wglub_d"].rearrange("(kt p) c -> p kt c", p=128)
    wout_v = L["woutb_d"].rearrange("(kt p) c -> p kt c", p=128)
    wff1_v = L["wff1b_d"].rearrange("(kt p) c -> p kt c", p=128)
    wff2_v = L["wff2b_d"].rearrange("(kt p) c -> p kt c", p=128)
    YG_v = YG_d.rearrange("k p t -> p k t")
    A.mark()
    wblk = [A.tile([16, 512], BF16) for _ in range(3)]
    xmid = A.tile([4, 2048], F32)
    actT = A.tile([16, 512], BF16)
    stat = A.tile([16], F32)
    gtmp = [A.tile([512], F32) for _ in range(2)]
    cw = {"w": 0, "p": 0}

    def next_w():
        w = wblk[cw["w"] % 3]
        cw["w"] += 1
        return w

    n_t3 = int(os.environ.get("MK_P3_TILES", "8"))
    for ti in range(n_t3):
        tok0 = ti * TT
        for sub in range(4):
            k.dma("sp", xmid[sub], dr(xo_d[tok0 + sub * 128: tok0 + (sub + 1) * 128, :], "x"))
        k.memset("dve", stat.full(), 0.0)
        A.mark()
        nt_ = [A.tile([8, 129], F32) for _ in range(3)]
        nt = [nt_, nt_]
        attn = A.tile([8, 128], F32)
        attb = A.tile([1024], BF16)
        junk = A.tile([1024], BF16)
        rec = A.tile([8], F32)
        for sub in range(4):
            t0 = tok0 + sub * 128
            n0, n1, n2 = nt[sub % 2]
            for pi, t_ in enumerate((n0, n1, n2)):
                k.dma("sp", t_.full(), dr(NUM_d[pi][t0:t0 + 128], f"NUM{pi}", t0, t0 + 128))
            k.tt("dve", n0.full(), n0.full(), n1.full(), ADD)
            k.tt("dve", n0.full(), n0.full(), n2.full(), ADD)
            k.recip(rec.full(), n0[:, 128:129].w(n0.ap[:, :, 128]))
            k.tt("dve", attn.full(), n0[:, 0:128], rec.full().w(rec.ap.unsqueeze(2).broadcast_to([128, 8, 128])), MUL)
            ss = stat[sub:sub + 1]
            rs = stat[4 + sub:5 + sub]
            af = attn.full().w(attn.ap.rearrange("p a b -> p (a b)"))
            k.act(junk.full(), af, AF.Square, accum_out=ss)
            k.act(rs, ss, AF.Sqrt, scale=1.0 / 1024, bias=EPS)
            k.recip(rs, rs)
            k.ts("dve", attb.full(), af, rs, op0=MUL)
            pT = ps[sub % 2]
            pTb = pT.full().w(pT.ap.bitcast(BF16))
            for h in range(8):
                k.tr(pTb.w(pTb.ap[:, h * 128:(h + 1) * 128]), attb[h * 128:(h + 1) * 128], identb.full())
            for h in range(8):
                k.act(actT[h, sub * 128:(sub + 1) * 128], pTb.w(pTb.ap[:, h * 128:(h + 1) * 128]), AF.Identity,
                      scale=og[h:h + 1])
        yg = A.tile([8, 512], F32)
        ygb = A.tile([8, 512], BF16)
        ssm = yg
        gate = [A.tile([512], F32) for _ in range(2)]
        sq = [A.tile([512], F32) for _ in range(2)]
        rbc = A.tile([512], F32)
        k.dma("sp", yg.full(), dr(YG_v[:, :, tok0:tok0 + TT], "YG", tok0, tok0 + TT))
        k.copy("act", ygb.full(), yg.full())
        wg = next_w()
        wgv = wg.full().w(wg.ap.rearrange("p a b -> p (a b)").rearrange("p (a b) -> p a b", b=1024))
        k.dma("sp", wgv, dr(wglu_v, "wglub"))
        for co in range(8):
            pb = ps[2 + co % 2]
            for kt in range(8):
                k.mm(pb[0:512], wgv.w(wgv.ap[:, kt, co * 128:(co + 1) * 128]), ygb[kt], kt == 0, kt == 7)
            g_ = gate[co % 2].full()
            s_ = sq[co % 2].full()
            k.act(g_, pb[0:512], AF.Sigmoid, bias=bglu[co:co + 1])
            k.tt("dve", ssm[co], yg[co], g_, MUL)
            k.act(s_, ssm[co], AF.Square)
            k.mm(ps[4][0:512], ones_f.full(), s_, co == 0, co == 7)
        k.act(rbc.full(), ps[4][0:512], AF.Sqrt, scale=1.0 / 1024, bias=EPS)
        k.recip(rbc.full(), rbc.full())
        for co in range(8):
            k.stt("dve", actT[8 + co], ssm[co], og[8 + co:9 + co], rbc.full(), MUL, MUL)
        A.release()
        for cc in range(4):
            wb = next_w()
            k.dma("sp", wb.full(), dr(wout_v[:, :, cc * 512:(cc + 1) * 512], "woutb"))
            for sub in range(4):
                pb = ps[5 + cw["p"] % 2]
                cw["p"] += 1
                for kt in range(16):
                    k.mm(pb[0:512], actT[kt, sub * 128:(sub + 1) * 128], wb[kt], kt == 0, kt == 15)
                xs = xmid[sub, cc * 512:(cc + 1) * 512]
                gt_ = gtmp[cw["p"] % 2].full()
                k.tt("dve", gt_, pb[0:512], g1bc_t[cc * 512:(cc + 1) * 512], MUL)
                k.tt("pool", xs, xs, gt_, ADD)
        A.mark()
        hid = A.tile([64, 512], BF16)
        xn_t = Tile(A.ap[:, hid.lo // 4:(hid.lo + 8192) // 4], "sb", hid.lo, [2048], 4)
        xn = [xn_t, xn_t]
        junk2 = Tile(A.ap[:, (hid.lo + 8192) // 4:(hid.lo + 12288) // 4].bitcast(BF16), "sb", hid.lo + 8192, [2048], 2)
        rl = [A.tile([512], F32) for _ in range(2)]
        ot = gtmp
        for sub in range(4):
            ss = stat[8 + sub:9 + sub]
            rs = stat[12 + sub:13 + sub]
            k.act(junk2.full(), xmid[sub], AF.Square, accum_out=ss)
            k.act(rs, ss, AF.Sqrt, scale=1.0 / D_MODEL, bias=EPS)
            k.recip(rs, rs)
            x_ = xn[sub % 2]
            k.ts("dve", x_.full(), xmid[sub], rs, op0=MUL)
            for kq in range(4):
                bank = kq % 2
                for j in range(4):
                    kt = kq * 4 + j
                    k.tr(ps[bank][j * 128:(j + 1) * 128], x_[kt * 128:(kt + 1) * 128], ident.full())
                for j in range(4):
                    kt = kq * 4 + j
                    k.act(actT[kt, sub * 128:(sub + 1) * 128], ps[bank][j * 128:(j + 1) * 128], AF.Identity,
                          scale=m2[kt:kt + 1], bias=sh2[kt:kt + 1])
        for blk in range(16):
            wb = next_w()
            k.dma("sp", wb.full(), dr(wff1_v[:, :, blk * 512:(blk + 1) * 512], "wff1b"))
            for co in range(4):
                pb = ps[2 + cw["p"] % 3]
                cw["p"] += 1
                for kt in range(16):
                    k.mm(pb[0:512], wb[kt, co * 128:(co + 1) * 128], actT[kt], kt == 0, kt == 15)
                r_ = rl[(blk * 4 + co) % 2].full()
                k.act(r_, pb[0:512], AF.Relu)
                k.tt("pool", hid[blk * 4 + co], r_, r_, MUL)
        for cc in range(4):
            bset = (cc % 2) * 4
            for q in range(4):
                wb = next_w()
                k.dma("sp", wb.full(), dr(wff2_v[:, q * 16:(q + 1) * 16, cc * 512:(cc + 1) * 512], "wff2b"))
                for sub in range(4):
                    for kt in range(16):
                        k.mm(ps[bset + sub][0:512], hid[q * 16 + kt, sub * 128:(sub + 1) * 128], wb[kt],
                             q == 0 and kt == 0, q == 3 and kt == 15)
            for sub in range(4):
                o_ = ot[sub % 2].full()
                k.tt("dve", o_, ps[bset + sub][0:512], g2bc_t[cc * 512:(cc + 1) * 512], MUL)
                k.tt("pool", o_, o_, xmid[sub, cc * 512:(cc + 1) * 512], ADD)
                r0 = tok0 + sub * 128
                k.dma("pool", dr(out_d[r0:r0 + 128, cc * 512:(cc + 1) * 512], "out", r0 * 4 + cc, r0 * 4 + cc + 1), o_)
        A.release()
    A.release()
```

```python
import bisect
import contextlib
import math
import os
import numpy as np
import concourse.bass as bass
import concourse.mybir as mybir
from concourse.bass_utils import run_bass_kernel_spmd

F32 = mybir.dt.float32
BF16 = mybir.dt.bfloat16
AF = mybir.ActivationFunctionType
ALU = mybir.AluOpType
AX = mybir.AxisListType
EPOCH = 30000
N_DMA_SEMS = 48
ESZ = {F32: 4, BF16: 2}

D_MODEL = 2048
SEQ = 8192
NTOK = 4096
NPRE = 4096
KVPRE = 2048
TT = 512
EPS = 1e-6


class V:
    __slots__ = ("ap", "space", "lo", "hi")

    def __init__(self, ap, space, lo, hi):
        self.ap, self.space, self.lo, self.hi = ap, space, lo, hi

    def w(self, ap):
        return V(ap, self.space, self.lo, self.hi)


class Tile:
    def __init__(self, ap, space, lo, free_shape, esz):
        self.ap, self.space, self.lo, self.free_shape, self.esz = ap, space, lo, tuple(free_shape), esz
        n = 1
        for s in free_shape:
            n *= s
        self.hi = lo + n * esz

    def full(self):
        return V(self.ap, self.space, self.lo, self.hi)

    def __getitem__(self, idx):
        if not isinstance(idx, tuple):
            idx = (idx,)
        return self.v(idx)

    def v(self, idx, p=None):
        idx = tuple(idx) + (slice(None),) * (len(self.free_shape) - len(idx))
        strides = []
        st = 1
        for s in reversed(self.free_shape):
            strides.append(st)
            st *= s
        strides = strides[::-1]
        mn = mx = 0
        for i, s, stv in zip(idx, self.free_shape, strides):
            if isinstance(i, int):
                mn += i * stv
                mx += i * stv
            else:
                a, b, c = i.indices(s)
                assert b > a
                last = a + ((b - 1 - a) // c) * c
                mn += a * stv
                mx += last * stv
        ps = slice(None) if p is None else p
        if self.space == "ps":
            return V(self.ap[(ps,) + idx], self.space, self.lo, self.hi)
        return V(self.ap[(ps,) + idx], self.space, self.lo + mn * self.esz, self.lo + (mx + 1) * self.esz)


def dr(ap, name, lo=0, hi=1 << 40):
    return V(ap, ("dram", name), lo, hi)


class Arena:
    def __init__(self, nc, nbytes, name="arena"):
        self.nbytes = nbytes
        self.t = nc.alloc_sbuf_tensor(name, [128, nbytes // 4], F32)
        self.ap = self.t.ap()
        self.off = 0
        self.marks = []
        self.peak = 0

    def tile(self, free_shape, dtype):
        esz = ESZ[dtype]
        n = 1
        for s in free_shape:
            n *= s
        nb = (n * esz + 63) // 64 * 64
        lo = self.off
        assert lo + nb <= self.nbytes, f"SBUF arena overflow {lo}+{nb}>{self.nbytes}"
        self.off += nb
        self.peak = max(self.peak, self.off)
        ap = self.ap[:, lo // 4:(lo + nb) // 4]
        if dtype != F32:
            ap = ap.bitcast(dtype)
        ap = ap[:, 0:n]
        if len(free_shape) > 1:
            names = " ".join(f"a{i}" for i in range(len(free_shape)))
            kw = {f"a{i}": s for i, s in enumerate(free_shape)}
            ap = ap.rearrange(f"p ({names}) -> p {names}", **kw)
        return Tile(ap, "sb", lo, free_shape, esz)

    def mark(self):
        self.marks.append(self.off)

    def release(self):
        self.off = self.marks.pop()


class Sched:
    ENGS = ["pe", "act", "dve", "pool", "sp"]

    def __init__(self, nc):
        self.nc = nc
        self.ops = {e: [] for e in self.ENGS}
        self.count = {e: 0 for e in self.ENGS}
        self.waited = {e: {} for e in self.ENGS}
        self.iv = {}
        self.dma_uses = [0] * N_DMA_SEMS
        self.dma_pool = {"sp": list(range(0, 28)), "pool": list(range(28, 44)), "act": list(range(44, 48))}
        self.dma_rr = {q: 0 for q in self.dma_pool}

    def _split(self, space, pos):
        starts, recs = self.iv.setdefault(space, ([], []))
        i = bisect.bisect_right(starts, pos) - 1
        if i >= 0:
            r = recs[i]
            if r["lo"] < pos < r["hi"]:
                r2 = {"lo": pos, "hi": r["hi"], "w": r["w"], "r": list(r["r"])}
                r["hi"] = pos
                starts.insert(i + 1, pos)
                recs.insert(i + 1, r2)

    def _records(self, space, lo, hi):
        starts, recs = self.iv.setdefault(space, ([], []))
        self._split(space, lo)
        self._split(space, hi)
        i = bisect.bisect_left(starts, lo)
        out = []
        cur = lo
        while cur < hi:
            if i < len(starts) and starts[i] == cur:
                out.append(recs[i])
                cur = recs[i]["hi"]
                i += 1
            else:
                nxt = min(starts[i] if i < len(starts) else hi, hi)
                r = {"lo": cur, "hi": nxt, "w": None, "r": []}
                starts.insert(i, cur)
                recs.insert(i, r)
                out.append(r)
                i += 1
                cur = nxt
        return out

    def _add(self, eng, fn, reads, writes, is_dma):
        if is_dma:
            pool_ = self.dma_pool[eng]
            si = pool_[self.dma_rr[eng] % len(pool_)]
            self.dma_rr[eng] += 1
            prev = self.dma_uses[si]
            self.dma_uses[si] += 1
            me = ("dma", si, 16 * (prev + 1))
        else:
            self.count[eng] += 1
            me = ("eng", eng, self.count[eng])
        raw = set()
        deps = set()
        for v in reads:
            for r in self._records(v.space, v.lo, v.hi):
                if r["w"] is not None:
                    raw.add(r["w"])
                    deps.add(r["w"])
                if v.space == "ps":
                    deps.update(d for d in r["r"] if d[1] != eng)
                r["r"].append(me)
                if len(r["r"]) > 16:
                    latest = {}
                    for d in r["r"]:
                        k = (d[0], d[1])
                        if k not in latest or d[2] > latest[k][2]:
                            latest[k] = d
                    r["r"] = list(latest.values())
        for v in writes:
            for r in self._records(v.space, v.lo, v.hi):
                if r["w"] is not None:
                    deps.add(r["w"])
                deps.update(r["r"])
            starts, rl = self.iv[v.space]
            i0 = bisect.bisect_left(starts, v.lo)
            i1 = bisect.bisect_left(starts, v.hi)
            del starts[i0:i1]
            del rl[i0:i1]
            starts.insert(i0, v.lo)
            rl.insert(i0, {"lo": v.lo, "hi": v.hi, "w": me, "r": []})
        deps.discard(me)
        if is_dma and prev > 0:
            deps.add(("dma", si, 16 * prev))
        need = {}
        for d in deps:
            if d[0] == "eng":
                if d[1] == eng and eng == "pe":
                    continue
                key = ("eng", d[1])
            else:
                key = ("dma", d[1])
            val = d[2]
            if self.waited[eng].get(key, 0) >= val:
                continue
            if need.get(key, 0) < val:
                need[key] = val
        for k, v in need.items():
            self.waited[eng][k] = v
        self.ops[eng].append((fn, list(need.items()), me))
        return me

    def op(self, eng, fn, reads=(), writes=()):
        return self._add(eng, fn, list(reads), list(writes), False)

    def dma(self, queue, out, in_, after=(), **kw):
        fn = lambda e: e.dma_start(out=out.ap, in_=in_.ap, **kw)
        return self._add(queue, fn, [in_] + list(after), [out], True)

    def barrier(self):
        for e in self.ENGS:
            need = {}
            for e2 in self.ENGS:
                if e2 != e and self.count[e2] > 0:
                    need[("eng", e2)] = self.count[e2]
            for si, u in enumerate(self.dma_uses):
                if u > 0:
                    need[("dma", si)] = 16 * u
            waits = []
            for k, v in need.items():
                if self.waited[e].get(k, 0) < v:
                    self.waited[e][k] = v
                    waits.append((k, v))
            if waits:
                self.ops[e].append((None, waits, None))

    def emit(self):
        nc = self.nc
        with contextlib.ExitStack() as st:
            esem = {}
            for e in self.ENGS:
                nep = max((self.count[e] + EPOCH - 1) // EPOCH, 1)
                esem[e] = [st.enter_context(nc.semaphore(f"s_{e}_{i}")) for i in range(nep)]
            dsem = [st.enter_context(nc.semaphore(f"s_dma_{i}")) for i in range(N_DMA_SEMS)]
            block = st.enter_context(nc.Block())

            def replay(ename, eobj):
                for fn, waits, me in self.ops[ename]:
                    for key, val in waits:
                        if key[0] == "eng":
                            eobj.wait_ge(esem[key[1]][(val - 1) // EPOCH], (val - 1) % EPOCH + 1)
                        else:
                            eobj.wait_ge(dsem[key[1]], val)
                    if fn is None:
                        continue
                    inst = fn(eobj)
                    if me[0] == "eng":
                        n = me[2]
                        inst.then_inc(esem[ename][(n - 1) // EPOCH], 1)
                    else:
                        inst.then_inc(dsem[me[1]], 16)

            @block.tensor
            def _(e):
                replay("pe", e)

            @block.scalar
            def _(e):
                replay("act", e)

            @block.vector
            def _(e):
                replay("dve", e)

            @block.gpsimd
            def _(e):
                replay("pool", e)

            @block.sync
            def _(e):
                replay("sp", e)


class K:
    def __init__(self, nc):
        self.nc = nc
        self.s = Sched(nc)
        self.rr = 0

    def _apv(self, x):
        return x.ap if isinstance(x, V) else x

    def act(self, out, in_, func, bias=None, scale=None, accum_out=None):
        kw = {}
        rd = [in_]
        wr = [out]
        if bias is not None:
            kw["bias"] = self._apv(bias)
            if isinstance(bias, V):
                rd.append(bias)
        if scale is not None:
            kw["scale"] = self._apv(scale)
            if isinstance(scale, V):
                rd.append(scale)
        if accum_out is not None:
            kw["accum_out"] = accum_out.ap
            wr.append(accum_out)
        self.s.op("act", lambda e: e.activation(out=out.ap, in_=in_.ap, func=func, **kw), rd, wr)

    def tt(self, eng, out, a, b, op):
        self.s.op(eng, lambda e: e.tensor_tensor(out=out.ap, in0=a.ap, in1=b.ap, op=op), [a, b], [out])

    def ts(self, eng, out, a, s1, s2=None, op0=ALU.mult, op1=None, accum_out=None):
        rd = [a] + [x for x in (s1, s2) if isinstance(x, V)]
        kw = {}
        if op1 is not None:
            kw["op1"] = op1
        wr = [out]
        if accum_out is not None:
            kw["accum_out"] = accum_out.ap
            wr.append(accum_out)
        self.s.op(eng, lambda e: e.tensor_scalar(out=out.ap, in0=a.ap, scalar1=self._apv(s1), scalar2=self._apv(s2),
                                                 op0=op0, **kw), rd, wr)

    def stt(self, eng, out, a, scalar, b, op0, op1):
        rd = [a, b] + ([scalar] if isinstance(scalar, V) else [])
        self.s.op(eng, lambda e: e.scalar_tensor_tensor(out=out.ap, in0=a.ap, scalar=self._apv(scalar), in1=b.ap,
                                                        op0=op0, op1=op1), rd, [out])

    def copy(self, eng, out, in_):
        if eng == "act":
            self.s.op("act", lambda e: e.copy(out=out.ap, in_=in_.ap), [in_], [out])
        else:
            self.s.op(eng, lambda e: e.tensor_copy(out=out.ap, in_=in_.ap), [in_], [out])

    def memset(self, eng, out, val):
        self.s.op(eng, lambda e: e.memset(out.ap, val), [], [out])

    def recip(self, out, in_):
        self.s.op("dve", lambda e: e.reciprocal(out=out.ap, in_=in_.ap), [in_], [out])

    def reduce_sum(self, out, in_):
        self.s.op("dve", lambda e: e.tensor_reduce(out=out.ap, in_=in_.ap, axis=AX.X, op=ALU.add), [in_], [out])

    def mm(self, out, lhsT, rhs, start, stop):
        self.s.op("pe", lambda e: e.matmul(out.ap, lhsT=lhsT.ap, rhs=rhs.ap, start=start, stop=stop),
                  [lhsT, rhs], [out])

    def tr(self, out, in_, ident):
        self.s.op("pe", lambda e: e.transpose(out=out.ap, in_=in_.ap, identity=ident.ap), [in_, ident], [out])

    def dma(self, q, out, in_):
        self.s.dma(q, out, in_)

    def cast_rr(self, out, in_):
        eng = ("act", "dve")[self.rr % 2]
        self.rr += 1
        self.copy(eng, out, in_)


def bc(v, shape, axis):
    return v.w(v.ap.unsqueeze(axis).broadcast_to(shape))


DEBUG = os.environ.get("MK_DEBUG", "")


def build(stop_after=99):
    nc = bass.Bass("TRN2", target_bir_lowering=False)
    k = K(nc)
    dbg = set(DEBUG.split(",")) if DEBUG else set()

    def din(name, shape, dt=F32):
        return nc.dram_tensor(name, list(shape), dt, kind="ExternalInput").ap()

    def dscr(name, shape, dt):
        kind = "ExternalOutput" if name in dbg else "Internal"
        return nc.dram_tensor(name, list(shape), dt, kind=kind).ap()

    xo_d = din("xo", [NTOK, D_MODEL])
    xp_d = din("xp", [NPRE, D_MODEL])
    pmask_d = din("pmask", [128, 1])
    ccol_d = din("c_col", [128, 16])
    wada_d = din("w_ada", [2048, 12288])
    bada_d = din("b_ada", [1, 12288])
    g1n_d = din("g1n", [128, 16])
    g2n_d = din("g2n", [128, 16])
    win_d = din("w_in", [2048, 4096])
    wglu_d = din("w_glu", [1024, 1024])
    wout_d = din("w_out", [2048, 2048])
    wff1_d = din("w_ff1", [2048, 8192])
    wff2_d = din("w_ff2", [8192, 2048])
    qg_d = din("qg", [128, 1])
    kg_d = din("kg", [128, 1])
    og_d = din("og", [128, 16])
    bglu_d = din("bglu", [128, 8])
    dsk_d = din("dsk", [128, 8])
    ident_d = din("ident", [128, 128])
    tri_d = din("tri", [128, 128])
    abias_d = din("abias", [128, 24, 256])
    sB_d = {n: din("sB_" + n, [128, 8, 64]) for n in ("lre", "lim", "lst", "bre", "bim")}
    sS_d = {n: din("sS_" + n, [128, 32]) for n in ("lre", "lim", "lst")}
    bsel_d = {n: din("bsel_" + n, [128, 32, 32]) for n in ("re", "im")}
    cblk_d = {n: din("cblk_" + n, [128, 32, 128]) for n in ("re", "im")}
    mask8_d = din("mask8", [128, 8])

    out_d = nc.dram_tensor("out", [NTOK, D_MODEL], F32, kind="ExternalOutput").ap()

    winb_d = dscr("winb", [2048, 4096], BF16)
    wglub_d = dscr("wglub", [1024, 1024], BF16)
    woutb_d = dscr("woutb", [2048, 2048], BF16)
    wff1b_d = dscr("wff1b", [2048, 8192], BF16)
    wff2b_d = dscr("wff2b", [8192, 2048], BF16)
    QT_d = dscr("QT", [8, 128, NTOK], BF16)
    KT_d = dscr("KT", [8, 128, NTOK + KVPRE], BF16)
    Vs_d = dscr("Vs", [NTOK + KVPRE, 1024], BF16)
    UT_d = dscr("UT", [8, 128, NTOK], F32)
    Upre_d = dscr("Upre", [NPRE, 1024], BF16)
    NUM_d = [dscr(f"NUM{p}", [NTOK, 8, 129], F32) for p in range(3)]
    YG_d = dscr("YG", [8, 128, NTOK], F32)
    modbc_dbg = dscr("modbc_dbg", [128, 12288], F32) if "modbc_dbg" in dbg else None

    A = Arena(nc, 190 * 1024)
    ps = [Tile(nc.alloc_psum_tensor(f"ps{i}", [128, 512], F32).ap(), "ps", i * 2048, [512], 4) for i in range(8)]

    def psv(bank, a, b):
        return ps[bank][a:b]

    ident = A.tile([128], F32)
    k.dma("sp", ident.full(), dr(ident_d, "ident"))
    identb = A.tile([128], BF16)
    k.copy("dve", identb.full(), ident.full())
    ones_f = A.tile([128], F32)
    k.memset("dve", ones_f.full(), 1.0)
    pm = A.tile([1], F32)
    k.dma("sp", pm.full(), dr(pmask_d, "pmask"))
    g1n = A.tile([16], F32)
    g2n = A.tile([16], F32)
    og = A.tile([16], F32)
    bglu = A.tile([8], F32)
    dsk = A.tile([8], F32)
    qgs = A.tile([1], F32)
    kgs = A.tile([1], F32)
    for t_, d_, n_ in ((g1n, g1n_d, "g1n"), (g2n, g2n_d, "g2n"), (og, og_d, "og"), (bglu, bglu_d, "bglu"),
                       (dsk, dsk_d, "dsk"), (qgs, qg_d, "qg"), (kgs, kg_d, "kg")):
        k.dma("sp", t_.full(), dr(d_, n_))
    k.ts("dve", qgs.full(), qgs.full(), 128.0 ** -0.5, op0=ALU.mult)
    m1 = A.tile([16], F32)
    sh1 = A.tile([16], F32)
    m2 = A.tile([16], F32)
    sh2 = A.tile([16], F32)

    def cast_dma0(src_d, dst_d, name, R, rows_per, after=()):
        for r0 in range(0, R, rows_per):
            k.s.dma("pool", dr(dst_d[r0:r0 + rows_per, :], name, r0, r0 + rows_per),
                    dr(src_d[r0:r0 + rows_per, :], name + "_src"), after=after)

    cast_dma0(win_d, winb_d, "winb", 2048, 256)

    gate = [dr(winb_d, "winb")]
    cast_dma0(wglu_d, wglub_d, "wglub", 1024, 256, after=gate)
    cast_dma0(wout_d, woutb_d, "woutb", 2048, 256, after=gate)
    cast_dma0(wff1_d, wff1b_d, "wff1b", 2048, 128, after=gate)
    cast_dma0(wff2_d, wff2b_d, "wff2b", 8192, 512, after=gate)
    g1bc_t = A.tile([2048], F32)
    g2bc_t = A.tile([2048], F32)
    wada_v = wada_d.rearrange("(kt p) c -> p kt c", p=128)
    A.mark()
    ccol = A.tile([16], F32)
    k.dma("sp", ccol.full(), dr(ccol_d, "ccol"))
    scv = A.tile([16], F32)
    k.act(scv.full(), ccol.full(), AF.Silu)
    rep = A.tile([16, 128], F32)
    k.copy("dve", rep.full(), bc(scv.full(), [128, 16, 128], 2))
    shsc2 = A.tile([4096], F32)
    wa1 = A.tile([16, 512], F32)
    A.mark()
    mod1 = A.tile([4096], F32)
    wa0 = A.tile([16, 512], F32)
    k.dma("sp", mod1.full(), dr(bada_d[:, 0:4096].partition_broadcast(128), "bada"))
    was = [wa0, wa1]
    for ch in range(8):
        buf = was[ch % 2]
        k.dma("sp", buf.full(), dr(wada_v[:, :, ch * 512:(ch + 1) * 512], "w_ada"))
        pb = psv(ch % 2, 0, 512)
        for kt in range(16):
            k.mm(pb, rep[kt], buf[kt], kt == 0, kt == 15)
        k.tt("dve", mod1[ch * 512:(ch + 1) * 512], pb, mod1[ch * 512:(ch + 1) * 512], ALU.add)

    def to_cols(srcrow, off, dst, bank0):
        for kq in range(4):
            bank = bank0 + kq % 2
            for j in range(4):
                kt = kq * 4 + j
                k.tr(psv(bank, j * 128, (j + 1) * 128), srcrow[off + kt * 128: off + (kt + 1) * 128], ident.full())
            src = ps[bank].full()
            src = src.w(src.ap.rearrange("p (a b) -> p a b", b=128)[:, :, 0])
            k.copy("dve", dst[kq * 4:(kq + 1) * 4], src)

    sc1 = A.tile([16], F32)
    to_cols(mod1, 0, sh1, 2)
    to_cols(mod1, 2048, sc1, 2)
    k.stt("dve", m1.full(), sc1.full(), 1.0, g1n.full(), ALU.add, ALU.mult)
    A.release()
    k.dma("sp", g1bc_t.full(), dr(bada_d[:, 4096:6144].partition_broadcast(128), "bada"))
    k.dma("sp", shsc2.full(), dr(bada_d[:, 6144:10240].partition_broadcast(128), "bada"))
    k.dma("sp", g2bc_t.full(), dr(bada_d[:, 10240:12288].partition_broadcast(128), "bada"))
    mod_todo = list(range(8, 24))

    def emit_mod_chunk():
        if not mod_todo:
            return
        ch = mod_todo.pop(0)
        k.dma("sp", wa1.full(), dr(wada_v[:, :, ch * 512:(ch + 1) * 512], "w_ada"))
        pb = psv(7, 0, 512)
        for kt in range(16):
            k.mm(pb, rep[kt], wa1[kt], kt == 0, kt == 15)
        if ch < 12:
            dst = g1bc_t[(ch - 8) * 512:(ch - 7) * 512]
        elif ch < 20:
            dst = shsc2[(ch - 12) * 512:(ch - 11) * 512]
        else:
            dst = g2bc_t[(ch - 20) * 512:(ch - 19) * 512]
        k.tt("dve", dst, pb, dst, ALU.add)

    A.mark()
    xin = [A.tile([2048], F32) for _ in range(2)]
    junk = A.tile([2048], BF16)
    hT = [A.tile([16, 512], BF16) for _ in range(2)]
    wblk = [A.tile([16, 512], BF16) for _ in range(3)]
    sqt = [A.tile([512], F32) for _ in range(2)]
    rst = [A.tile([512], F32) for _ in range(2)]
    obt = [A.tile([512], BF16) for _ in range(3)]
    oft = [A.tile([512], F32) for _ in range(2)]
    stat = [A.tile([8], F32) for _ in range(2)]
    winb_v = winb_d.rearrange("(kt p) c -> p kt c", p=128)
    ctr = {"w": 0, "p": 0, "s": 0, "o": 0, "f": 0}
    n_tiles1 = 16 if stop_after >= 2 else int(os.environ.get("MK_P1_TILES", "16"))
    tl1 = os.environ.get("MK_P1_LIST", "")
    tiles1 = [int(t_) for t_ in tl1.split(",")] if tl1 else list(range(n_tiles1))
    pending = []

    def flush_pending():
        while pending:
            pending.pop(0)()

    def emit_norm(ti, sub):
        is_pre = ti < 8
        tok0 = (ti % 8) * TT
        src = xp_d if is_pre else xo_d
        h = hT[ti % 2]
        stt_ = stat[ti % 2]
        if sub == 0:
            k.memset("dve", stt_.full(), 0.0)
        xt = xin[(ti * 4 + sub) % 2]
        k.dma("sp", xt.full(), dr(src[tok0 + sub * 128: tok0 + (sub + 1) * 128, :], "x"))
        ss = stt_[sub:sub + 1]
        rs = stt_[4 + sub:5 + sub]
        k.act(junk.full(), xt.full(), AF.Square, accum_out=ss)
        k.act(rs, ss, AF.Sqrt, scale=1.0 / D_MODEL, bias=EPS)
        k.recip(rs, rs)
        k.ts("dve", xt.full(), xt.full(), rs, op0=ALU.mult)
        for kq in range(4):
            bank = kq % 2
            for j in range(4):
                kt = kq * 4 + j
                k.tr(psv(bank, j * 128, (j + 1) * 128), xt[kt * 128:(kt + 1) * 128], ident.full())
            for j in range(4):
                kt = kq * 4 + j
                k.act(h[kt, sub * 128:(sub + 1) * 128], psv(bank, j * 128, (j + 1) * 128), AF.Identity,
                      scale=m1[kt:kt + 1], bias=sh1[kt:kt + 1])

    for sub in range(4):
        emit_norm(tiles1[0], sub)
    for tidx, ti in enumerate(tiles1):
        emit_mod_chunk()
        is_pre = ti < 8
        tok0 = (ti % 8) * TT
        h = hT[ti % 2]
        nxt = tiles1[tidx + 1] if tidx + 1 < len(tiles1) else None
        nsub = [0]
        blocks = []
        if not is_pre:
            blocks += [("q", 0), ("q", 1)]
        if ti >= 4:
            blocks += [("k", 2), ("k", 3), ("v", 4), ("v", 5)]
        blocks += [("u", 6), ("u", 7)]
        w0 = (ti - 4) * TT
        for kind, cb in blocks:
            wb = wblk[ctr["w"] % 3]
            ctr["w"] += 1
            k.dma("sp", wb.full(), dr(winb_v[:, :, cb * 512:(cb + 1) * 512], "winb"))
            feature_major = kind in ("q", "k") or (kind == "u" and not is_pre)
            if feature_major:
                for co in range(4):
                    pb = psv(2 + ctr["p"] % 3, 0, 512)
                    ctr["p"] += 1
                    for kt in range(16):
                        k.mm(pb, wb[kt, co * 128:(co + 1) * 128], h[kt], kt == 0, kt == 15)
                    if kind in ("q", "k"):
                        sq = sqt[ctr["s"] % 2].full()
                        rs = rst[ctr["s"] % 2].full()
                        pss = psv(5 + ctr["s"] % 2, 0, 512)
                        ctr["s"] += 1
                        k.act(sq, pb, AF.Square)
                        flush_pending()

                        def fin(sq=sq, rs=rs, pss=pss, pb=pb, kind=kind, cb=cb, co=co, tok0=tok0, w0=w0):
                            k.mm(pss, ones_f.full(), sq, True, True)
                            k.act(rs, pss, AF.Sqrt, scale=1.0 / 128, bias=EPS)
                            k.recip(rs, rs)
                            ob = obt[ctr["o"] % 3].full()
                            ctr["o"] += 1
                            gv = qgs if kind == "q" else kgs
                            k.stt("dve", ob, pb, gv.full(), rs, ALU.mult, ALU.mult)
                            head = (cb % 2) * 4 + co
                            if kind == "q":
                                k.dma("pool", dr(QT_d[head, :, tok0:tok0 + TT], "QT", tok0, tok0 + TT), ob)
                            else:
                                k.dma("pool", dr(KT_d[head, :, w0:w0 + TT], "KT", w0, w0 + TT), ob)
                        pending.append(fin)
                    else:
                        flush_pending()
                        of = oft[ctr["f"] % 2].full()
                        ctr["f"] += 1
                        k.copy("act", of, pb)
                        ut = (cb - 6) * 4 + co
                        k.dma("pool", dr(UT_d[ut, :, tok0:tok0 + TT], "UT", tok0, tok0 + TT), of)
            else:
                for sub in range(4):
                    pb = psv(2 + ctr["p"] % 3, 0, 512)
                    ctr["p"] += 1
                    for kt in range(16):
                        k.mm(pb, h[kt, sub * 128:(sub + 1) * 128], wb[kt], kt == 0, kt == 15)
                    flush_pending()
                    ob = obt[ctr["o"] % 3].full()
                    ctr["o"] += 1
                    if kind == "v":
                        if is_pre:
                            k.ts("dve", ob, pb, pm.full(), op0=ALU.mult)
                        else:
                            k.copy("act", ob, pb)
                        r0 = w0 + sub * 128
                        k.dma("pool", dr(Vs_d[r0:r0 + 128, (cb - 4) * 512:(cb - 3) * 512], "Vs", r0, r0 + 128), ob)
                    else:
                        k.copy("act", ob, pb)
                        r0 = tok0 + sub * 128
                        k.dma("pool", dr(Upre_d[r0:r0 + 128, (cb - 6) * 512:(cb - 5) * 512], "Upre", r0, r0 + 128), ob)
            if nxt is not None and nsub[0] < 4 and (len(blocks) - blocks.index((kind, cb)) <= 4 or len(blocks) <= 4):
                emit_norm(nxt, nsub[0])
                nsub[0] += 1
        while nxt is not None and nsub[0] < 4:
            emit_norm(nxt, nsub[0])
            nsub[0] += 1
    flush_pending()
    A.release()
    while mod_todo:
        emit_mod_chunk()
    sc2 = A.tile([16], F32)
    to_cols(shsc2, 0, sh2, 2)
    to_cols(shsc2, 2048, sc2, 2)
    k.stt("dve", m2.full(), sc2.full(), 1.0, g2n.full(), ALU.add, ALU.mult)
    A.release()
    if stop_after <= 1:
        k.s.barrier()
        k.s.emit()
        return nc
    return build_rest(nc, k, A, ps, psv, locals())


def _consts():
    ident = np.eye(128, dtype=np.float32)
    tri = np.triu(np.ones((128, 128), np.float32))
    kk = np.arange(128)[:, None]
    qq = np.arange(128)[None, :]
    ab = np.full((128, 3, 8, 2, 128), -30000.0, np.float32)
    for pi, d in enumerate((1, 4, 16)):
        for h in range(8):
            slope = 2.0 ** (-8.0 * (h + 1.0) / 8.0)
            st0 = qq - kk + 128
            b0 = np.where(kk >= qq, -slope * st0 * d, -30000.0)
            st1 = qq - kk
            b1 = np.where(kk <= qq, -slope * st1 * d, -30000.0)
            ab[:, pi, h, 0, :] = b0
            ab[:, pi, h, 1, :] = b1
    mask8 = np.zeros((8, 16, 8), np.float32)
    for g in range(8):
        mask8[g, :, g] = 1.0
    return ident, tri, ab.reshape(128, 24, 256), mask8.reshape(128, 8)


def prep_inputs(inputs):
    L = 0
    f = lambda a: np.ascontiguousarray(np.asarray(a, dtype=np.float32))
    x = np.asarray(inputs["x"], dtype=np.float32)
    c = np.asarray(inputs["c"], dtype=np.float32)
    col16 = lambda v: f(np.asarray(v).reshape(16, 128).T)
    col8 = lambda v: f(np.asarray(v).reshape(8, 128).T)
    ident, tri, abias, mask8 = _consts()
    lam_re = np.asarray(inputs["lam_re"][L], np.float32)
    lam_im = np.asarray(inputs["lam_im"][L], np.float32)
    lst = np.asarray(inputs["log_step"][L], np.float32)
    b_re = np.asarray(inputs["b_re"][L], np.float32)
    b_im = np.asarray(inputs["b_im"][L], np.float32)
    c_re = np.asarray(inputs["c_re"][L], np.float32)
    c_im = np.asarray(inputs["c_im"][L], np.float32)

    def layB(a):
        t = a.reshape(8, 8, 64).transpose(1, 0, 2)
        return f(np.broadcast_to(t[:, None], (8, 16, 8, 64)).reshape(128, 8, 64))

    def layS(a):
        return f(a.reshape(32, 2, 64).transpose(1, 2, 0).reshape(128, 32))

    lstB = np.broadcast_to(lst[:, None], (64, 64))
    shared = {
        "w_ada": f(inputs["w_ada"][L]), "b_ada": f(np.asarray(inputs["b_ada"][L])[None, :]),
        "g1n": col16(inputs["norm1_g"][L]), "g2n": col16(inputs["norm2_g"][L]),
        "w_in": f(inputs["w_in"][L]), "w_glu": f(inputs["w_glu"][L]), "w_out": f(inputs["w_out"][L]),
        "w_ff1": f(inputs["w_ff1"][L]), "w_ff2": f(inputs["w_ff2"][L]),
        "qg": f(np.asarray(inputs["q_norm_g"][L]).reshape(128, 1)),
        "kg": f(np.asarray(inputs["k_norm_g"][L]).reshape(128, 1)),
        "og": col16(np.concatenate([np.asarray(inputs["attn_out_g"][L]), np.asarray(inputs["ssm_out_g"][L])])),
        "bglu": col8(inputs["b_glu"][L]), "dsk": col8(inputs["d_skip"][L]),
        "ident": ident, "tri": tri, "abias": abias, "mask8": mask8,
        "sB_lre": layB(lam_re), "sB_lim": layB(lam_im), "sB_lst": layB(lstB),
        "sB_bre": f(b_re.reshape(8, 8, 64, 16).transpose(1, 3, 0, 2).reshape(128, 8, 64)),
        "sB_bim": f(b_im.reshape(8, 8, 64, 16).transpose(1, 3, 0, 2).reshape(128, 8, 64)),
        "sS_lre": layS(lam_re), "sS_lim": layS(lam_im), "sS_lst": layS(lstB),
    }
    for nm, b in (("re", b_re), ("im", b_im)):
        arr = np.zeros((2, 64, 32, 2, 16), np.float32)
        br = b.reshape(32, 2, 64, 16)
        for gh in range(2):
            arr[gh, :, :, gh, :] = br[:, gh].transpose(1, 0, 2)
        shared["bsel_" + nm] = arr.reshape(128, 32, 32)
    for nm, cm in (("re", c_re), ("im", c_im)):
        arr = np.zeros((2, 64, 32, 8, 16), np.float32)
        for g in range(64):
            arr[g % 2, :, g // 2, g % 8, :] = cm[g].T
        shared["cblk_" + nm] = arr.reshape(128, 32, 128)
    in_maps = []
    zeros_pre = np.zeros((NPRE, D_MODEL), np.float32)
    for core in range(8):
        b, half = divmod(core, 2)
        m = dict(shared)
        m["xo"] = f(x[b, half * NTOK:(half + 1) * NTOK])
        m["xp"] = f(x[b, 0:NPRE]) if half == 1 else zeros_pre
        m["pmask"] = np.full((128, 1), float(half), np.float32)
        m["c_col"] = f(c[b].reshape(16, 128).T)
        in_maps.append(m)
    return in_maps


def kernel(**inputs):
    in_maps = prep_inputs(inputs)
    nc = build()
    res = run_bass_kernel_spmd(nc, in_maps, core_ids=list(range(8)))
    out = np.empty((4, SEQ, D_MODEL), np.float32)
    for core in range(8):
        b, half = divmod(core, 2)
        out[b, half * NTOK:(half + 1) * NTOK] = res.results[core]["out"]
    return out


def build_rest(nc, k, A, ps, psv, L):
    stop_after = L["stop_after"]
    ident, identb, ones_f, pm = L["ident"], L["identb"], L["ones_f"], L["pm"]
    m2, sh2, bglu, dsk = L["m2"], L["sh2"], L["bglu"], L["dsk"]
    QT_d, KT_d, Vs_d, UT_d, Upre_d, NUM_d, YG_d = (L[n] for n in ("QT_d", "KT_d", "Vs_d", "UT_d", "Upre_d", "NUM_d", "YG_d"))
    xo_d, out_d = L["xo_d"], L["out_d"]
    skip = set(os.environ.get("MK_SKIP", "").split(","))

    def finish():
        k.s.barrier()
        k.s.emit()
        return nc

    A.mark()
    if "attn" not in skip:
        abias = A.tile([24, 256], F32)
        k.dma("sp", abias.full(), dr(L["abias_d"], "abias"))
        KTw = [A.tile([4096], BF16) for _ in range(2)]
        QTs = [A.tile([2048], BF16) for _ in range(2)]
        V1 = [A.tile([17, 129], BF16) for _ in range(2)]
        V4 = [A.tile([5, 4, 129], BF16) for _ in range(2)]
        V16 = [A.tile([2, 16, 129], BF16) for _ in range(2)]
        ssb = [A.tile([256], F32) for _ in range(4)]
        pbf = [A.tile([256], BF16) for _ in range(4)]
        stg = [A.tile([16, 129], F32) for _ in range(2)]
        cu = {"u": 0, "g": 0}
        n_span = int(os.environ.get("MK_SPANS", "2"))
        n_head = int(os.environ.get("MK_HEADS", "8"))
        for sp in range(n_span):
            S0 = sp * 2048
            for h in range(n_head):
                bi = (sp * 8 + h) % 2
                kw_, qs_, v1, v4, v16 = KTw[bi], QTs[bi], V1[bi], V4[bi], V16[bi]
                hc = slice(h * 128, (h + 1) * 128)
                k.dma("sp", kw_.full(), dr(KT_d[h, :, S0:S0 + 4096], "KT", S0, S0 + 4096))
                k.dma("sp", qs_.full(), dr(QT_d[h, :, S0:S0 + 2048], "QT", S0, S0 + 2048))
                wb1 = S0 + 2048 - 128
                src1 = Vs_d[wb1:wb1 + 17 * 128, hc].rearrange("(b p) e -> p b e", p=128)
                k.dma("sp", v1[0:9, 0:128], dr(src1[:, 0:9, :], "Vs", wb1, wb1 + 9 * 128))
                k.dma("sp", v1[9:17, 0:128], dr(src1[:, 9:17, :], "Vs", wb1 + 9 * 128, wb1 + 17 * 128))
                wb4 = S0 + 2048 - 512
                src4 = Vs_d[wb4:wb4 + 2560, hc].rearrange("(j m r) e -> m j r e", j=5, m=128, r=4)
                for j4 in range(5):
                    k.dma("sp", v4[j4, :, 0:128], dr(src4[:, j4], "Vs", wb4 + 512 * j4, wb4 + 512 * (j4 + 1)))
                src16 = Vs_d[S0:S0 + 4096, hc].rearrange("(j m r) e -> m j r e", j=2, m=128, r=16)
                for jj in range(2):
                    k.dma("sp", v16[jj, :, 0:128], dr(src16[:, jj], "Vs", S0 + jj * 2048, S0 + (jj + 1) * 2048))
                k.memset("dve", v1[:, 128:129], 1.0)
                k.memset("dve", v4[:, :, 128:129], 1.0)
                k.memset("dve", v16[:, :, 128:129], 1.0)
                if sp == 0:
                    k.copy("dve", v1[0:1, 128:129], bc(pm.full(), [128, 1, 1], 1))
                    k.copy("dve", v4[0, :, 128:129], bc(pm.full(), [128, 4, 1], 1))
                    k.copy("dve", v16[0, :, 128:129], bc(pm.full(), [128, 16, 1], 1))
                for pi in range(3):
                    sg = stg[cu["g"] % 2]
                    cu["g"] += 1

                    def operands(u, pi=pi):
                        if pi == 0:
                            return (qs_[128 * u:128 * (u + 1)], kw_[1920 + 128 * u:2048 + 128 * u],
                                    kw_[2048 + 128 * u:2176 + 128 * u], v1[u], v1[u + 1])
                        if pi == 1:
                            jq, r = divmod(u, 4)
                            return (qs_[slice(512 * jq + r, 512 * (jq + 1), 4)],
                                    kw_[slice(1536 + 512 * jq + r, 2048 + 512 * jq, 4)],
                                    kw_[slice(2048 + 512 * jq + r, 2560 + 512 * jq, 4)], v4[jq, r], v4[jq + 1, r])
                        r = u
                        return (qs_[slice(r, 2048, 16)], kw_[slice(r, 2048, 16)], kw_[slice(2048 + r, 4096, 16)],
                                v16[0, r], v16[1, r])

                    base = cu["u"]
                    cu["u"] += 16

                    def stageA(u, pi=pi, base=base):
                        q, kp, kc, vp, vc = operands(u)
                        i = base + u
                        pS = ps[i % 4]
                        k.mm(pS[0:128], kp, q, True, True)
                        k.mm(pS[128:256], kc, q, True, True)
                        sb_ = ssb[i % 4].full()
                        k.tt("dve", sb_, pS[0:256], abias[pi * 8 + h], ALU.add)
                        k.act(pbf[i % 4].full(), sb_, AF.Exp)

                    def stageB(u, sg=sg, base=base):
                        q, kp, kc, vp, vc = operands(u)
                        i = base + u
                        pO = ps[4 + i % 4]
                        pb_ = pbf[i % 4]
                        k.mm(pO[0:129], pb_[0:128], vp, True, False)
                        k.mm(pO[0:129], pb_[128:256], vc, False, True)
                        k.copy("act", sg[u], pO[0:129])

                    SK = 2
                    for step in range(16 + SK):
                        if step < 16:
                            stageA(step)
                        if step - SK >= 0:
                            stageB(step - SK)
                    nd = NUM_d[pi][S0:S0 + 2048, h, :]
                    if pi == 1:
                        dst = nd.rearrange("(j m r) e -> m j r e", j=4, m=128, r=4)
                        for j4 in range(4):
                            k.dma("pool", dr(dst[:, j4], f"NUM{pi}", S0 + 512 * j4, S0 + 512 * (j4 + 1)),
                                  sg[4 * j4:4 * (j4 + 1)])
                    else:
                        dst = nd.rearrange("(b m) e -> m b e", m=128) if pi == 0 else nd.rearrange("(m r) e -> m r e", r=16)
                        k.dma("pool", dr(dst, f"NUM{pi}", S0, S0 + 2048), sg.full())
    A.release()
    if stop_after <= 2:
        return finish()

    A.mark()
    if "ssm" not in skip:
        build_ssm(k, A, ps, L)
    A.release()
    if stop_after <= 3:
        return finish()
    build_phase3(k, A, ps, L)
    return finish()


def build_ssm(k, A, ps, L):
    pm, dsk = L["pm"], L["dsk"]
    UT_d, Upre_d, YG_d = L["UT_d"], L["Upre_d"], L["YG_d"]
    sB_d, sS_d, bsel_d, cblk_d = L["sB_d"], L["sS_d"], L["bsel_d"], L["cblk_d"]
    MUL, ADD, SUB = ALU.mult, ALU.add, ALU.subtract

    def load(shape, d_ap, name):
        t = A.tile(shape, F32)
        k.dma("sp", t.full(), dr(d_ap, name))
        return t

    def cpow(lre, lim, lst, fs, sign):
        T = lambda: A.tile(fs, F32)
        step, lr, li, m, sn, cs, zr, zi, t1, t2 = (T() for _ in range(10))
        k.act(step.full(), lst.full(), AF.Exp)
        k.tt("dve", lr.full(), lre.full(), step.full(), MUL)
        k.tt("dve", li.full(), lim.full(), step.full(), MUL)
        k.act(m.full(), lr.full(), AF.Exp, scale=sign / 16.0)
        k.act(sn.full(), li.full(), AF.Sin, scale=sign / 16.0)
        k.act(cs.full(), li.full(), AF.Sin, scale=-1.0 / 16.0, bias=math.pi / 2)
        k.tt("dve", zr.full(), m.full(), cs.full(), MUL)
        k.tt("dve", zi.full(), m.full(), sn.full(), MUL)
        for _ in range(4):
            k.tt("dve", t1.full(), zr.full(), zr.full(), MUL)
            k.tt("dve", t2.full(), zi.full(), zi.full(), MUL)
            k.stt("dve", zi.full(), zr.full(), 2.0, zi.full(), MUL, MUL)
            k.tt("dve", zr.full(), t1.full(), t2.full(), SUB)
        return zr, zi, lr, li

    def cmul(outr, outi, ar, ai, br, bi, t1, t2):
        k.tt("dve", t1, ar, br, MUL)
        k.tt("dve", t2, ai, bi, MUL)
        k.tt("dve", outr, t1, t2, SUB)
        k.tt("dve", t1, ar, bi, MUL)
        k.tt("dve", t2, ai, br, MUL)
        k.tt("dve", outi, t1, t2, ADD)

    def kappa(ar, ai, lre, lim, lst, fs):
        T = lambda: A.tile(fs, F32)
        am1, den, t1, t2, kr, ki = (T() for _ in range(6))
        k.ts("dve", am1.full(), ar.full(), -1.0, op0=ADD)
        k.tt("dve", t1.full(), lre.full(), lre.full(), MUL)
        k.tt("dve", t2.full(), lim.full(), lim.full(), MUL)
        k.tt("dve", den.full(), t1.full(), t2.full(), ADD)
        k.recip(den.full(), den.full())
        k.tt("dve", t1.full(), am1.full(), lre.full(), MUL)
        k.tt("dve", t2.full(), ai.full(), lim.full(), MUL)
        k.tt("dve", kr.full(), t1.full(), t2.full(), ADD)
        k.tt("dve", kr.full(), kr.full(), den.full(), MUL)
        k.tt("dve", t1.full(), ai.full(), lre.full(), MUL)
        k.tt("dve", t2.full(), am1.full(), lim.full(), MUL)
        k.tt("dve", ki.full(), t1.full(), t2.full(), SUB)
        k.tt("dve", ki.full(), ki.full(), den.full(), MUL)
        return kr, ki

    Bblk = {n: A.tile([8, 8, 64], BF16) for n in ("re", "im")}
    Cb = {n: A.tile([32, 128], BF16) for n in ("re", "imn")}
    def subtile(par, i0):
        fs = par.free_shape[1:]
        n_ = 1
        for d_ in fs:
            n_ *= d_
        return Tile(par.ap[(slice(None), i0)], par.space, par.lo + i0 * n_ * par.esz, fs, par.esz)

    Tpos2 = A.tile([2, 32, 128], F32)
    Tpos = {"re": subtile(Tpos2, 0), "im": subtile(Tpos2, 1)}
    TinvT2 = A.tile([2, 4096], F32)
    TinvT = {"re": subtile(TinvT2, 0), "im": subtile(TinvT2, 1)}
    a128 = {n: A.tile([32], F32) for n in ("re", "im")}
    trib = A.tile([128], BF16)
    cr = A.tile([32], F32)
    ci = A.tile([32], F32)
    Gend = {n: A.tile([32], F32) for n in ("re", "im")}
    sr_ = A.tile([32], F32)
    si_ = A.tile([32], F32)
    tA = A.tile([32], F32)
    tB = A.tile([32], F32)
    A.mark()
    TinvTb = {n: A.tile([4096], BF16) for n in ("re", "im")}
    Bsel = {n: A.tile([32, 32], F32) for n in ("re", "im")}

    A.mark()
    fsB = [8, 64]
    lre, lim, lst = (load(fsB, sB_d[n], "sB" + n) for n in ("lre", "lim", "lst"))
    bre, bim = load(fsB, sB_d["bre"], "sBbre"), load(fsB, sB_d["bim"], "sBbim")
    mask8 = load([8], L["mask8_d"], "mask8")
    ar, ai, _, _ = cpow(lre, lim, lst, fsB, 1.0)
    kr, ki = kappa(ar, ai, lre, lim, lst, fsB)
    BbR, BbI, t1, t2 = (A.tile(fsB, F32) for _ in range(4))
    cmul(BbR.full(), BbI.full(), kr.full(), ki.full(), bre.full(), bim.full(), t1.full(), t2.full())
    mb = mask8.full().w(mask8.ap.unsqueeze(1).unsqueeze(3).broadcast_to([128, 8, 8, 64]))
    for n, src in (("re", BbR), ("im", BbI)):
        sv = src.full().w(src.ap.unsqueeze(2).broadcast_to([128, 8, 8, 64]))
        k.tt("dve", Bblk[n].full(), sv, mb, MUL)
    A.release()

    ssm_stop = int(os.environ.get("MK_SSM_STOP", "9"))
    if ssm_stop <= 1:
        return
    A.mark()
    fsS = [32]
    lre, lim, lst = (load(fsS, sS_d[n], "sS" + n) for n in ("lre", "lim", "lst"))
    ar, ai, _, _ = cpow(lre, lim, lst, fsS, 1.0)
    ir, ii, _, _ = cpow(lre, lim, lst, fsS, -1.0)
    kr, ki = kappa(ar, ai, lre, lim, lst, fsS)
    sub_stop = int(os.environ.get("MK_SSM_SUB", "99"))
    if sub_stop <= 1:
        A.release()
        return
    A.mark()
    bsr = load([32, 32], bsel_d["re"], "bselre")
    bsi = load([32, 32], bsel_d["im"], "bselim")
    t1 = A.tile([32, 32], F32)
    t2 = A.tile([32, 32], F32)
    krb = kr.full().w(kr.ap.unsqueeze(2).broadcast_to([128, 32, 32]))
    kib = ki.full().w(ki.ap.unsqueeze(2).broadcast_to([128, 32, 32]))
    cmul(Bsel["re"].full(), Bsel["im"].full(), krb, kib, bsr.full(), bsi.full(), t1.full(), t2.full())
    A.release()
    if sub_stop <= 2:
        A.release()
        return
    A.mark()
    ctmp = A.tile([32, 128], F32)
    k.dma("sp", ctmp.full(), dr(cblk_d["re"], "cblkre"))
    k.copy("act", Cb["re"].full(), ctmp.full())
    ctmp2 = A.tile([32, 128], F32)
    k.dma("sp", ctmp2.full(), dr(cblk_d["im"], "cblkim"))
    k.act(Cb["imn"].full(), ctmp2.full(), AF.Identity, scale=-1.0)
    A.release()
    trf = A.tile([128], F32)
    k.dma("sp", trf.full(), dr(L["tri_d"], "tri"))
    k.copy("dve", trib.full(), trf.full())
    if sub_stop <= 3:
        A.release()
        return

    def table(dst_r, dst_i, br_, bi_, want_p128=None):
        A.mark()
        pr = A.tile([32], F32)
        pi_ = A.tile([32], F32)
        q1 = A.tile([32], F32)
        q2 = A.tile([32], F32)
        x1 = Tile(A.ap[:, TinvT["re"].lo // 4:(TinvT["re"].lo + 8192) // 4].rearrange("p (a b) -> p a b", b=64),
                  "sb", TinvT["re"].lo, [32, 64], 4)
        x2 = Tile(A.ap[:, (TinvT["re"].lo + 8192) // 4:(TinvT["re"].lo + 16384) // 4].rearrange("p (a b) -> p a b", b=64),
                  "sb", TinvT["re"].lo + 8192, [32, 64], 4)
        k.memset("dve", dst_r[:, 0:1], 1.0)
        k.memset("dve", dst_i[:, 0:1], 0.0)
        k.copy("dve", dst_r[:, 1:2], br_.full().w(br_.ap.unsqueeze(2)))
        k.copy("dve", dst_i[:, 1:2], bi_.full().w(bi_.ap.unsqueeze(2)))
        k.copy("dve", pr.full(), br_.full())
        k.copy("dve", pi_.full(), bi_.full())
        for j in range(1, int(os.environ.get("MK_TAB_J", "8"))):
            k.tt("dve", q1.full(), pr.full(), pr.full(), MUL)
            k.tt("dve", q2.full(), pi_.full(), pi_.full(), MUL)
            k.stt("dve", pi_.full(), pr.full(), 2.0, pi_.full(), MUL, MUL)
            k.tt("dve", pr.full(), q1.full(), q2.full(), SUB)
            if j == 7:
                break
            n = 1 << j
            prb = pr.full().w(pr.ap.unsqueeze(2).broadcast_to([128, 32, n]))
            pib = pi_.full().w(pi_.ap.unsqueeze(2).broadcast_to([128, 32, n]))
            cmul(dst_r[:, n:2 * n], dst_i[:, n:2 * n], dst_r[:, 0:n], dst_i[:, 0:n], prb, pib,
                 x1[:, 0:n], x2[:, 0:n])
        if want_p128 is not None:
            k.copy("dve", want_p128[0].full(), pr.full())
            k.copy("dve", want_p128[1].full(), pi_.full())
        A.release()

    table(Tpos["re"], Tpos["im"], ar, ai, want_p128=(a128["re"], a128["im"]))
    if sub_stop <= 4:
        A.release()
        return
    TiS = {n: A.tile([32, 128], F32) for n in ("re", "im")}
    table(TiS["re"], TiS["im"], ir, ii)
    if sub_stop <= 5:
        A.release()
        return
    cnt = 0
    for n in ("re", "im"):
        for g4 in range(8):
            bank = cnt % 2
            cnt += 1
            for j in range(4):
                gp = g4 * 4 + j
                k.tr(ps[bank][j * 128:(j + 1) * 128], TiS[n][gp], L["ident"].full())
            k.copy("dve", TinvT[n][g4 * 512:(g4 + 1) * 512], ps[bank][0:512])
            k.copy("act", TinvTb[n][g4 * 512:(g4 + 1) * 512], TinvT[n][g4 * 512:(g4 + 1) * 512])
    A.release()

    def carry_update():
        k.tt("dve", sr_.full(), Gend["re"].full(), cr.full(), ADD)
        k.tt("dve", si_.full(), Gend["im"].full(), ci.full(), ADD)
        cmul(cr.full(), ci.full(), a128["re"].full(), a128["im"].full(), sr_.full(), si_.full(), tA.full(), tB.full())

    k.memset("dve", cr.full(), 0.0)
    k.memset("dve", ci.full(), 0.0)
    if ssm_stop <= 2:
        A.release()
        return

    A.mark()
    upb = [A.tile([1024], BF16) for _ in range(2)]
    w1 = [A.tile([512], F32) for _ in range(2)]
    w2 = [A.tile([512], F32) for _ in range(2)]
    n_pre = int(os.environ.get("MK_NPRE", "32"))
    for n in range(32 - n_pre, 32):
        up = upb[n % 2]
        k.dma("sp", up.full(), dr(Upre_d[n * 128:(n + 1) * 128, :], "Upre", n * 128, (n + 1) * 128))
        pso = (n % 2) * 4
        for gp in range(32):
            hf, c0 = divmod(gp, 16)
            k.mm(ps[pso + hf][c0 * 32:(c0 + 1) * 32], TinvTb["re"][gp * 128:(gp + 1) * 128], up[gp * 32:(gp + 1) * 32], True, True)
            k.mm(ps[pso + 2 + hf][c0 * 32:(c0 + 1) * 32], TinvTb["im"][gp * 128:(gp + 1) * 128], up[gp * 32:(gp + 1) * 32], True, True)
        for hf in range(2):
            Mre, Mim = ps[pso + hf][0:512], ps[pso + 2 + hf][0:512]
            bsr_ = Bsel["re"][hf * 16:(hf + 1) * 16]
            bsr_ = bsr_.w(bsr_.ap.rearrange("p a b -> p (a b)"))
            bsi_ = Bsel["im"][hf * 16:(hf + 1) * 16]
            bsi_ = bsi_.w(bsi_.ap.rearrange("p a b -> p (a b)"))
            x1, x2 = w1[hf].full(), w2[hf].full()
            k.tt("dve", x1, Mre, bsr_, MUL)
            k.tt("dve", x2, Mim, bsi_, MUL)
            k.tt("pool", x1, x1, x2, SUB)
            k.reduce_sum(Gend["re"][hf * 16:(hf + 1) * 16], x1.w(x1.ap.rearrange("p (a b) -> p a b", b=32)))
            k.tt("dve", x1, Mim, bsr_, MUL)
            k.tt("dve", x2, Mre, bsi_, MUL)
            k.tt("pool", x1, x1, x2, ADD)
            k.reduce_sum(Gend["im"][hf * 16:(hf + 1) * 16], x1.w(x1.ap.rearrange("p (a b) -> p a b", b=32)))
        carry_update()
    A.release()
    A.release()
    k.ts("dve", cr.full(), cr.full(), pm.full(), op0=MUL)
    k.ts("dve", ci.full(), ci.full(), pm.full(), op0=MUL)

    if ssm_stop <= 3:
        return
    A.mark()
    uTf = [A.tile([8, 128], F32) for _ in range(2)]
    ub = [A.tile([8, 128], BF16) for _ in range(2)]
    bpp = {n: [A.tile([512], BF16) for _ in range(2)] for n in ("re", "im")}
    f1 = [A.tile([512], F32) for _ in range(2)]
    f2 = [A.tile([512], F32) for _ in range(2)]
    hT = {n: [A.tile([4, 128], BF16) for _ in range(2)] for n in ("re", "im")}
    ytmp = [A.tile([128], F32) for _ in range(2)]
    ygt = [A.tile([8, 128], F32) for _ in range(2)]
    UT_v = UT_d.rearrange("k p t -> p k t")
    YG_v = YG_d.rearrange("k p t -> p k t")
    n_own = int(os.environ.get("MK_NOWN", "32"))
    cr2 = [cr, A.tile([32], F32)]
    ci2 = [ci, A.tile([32], F32)]

    def carry_update2(n):
        k.tt("dve", sr_.full(), Gend["re"].full(), cr2[n % 2].full(), ADD)
        k.tt("dve", si_.full(), Gend["im"].full(), ci2[n % 2].full(), ADD)
        cmul(cr2[(n + 1) % 2].full(), ci2[(n + 1) % 2].full(), a128["re"].full(), a128["im"].full(),
             sr_.full(), si_.full(), tA.full(), tB.full())

    def S1(it):
        n, kt = divmod(it, 8)
        if kt == 0:
            k.dma("sp", uTf[n % 2].full(), dr(UT_v[:, :, n * 128:(n + 1) * 128], "UT", n * 128, (n + 1) * 128))
            k.copy("act", ub[n % 2].full(), uTf[n % 2].full())
        u_b = ub[n % 2]
        k.mm(ps[0][0:512], u_b[kt], Bblk["re"][kt].w(Bblk["re"].ap[:, kt].rearrange("p a b -> p (a b)")), True, True)
        k.mm(ps[1][0:512], u_b[kt], Bblk["im"][kt].w(Bblk["im"].ap[:, kt].rearrange("p a b -> p (a b)")), True, True)

    def S2(it):
        n, kt = divmod(it, 8)
        i2 = it % 2
        fsl = slice(kt * 512, (kt + 1) * 512)
        xa, xb = f12[0], f12[1]
        tt2 = TinvT2[:, fsl]
        bre_ = ps[0][0:512]
        bim_ = ps[1][0:512]
        k.tt("dve", xa.full(), bre_.w(bre_.ap.unsqueeze(1).broadcast_to([128, 2, 512])), tt2, MUL)
        k.tt("dve", xb.full(), bim_.w(bim_.ap.unsqueeze(1).broadcast_to([128, 2, 512])), tt2, MUL)
        k.tt("pool", bpp["re"][i2].full(), xa[0], xb[1], SUB)
        k.tt("pool", bpp["im"][i2].full(), xa[1], xb[0], ADD)

    def S3(it):
        n, kt = divmod(it, 8)
        i2 = it % 2
        pgr, pgi = ps[2 + 2 * i2], ps[3 + 2 * i2]
        br_, bi_ = bpp["re"][i2], bpp["im"][i2]
        for gq in range(4):
            k.mm(pgr[gq * 128:(gq + 1) * 128], br_[gq * 128:(gq + 1) * 128], trib.full(), True, True)
        for gq in range(4):
            k.mm(pgi[gq * 128:(gq + 1) * 128], bi_[gq * 128:(gq + 1) * 128], trib.full(), True, True)
        lastr = pgr.full().w(pgr.ap.rearrange("p (a b) -> p a b", b=128)[:, :, 127])
        lasti = pgi.full().w(pgi.ap.rearrange("p (a b) -> p a b", b=128)[:, :, 127])
        k.copy("act", Gend["re"][kt * 4:(kt + 1) * 4], lastr)
        k.copy("act", Gend["im"][kt * 4:(kt + 1) * 4], lasti)
        if kt == 7:
            carry_update2(n)

    def S4(it):
        n, kt = divmod(it, 8)
        i2 = it % 2
        pgr, pgi = ps[2 + 2 * i2], ps[3 + 2 * i2]
        hr, hi_ = hT["re"][i2], hT["im"][i2]
        crn, cin_ = cr2[n % 2], ci2[n % 2]
        gcr, gci = Gc["re"][i2], Gc["im"][i2]
        for gq in range(4):
            gp = kt * 4 + gq
            k.act(gcr[gq], pgr[gq * 128:(gq + 1) * 128], AF.Identity, bias=crn[gp:gp + 1])
            k.act(gci[gq], pgi[gq * 128:(gq + 1) * 128], AF.Identity, bias=cin_[gp:gp + 1])
        qa, qb = mtmp[i2]
        tp2 = Tpos2[:, kt * 4:(kt + 1) * 4]
        gr_, gi_ = gcr.full(), gci.full()
        k.tt("dve", qa.full(), gr_.w(gr_.ap.unsqueeze(1).broadcast_to([128, 2, 4, 128])), tp2, MUL)
        k.tt("dve", qb.full(), gi_.w(gi_.ap.unsqueeze(1).broadcast_to([128, 2, 4, 128])), tp2, MUL)
        k.tt("pool", hr.full(), qa[0], qb[1], SUB)
        k.tt("pool", hi_.full(), qa[1], qb[0], ADD)

    def S5(it):
        n, kt = divmod(it, 8)
        i2 = it % 2
        hr, hi_ = hT["re"][i2], hT["im"][i2]
        py = ps[6 + i2]
        for gq in range(4):
            gp = kt * 4 + gq
            k.mm(py[0:128], Cb["re"][gp], hr[gq], gq == 0, False)
            k.mm(py[0:128], Cb["imn"][gp], hi_[gq], False, gq == 3)

    def S6(it):
        n, kt = divmod(it, 8)
        i2 = it % 2
        py = ps[6 + i2]
        yt = ytmp[i2].full()
        yg = ygt[n % 2]
        k.stt("dve", yt, uTf[n % 2][kt], dsk[kt:kt + 1], py[0:128], MUL, ADD)
        k.act(yg[kt], yt, AF.Gelu_apprx_tanh)
        if kt == 7:
            k.dma("pool", dr(YG_v[:, :, n * 128:(n + 1) * 128], "YG", n * 128, (n + 1) * 128), yg.full())

    Gc = {n_: [A.tile([4, 128], F32) for _ in range(2)] for n_ in ("re", "im")}
    mtmp = [[A.tile([2, 4, 128], F32) for _ in range(2)] for _ in range(2)]
    f12 = [A.tile([2, 512], F32) for _ in range(2)]
    stages = [S1, S2, S3, S4, S5, S6]
    n_it = n_own * 8
    for slot in range(n_it + len(stages) - 1):
        for si in range(len(stages) - 1, -1, -1):
            it_ = slot - si
            if 0 <= it_ < n_it:
                stages[si](it_)
    A.release()


def build_phase3(k, A, ps, L):
    ident, identb, ones_f = L["ident"], L["identb"], L["ones_f"]
    m2, sh2, bglu = L["m2"], L["sh2"], L["bglu"]
    og, g1bc_t, g2bc_t = L["og"], L["g1bc_t"], L["g2bc_t"]
    NUM_d, YG_d, xo_d, out_d = L["NUM_d"], L["YG_d"], L["xo_d"], L["out_d"]
    MUL, ADD = ALU.mult, ALU.add
    wglu_v = L["wglub_d"].rearrange("(kt p) c -> p kt c", p=128)
    wout_v = L["woutb_d"].rearrange("(kt p) c -> p kt c", p=128)
    wff1_v = L["wff1b_d"].rearrange("(kt p) c -> p kt c", p=128)
    wff2_v = L["wff2b_d"].rearrange("(kt p) c -> p kt c", p=128)
    YG_v = YG_d.rearrange("k p t -> p k t")
    A.mark()
    wblk = [A.tile([16, 512], BF16) for _ in range(3)]
    xmid = A.tile([4, 2048], F32)
    actT = A.tile([16, 512], BF16)
    stat = A.tile([16], F32)
    gtmp = [A.tile([512], F32) for _ in range(2)]
    cw = {"w": 0, "p": 0}

    def next_w():
        w = wblk[cw["w"] % 3]
        cw["w"] += 1
        return w

    n_t3 = int(os.environ.get("MK_P3_TILES", "8"))
    for ti in range(n_t3):
        tok0 = ti * TT
        for sub in range(4):
            k.dma("sp", xmid[sub], dr(xo_d[tok0 + sub * 128: tok0 + (sub + 1) * 128, :], "x"))
        k.memset("dve", stat.full(), 0.0)
        A.mark()
        nt_ = [A.tile([8, 129], F32) for _ in range(3)]
        nt = [nt_, nt_]
        attn = A.tile([8, 128], F32)
        attb = A.tile([1024], BF16)
        junk = A.tile([1024], BF16)
        rec = A.tile([8], F32)
        for sub in range(4):
            t0 = tok0 + sub * 128
            n0, n1, n2 = nt[sub % 2]
            for pi, t_ in enumerate((n0, n1, n2)):
                k.dma("sp", t_.full(), dr(NUM_d[pi][t0:t0 + 128], f"NUM{pi}", t0, t0 + 128))
            k.tt("dve", n0.full(), n0.full(), n1.full(), ADD)
            k.tt("dve", n0.full(), n0.full(), n2.full(), ADD)
            k.recip(rec.full(), n0[:, 128:129].w(n0.ap[:, :, 128]))
            k.tt("dve", attn.full(), n0[:, 0:128], rec.full().w(rec.ap.unsqueeze(2).broadcast_to([128, 8, 128])), MUL)
            ss = stat[sub:sub + 1]
            rs = stat[4 + sub:5 + sub]
            af = attn.full().w(attn.ap.rearrange("p a b -> p (a b)"))
            k.act(junk.full(), af, AF.Square, accum_out=ss)
            k.act(rs, ss, AF.Sqrt, scale=1.0 / 1024, bias=EPS)
            k.recip(rs, rs)
            k.ts("dve", attb.full(), af, rs, op0=MUL)
            pT = ps[sub % 2]
            pTb = pT.full().w(pT.ap.bitcast(BF16))
            for h in range(8):
                k.tr(pTb.w(pTb.ap[:, h * 128:(h + 1) * 128]), attb[h * 128:(h + 1) * 128], identb.full())
            for h in range(8):
                k.act(actT[h, sub * 128:(sub + 1) * 128], pTb.w(pTb.ap[:, h * 128:(h + 1) * 128]), AF.Identity,
                      scale=og[h:h + 1])
        yg = A.tile([8, 512], F32)
        ygb = A.tile([8, 512], BF16)
        ssm = yg
        gate = [A.tile([512], F32) for _ in range(2)]
        sq = [A.tile([512], F32) for _ in range(2)]
        rbc = A.tile([512], F32)
        k.dma("sp", yg.full(), dr(YG_v[:, :, tok0:tok0 + TT], "YG", tok0, tok0 + TT))
        k.copy("act", ygb.full(), yg.full())
        wg = next_w()
        wgv = wg.full().w(wg.ap.rearrange("p a b -> p (a b)").rearrange("p (a b) -> p a b", b=1024))
        k.dma("sp", wgv, dr(wglu_v, "wglub"))
        for co in range(8):
            pb = ps[2 + co % 2]
            for kt in range(8):
                k.mm(pb[0:512], wgv.w(wgv.ap[:, kt, co * 128:(co + 1) * 128]), ygb[kt], kt == 0, kt == 7)
            g_ = gate[co % 2].full()
            s_ = sq[co % 2].full()
            k.act(g_, pb[0:512], AF.Sigmoid, bias=bglu[co:co + 1])
            k.tt("dve", ssm[co], yg[co], g_, MUL)
            k.act(s_, ssm[co], AF.Square)
            k.mm(ps[4][0:512], ones_f.full(), s_, co == 0, co == 7)
        k.act(rbc.full(), ps[4][0:512], AF.Sqrt, scale=1.0 / 1024, bias=EPS)
        k.recip(rbc.full(), rbc.full())
        for co in range(8):
            k.stt("dve", actT[8 + co], ssm[co], og[8 + co:9 + co], rbc.full(), MUL, MUL)
        A.release()
        for cc in range(4):
            wb = next_w()
            k.dma("sp", wb.full(), dr(wout_v[:, :, cc * 512:(cc + 1) * 512], "woutb"))
            for sub in range(4):
                pb = ps[5 + cw["p"] % 2]
                cw["p"] += 1
                for kt in range(16):
                    k.mm(pb[0:512], actT[kt, sub * 128:(sub + 1) * 128], wb[kt], kt == 0, kt == 15)
                xs = xmid[sub, cc * 512:(cc + 1) * 512]
                gt_ = gtmp[cw["p"] % 2].full()
                k.tt("dve", gt_, pb[0:512], g1bc_t[cc * 512:(cc + 1) * 512], MUL)
                k.tt("pool", xs, xs, gt_, ADD)
        A.mark()
        hid = A.tile([64, 512], BF16)
        xn_t = Tile(A.ap[:, hid.lo // 4:(hid.lo + 8192) // 4], "sb", hid.lo, [2048], 4)
        xn = [xn_t, xn_t]
        junk2 = Tile(A.ap[:, (hid.lo + 8192) // 4:(hid.lo + 12288) // 4].bitcast(BF16), "sb", hid.lo + 8192, [2048], 2)
        rl = [A.tile([512], F32) for _ in range(2)]
        ot = gtmp
        for sub in range(4):
            ss = stat[8 + sub:9 + sub]
            rs = stat[12 + sub:13 + sub]
            k.act(junk2.full(), xmid[sub], AF.Square, accum_out=ss)
            k.act(rs, ss, AF.Sqrt, scale=1.0 / D_MODEL, bias=EPS)
            k.recip(rs, rs)
            x_ = xn[sub % 2]
            k.ts("dve", x_.full(), xmid[sub], rs, op0=MUL)
            for kq in range(4):
                bank = kq % 2
                for j in range(4):
                    kt = kq * 4 + j
                    k.tr(ps[bank][j * 128:(j + 1) * 128], x_[kt * 128:(kt + 1) * 128], ident.full())
                for j in range(4):
                    kt = kq * 4 + j
                    k.act(actT[kt, sub * 128:(sub + 1) * 128], ps[bank][j * 128:(j + 1) * 128], AF.Identity,
                          scale=m2[kt:kt + 1], bias=sh2[kt:kt + 1])
        for blk in range(16):
            wb = next_w()
            k.dma("sp", wb.full(), dr(wff1_v[:, :, blk * 512:(blk + 1) * 512], "wff1b"))
            for co in range(4):
                pb = ps[2 + cw["p"] % 3]
                cw["p"] += 1
                for kt in range(16):
                    k.mm(pb[0:512], wb[kt, co * 128:(co + 1) * 128], actT[kt], kt == 0, kt == 15)
                r_ = rl[(blk * 4 + co) % 2].full()
                k.act(r_, pb[0:512], AF.Relu)
                k.tt("pool", hid[blk * 4 + co], r_, r_, MUL)
        for cc in range(4):
            bset = (cc % 2) * 4
            for q in range(4):
                wb = next_w()
                k.dma("sp", wb.full(), dr(wff2_v[:, q * 16:(q + 1) * 16, cc * 512:(cc + 1) * 512], "wff2b"))
                for sub in range(4):
                    for kt in range(16):
                        k.mm(ps[bset + sub][0:512], hid[q * 16 + kt, sub * 128:(sub + 1) * 128], wb[kt],
                             q == 0 and kt == 0, q == 3 and kt == 15)
            for sub in range(4):
                o_ = ot[sub % 2].full()
                k.tt("dve", o_, ps[bset + sub][0:512], g2bc_t[cc * 512:(cc + 1) * 512], MUL)
                k.tt("pool", o_, o_, xmid[sub, cc * 512:(cc + 1) * 512], ADD)
                r0 = tok0 + sub * 128
                k.dma("pool", dr(out_d[r0:r0 + 128, cc * 512:(cc + 1) * 512], "out", r0 * 4 + cc, r0 * 4 + cc + 1), o_)
        A.release()
    A.release()
```
